# Optimizing a Trainium2 kernel written in Bass

```python
import jax, jax.numpy as jnp
from jax import lax
import numpy as np

D_MODEL = 2048
BATCH = 4
SEQ = 8192
DEPTH = 1

CHUNK = 64
EPS = 1e-6

RET_HEADS = 8
RET_DK = 256
RET_DV = 256
RET_QK_WIDTH = RET_HEADS * RET_DK
RET_WIDTH = RET_HEADS * RET_DV
ROPE_THETA = 10000.0

SSD_EXPAND = 2
SSD_WIDTH = SSD_EXPAND * D_MODEL
SSD_HEADDIM = 64
SSD_HEADS = SSD_WIDTH // SSD_HEADDIM
SSD_GROUPS = 8
SSD_HPG = SSD_HEADS // SSD_GROUPS
SSD_STATE = 128
SSD_CONV = 4
SSD_CONV_DIM = SSD_WIDTH + 2 * SSD_GROUPS * SSD_STATE
DT_MIN = 0.001
DT_MAX = 0.1

SPLITS = (RET_QK_WIDTH, RET_QK_WIDTH, RET_WIDTH, RET_WIDTH,
          SSD_WIDTH, SSD_CONV_DIM, SSD_HEADS, D_MODEL, D_MODEL)
IN_PROJ_DIM = sum(SPLITS)

kernel_name = "retention_ssd_gated_hybrid"

f32 = jnp.float32


def rmsnorm(x, w):
    xf = x.astype(f32)
    y = xf * lax.rsqrt(jnp.mean(xf * xf, axis=-1, keepdims=True) + EPS)
    return (y * w.astype(f32)).astype(x.dtype)


def to_chunks(t):
    b, s = t.shape[0], t.shape[1]
    return jnp.moveaxis(t.reshape((b, s // CHUNK, CHUNK) + t.shape[2:]), 1, 0)


def from_chunks(t):
    t = jnp.moveaxis(t, 0, 1)
    return t.reshape((t.shape[0], t.shape[1] * t.shape[2]) + t.shape[3:])


def rope(t, positions):
    half = t.shape[-1] // 2
    inv_freq = ROPE_THETA ** (-jnp.arange(half, dtype=f32) / half)
    ang = positions.astype(f32)[..., None] * inv_freq
    cos, sin = jnp.cos(ang)[:, :, None, :], jnp.sin(ang)[:, :, None, :]
    t1, t2 = t[..., :half], t[..., half:]
    return jnp.concatenate([t1 * cos - t2 * sin, t2 * cos + t1 * sin], axis=-1)


def retention(q, k, v, positions):
    b, s, _ = q.shape
    q = rope(q.reshape(b, s, RET_HEADS, RET_DK).astype(f32), positions)
    k = rope(k.reshape(b, s, RET_HEADS, RET_DK).astype(f32), positions) * (RET_DK ** -0.5)
    v = v.reshape(b, s, RET_HEADS, RET_DV).astype(f32)

    log_gamma = jnp.log1p(-(2.0 ** (-5.0 - jnp.arange(RET_HEADS, dtype=f32))))
    idx = jnp.arange(CHUNK, dtype=f32)
    intra = jnp.exp(jnp.abs(idx[:, None] - idx[None, :]) * log_gamma[:, None, None])
    q_decay = jnp.exp((idx[:, None] + 1.0) * log_gamma[None, :])[None, :, :, None]
    k_decay = jnp.exp((CHUNK - 1.0 - idx[:, None]) * log_gamma[None, :])[None, :, :, None]
    chunk_decay = jnp.exp(CHUNK * log_gamma)[None, :, None, None]

    def step(state, inp):
        qc, kc, vc = inp
        scores = jnp.einsum('blhd,bshd->bhls', qc, kc) * intra
        y = jnp.einsum('bhls,bshv->blhv', scores, vc)
        y = y + jnp.einsum('blhd,bhdv->blhv', qc, state) * q_decay
        state = state * chunk_decay + jnp.einsum('bshd,bshv->bhdv', kc * k_decay, vc)
        return state, y

    state0 = jnp.zeros((b, RET_HEADS, RET_DK, RET_DV), f32)
    _, y = lax.scan(step, state0, (to_chunks(q), to_chunks(k), to_chunks(v)))
    y = from_chunks(y)
    mu = jnp.mean(y, axis=-1, keepdims=True)
    var = jnp.mean(jnp.square(y - mu), axis=-1, keepdims=True)
    y = (y - mu) * lax.rsqrt(var + EPS)
    return y.reshape(b, s, RET_WIDTH)


def ssd(xbc, dt_raw, conv_w, conv_b, dt_bias, a_log, d_skip):
    b, s, _ = xbc.shape
    xbc = lax.conv_general_dilated(
        xbc.astype(f32), conv_w.astype(f32)[:, None, :], window_strides=(1,),
        padding=[(SSD_CONV - 1, 0)], dimension_numbers=('NWC', 'WIO', 'NWC'),
        feature_group_count=SSD_CONV_DIM) + conv_b.astype(f32)
    xbc = jax.nn.silu(xbc)
    gn = SSD_GROUPS * SSD_STATE
    xs = xbc[..., :SSD_WIDTH].reshape(b, s, SSD_GROUPS, SSD_HPG, SSD_HEADDIM)
    bm = xbc[..., SSD_WIDTH:SSD_WIDTH + gn].reshape(b, s, SSD_GROUPS, SSD_STATE)
    cm = xbc[..., SSD_WIDTH + gn:].reshape(b, s, SSD_GROUPS, SSD_STATE)
    dt = jax.nn.softplus(dt_raw.astype(f32) + dt_bias.astype(f32)).reshape(b, s, SSD_GROUPS, SSD_HPG)
    a = dt * (-jnp.exp(a_log.astype(f32))).reshape(SSD_GROUPS, SSD_HPG)
    xdt = xs * dt[..., None]
    causal = jnp.tril(jnp.ones((CHUNK, CHUNK), dtype=bool))[None, :, :, None, None]

    def step(state, inp):
        xc, bc, cc, ac = inp
        acum = jnp.cumsum(ac, axis=1)
        seg = acum[:, :, None] - acum[:, None, :]
        decay = jnp.exp(jnp.where(causal, seg, -jnp.inf))
        cb = jnp.einsum('blgn,bsgn->blsg', cc, bc)
        y = jnp.einsum('blsg,blsgh,bsghp->blghp', cb, decay, xc)
        y = y + jnp.einsum('blgn,bghpn->blghp', cc, state) * jnp.exp(acum)[..., None]
        tail = jnp.exp(acum[:, -1:] - acum)
        state = (state * jnp.exp(acum[:, -1])[..., None, None]
                 + jnp.einsum('bsgn,bsgh,bsghp->bghpn', bc, tail, xc))
        return state, y

    state0 = jnp.zeros((b, SSD_GROUPS, SSD_HPG, SSD_HEADDIM, SSD_STATE), f32)
    _, y = lax.scan(step, state0, (to_chunks(xdt), to_chunks(bm), to_chunks(cm), to_chunks(a)))
    y = from_chunks(y) + d_skip.astype(f32).reshape(SSD_GROUPS, SSD_HPG)[..., None] * xs
    return y.reshape(b, s, SSD_WIDTH)


def setup_inputs(seed: int = 0) -> dict:
    key = jax.random.key(seed)
    ks = jax.random.split(key, 16)
    x = jax.random.normal(ks[0], (BATCH, SEQ, D_MODEL), f32)
    offset = jax.random.randint(ks[1], (BATCH, 1), 0, 100000, dtype=jnp.int32)
    positions = offset + jnp.arange(SEQ, dtype=jnp.int32)[None, :]
    norm1_w = 1.0 + 0.02 * jax.random.normal(ks[2], (DEPTH, D_MODEL), f32)
    w_in = jax.random.normal(ks[3], (DEPTH, D_MODEL, IN_PROJ_DIM), f32) * D_MODEL ** -0.5
    conv_w = jax.random.normal(ks[4], (DEPTH, SSD_CONV, SSD_CONV_DIM), f32) * SSD_CONV ** -0.5
    conv_b = 0.01 * jax.random.normal(ks[5], (DEPTH, SSD_CONV_DIM), f32)
    u = jax.random.uniform(ks[6], (DEPTH, SSD_HEADS), f32)
    dt0 = jnp.exp(u * (np.log(DT_MAX) - np.log(DT_MIN)) + np.log(DT_MIN))
    dt_bias = dt0 + jnp.log(-jnp.expm1(-dt0))
    a_log = jnp.log(jax.random.uniform(ks[7], (DEPTH, SSD_HEADS), f32, minval=1.0, maxval=16.0))
    d_skip = 1.0 + 0.1 * jax.random.normal(ks[8], (DEPTH, SSD_HEADS), f32)
    ssd_norm_w = 1.0 + 0.02 * jax.random.normal(ks[9], (DEPTH, SSD_WIDTH), f32)
    w_br_ret = jax.random.normal(ks[10], (DEPTH, RET_WIDTH, D_MODEL), f32) * RET_WIDTH ** -0.5
    w_br_ssd = jax.random.normal(ks[11], (DEPTH, SSD_WIDTH, D_MODEL), f32) * SSD_WIDTH ** -0.5
    w_out = jax.random.normal(ks[12], (DEPTH, D_MODEL, D_MODEL), f32) * D_MODEL ** -0.5
    norm_f_w = 1.0 + 0.02 * jax.random.normal(ks[13], (D_MODEL,), f32)
    return {"x": x, "positions": positions, "norm1_w": norm1_w, "w_in": w_in,
            "conv_w": conv_w, "conv_b": conv_b, "dt_bias": dt_bias, "a_log": a_log,
            "d_skip": d_skip, "ssd_norm_w": ssd_norm_w, "w_br_ret": w_br_ret,
            "w_br_ssd": w_br_ssd, "w_out": w_out, "norm_f_w": norm_f_w}


def reference(x, positions, norm1_w, w_in, conv_w, conv_b, dt_bias, a_log, d_skip,
              ssd_norm_w, w_br_ret, w_br_ssd, w_out, norm_f_w):
    offsets = [int(o) for o in np.cumsum(SPLITS)[:-1]]
    for l in range(DEPTH):
        h = rmsnorm(x, norm1_w[l])
        proj = jnp.einsum('bsd,de->bse', h, w_in[l])
        q, k, v, g_ret, z, xbc, dt_raw, gate_r, gate_s = jnp.split(proj, offsets, axis=-1)
        y_r = (retention(q, k, v, positions) * jax.nn.silu(g_ret.astype(f32))).astype(x.dtype)
        y_s = ssd(xbc, dt_raw, conv_w[l], conv_b[l], dt_bias[l], a_log[l], d_skip[l])
        y_s = rmsnorm((y_s * jax.nn.silu(z.astype(f32))).astype(x.dtype), ssd_norm_w[l])
        p_r = jnp.einsum('bse,ed->bsd', y_r, w_br_ret[l])
        p_s = jnp.einsum('bse,ed->bsd', y_s, w_br_ssd[l])
        merged = jax.nn.sigmoid(gate_r) * p_r + jax.nn.sigmoid(gate_s) * p_s
        x = x + jnp.einsum('bsd,de->bse', merged, w_out[l])
    return rmsnorm(x, norm_f_w)
```

```python
import math
import numpy as np
import ml_dtypes
from contextlib import ExitStack
import concourse.bass as bass
import concourse.mybir as mybir
from concourse.bass_utils import run_bass_kernel_spmd

F32 = mybir.dt.float32
BF16 = mybir.dt.bfloat16
I32 = mybir.dt.int32
AF = mybir.ActivationFunctionType
ALU = mybir.AluOpType

SAME_ENGINE_SYNC = True
D = 2048
T = 512
EPS = 1e-6
GR = 528
NGRAN = 39


class _Stop(Exception):
    pass


class _Op:
    __slots__ = ("eng", "fn", "deps", "inc", "idx", "dsem", "dval")

    def __init__(self, eng, fn, inc, dsem):
        self.eng = eng
        self.fn = fn
        self.deps = []
        self.inc = inc
        self.idx = -1
        self.dsem = dsem
        self.dval = 0


class Sched:
    ENGS = ("pe", "act", "dve", "pool", "sp")

    def __init__(self, nc):
        self.nc = nc
        self.ops = {e: [] for e in self.ENGS}
        self.state = {}
        self.dcount = {}
        self.children = {}
        self.skip = False

    def _conf(self, key):
        fam = self.children.get(key[0], ())
        out = []
        for k in fam:
            n = min(len(k), len(key))
            if k[:n] == key[:n]:
                out.append(k)
        return out

    def _norm(self, keys):
        out = []
        for key in keys:
            if isinstance(key, list):
                out.extend(self._norm(key))
            elif isinstance(key, tuple):
                out.append(key)
            else:
                out.append((key,))
        return out

    def op(self, eng, fn, reads=(), writes=(), inc=True, dsem=None):
        if self.skip:
            return None
        reads = self._norm(reads)
        writes = self._norm(writes)
        o = _Op(eng, fn, inc, dsem)
        deps = []
        for key in reads:
            for k in self._conf(key):
                w = self.state[k][0]
                if w is not None:
                    deps.append(w)
        for key in writes:
            for k in self._conf(key):
                w, rs = self.state[k]
                if w is not None:
                    deps.append(w)
                deps.extend(rs)
        for key in reads:
            if key not in self.state:
                self.state[key] = [None, []]
                self.children.setdefault(key[0], set()).add(key)
            self.state[key][1].append(o)
        for key in writes:
            if key not in self.state:
                self.state[key] = [None, []]
                self.children.setdefault(key[0], set()).add(key)
            for k in self._conf(key):
                if k != key and len(k) > len(key):
                    self.state[k] = [None, []]
            self.state[key] = [o, []]
        if dsem is not None:
            self.dcount[dsem] = self.dcount.get(dsem, 0) + 1
            o.dval = 16 * self.dcount[dsem]
            o.inc = True
        o.idx = len(self.ops[eng])
        seen = set()
        for d in deps:
            if id(d) in seen or d is o:
                continue
            seen.add(id(d))
            if d.dsem is None and not d.inc:
                lst = self.ops[d.eng]
                covered = False
                for j in range(d.idx + 1, len(lst)):
                    if lst[j].inc and lst[j].dsem is None:
                        covered = True
                        break
                if not covered:
                    d.inc = True
            o.deps.append(d)
        self.ops[eng].append(o)
        return o

    def emit(self, stack):
        nc = self.nc
        sems = {}
        for e in self.ENGS:
            sems[e] = stack.enter_context(nc.semaphore("s_" + e))
        for name in self.dcount:
            sems["d:" + name] = stack.enter_context(nc.semaphore("d_" + name))
        for e in self.ENGS:
            for o in reversed(self.ops[e]):
                if o.dsem is None:
                    o.inc = True
                    break
        val = {}
        for e in self.ENGS:
            lst = self.ops[e]
            cnt = 0
            for o in lst:
                if o.dsem is not None:
                    val[id(o)] = ("d:" + o.dsem, o.dval)
                elif o.inc:
                    cnt += 1
                    val[id(o)] = (e, cnt)
                else:
                    val[id(o)] = (e, cnt + 1)
        block = stack.enter_context(nc.Block())

        def body_for(e):
            def body(engobj):
                waited = {}
                for o in self.ops[e]:
                    need = {}
                    for d in o.deps:
                        sname, v = val[id(d)]
                        if d.dsem is None and d.eng == e:
                            if e == "pe" or not SAME_ENGINE_SYNC:
                                continue
                        if waited.get(sname, 0) >= v:
                            continue
                        if need.get(sname, 0) < v:
                            need[sname] = v
                    for sname, v in need.items():
                        engobj.wait_ge(sems[sname], v)
                        waited[sname] = v
                    ins = o.fn(engobj)
                    if o.dsem is not None:
                        ins.then_inc(sems["d:" + o.dsem], 16)
                    elif o.inc:
                        ins.then_inc(sems[e], 1)
                if e == "sp":
                    for name, c in self.dcount.items():
                        engobj.wait_ge(sems["d:" + name], 16 * c)
            return body

        block.tensor(body_for("pe"))
        block.scalar(body_for("act"))
        block.vector(body_for("dve"))
        block.gpsimd(body_for("pool"))
        block.sync(body_for("sp"))


def weight_groups():
    G = []
    for h in range(8):
        G.append((f"qk{h}", "w_in", 0, [(h * 256, 256), (2048 + h * 256, 256)], "n1"))
        G.append((f"vg{h}", "w_in", 0, [(4096 + h * 256, 256), (6144 + h * 256, 256)], "n1"))
    G.append(("dt", "w_in", 0, [(18432, 64)], "n1"))
    for g in range(8):
        G.append((f"x{g}", "w_in", 0, [(12288 + g * 512, 512)], "n1"))
        G.append((f"bc{g}", "w_in", 0, [(16384 + g * 128, 128), (17408 + g * 128, 128)], "n1"))
        G.append((f"z{g}", "w_in", 0, [(8192 + g * 512, 512)], "n1"))
    for oc in range(4):
        G.append((f"gr{oc}", "w_in", 0, [(18496 + oc * 512, 512)], "n1"))
        G.append((f"gs{oc}", "w_in", 0, [(20544 + oc * 512, 512)], "n1"))
    for oc in range(4):
        G.append((f"br{oc}", "w_brr", 0, [(oc * 512, 512)], None))
        G.append((f"bs0{oc}", "w_brs", 0, [(oc * 512, 512)], "sn"))
        G.append((f"bs1{oc}", "w_brs", 16, [(oc * 512, 512)], "sn"))
    for oc in range(4):
        G.append((f"o{oc}", "w_out", 0, [(oc * 512, 512)], None))
    return G


def build_nc(NT, debug=None, NPRE=0):
    nc = bass.Bass("TRN2", target_bir_lowering=False)
    NTOK = NT * T
    NMAIN = NT - NPRE

    def din(name, shape, dt=F32):
        return nc.dram_tensor(name, shape, dt, kind="ExternalInput").ap()

    x = din("x", [NTOK, D])
    pos = din("pos", [1, NTOK], I32)
    wsrc = {"w_in": din("w_in", [D, 22592]), "w_brr": din("w_brr", [2048, 2048]),
            "w_brs": din("w_brs", [4096, 2048]), "w_out": din("w_out", [2048, 2048])}
    n1col_d = din("n1col", [128, 16])
    sncol_d = din("sncol", [128, 32])
    convw_d = din("convw", [128, 192])
    convb_d = din("convb", [128, 48])
    dch_d = din("dch", [128, 32])
    dtb_d = din("dtb", [1, 64])
    alog_d = din("alog", [1, 64])
    normf_d = din("normf", [1, D])
    cid_d = din("c_ident", [128, 128], BF16)
    cidf_d = din("c_identf", [128, 128])
    conesf_d = din("c_onesf", [128, 128])
    cones_d = din("c_ones", [128, 128], BF16)
    crm_d = din("c_retmask", [128, 8 * 128], BF16)
    cssd_d = din("c_ssd", [128, 6 * 128])
    cqd_d = din("c_qdec", [128, 8 * 64])
    ckd_d = din("c_kdec", [128, 8])
    cinvf_d = din("c_invf", [128, 1])
    out = nc.dram_tensor("out", [NMAIN * T, D], F32, kind="ExternalOutput").ap()
    flag_d = din("flag", [128, 1])
    WG = weight_groups()
    NG = len(WG)
    ws = nc.dram_tensor("ws", [NG, 128, 16 * 512], BF16, kind="Internal").ap()
    gidx = {g[0]: i for i, g in enumerate(WG)}

    st = ExitStack()
    with st:
        S = Sched(nc)

        def sb(name, shape, dt):
            return st.enter_context(nc.sbuf_tensor("s_" + name, shape, dt))

        xbuf = [sb(f"xbuf{i}", [128, D], F32) for i in range(2)]
        hT = sb("hT", [128, 16, T], BF16)
        wbuf = [sb(f"wbuf{i}", [128, 16, 512], BF16) for i in range(2)]
        yR = sb("yR", [128, 16, T], BF16)
        ysT = sb("ysT", [128, 32, T], BF16)
        Sret = sb("Sret", [128, 8, 512], F32)
        Sssd = sb("Sssd", [128, 8, 512], F32)
        arena = sb("arena", [128, NGRAN * GR], BF16)
        ident = sb("ident", [128, 128], BF16)
        identf = sb("identf", [128, 128], F32)
        onesf = sb("onesf", [128, 128], F32)
        onesS = sb("onesS", [128, 128], BF16)
        retmask = sb("retmask", [128, 8, 128], BF16)
        ssdm = sb("ssdm", [128, 6, 128], F32)
        qdec = sb("qdec", [128, 8, 64], F32)
        kdec = sb("kdec", [128, 8], F32)
        invf = sb("invf", [128, 1], F32)
        normf = sb("normf", [128, D], F32)
        n1col = sb("n1col", [128, 16], F32)
        sncol = sb("sncol", [128, 32], F32)
        convw = sb("convw", [128, 48, 4], F32)
        convb = sb("convb", [128, 48], F32)
        dch = sb("dch", [128, 32], F32)
        dtb = sb("dtb", [128, 64], F32)
        arow = sb("arow", [128, 64], F32)
        halo = sb("halo", [128, 48, 4], BF16)
        ssq = sb("ssq", [128, 4, 8], F32)
        sml = sb("sml", [128, 16], F32)
        flg = sb("flg", [128, 1], F32)
        hb = ysT[:, 0:4, :].rearrange("p a b -> p (a b)")
        HBK = [("ysT", i) for i in range(4)]

        pbank = [st.enter_context(nc.psum_tensor(f"pb{i}", [128, 512], F32)) for i in range(8)]
        rrA = [0]
        rrB = [0]

        AL = [[0, 1, 2]]

        def bankA():
            lst = AL[0]
            i = lst[rrA[0] % len(lst)]
            rrA[0] += 1
            return i

        BL = [[3, 4, 5, 6, 7]]

        def bankB():
            lst = BL[0]
            i = lst[rrB[0] % len(lst)]
            rrB[0] += 1
            return i

        def PB(i):
            return pbank[i][:]

        def PBb(i):
            return pbank[i][:].bitcast(BF16)

        def pk(i):
            return ("pb", i)

        def dump(name, ap, keys, shape, dt):
            if debug != "dump":
                return
            d = nc.dram_tensor(name, shape, dt, kind="ExternalOutput").ap()
            S.op("sp", lambda e: e.dma_start(out=d, in_=ap), reads=keys, dsem="dbg_" + name)

        class AB:
            def __init__(self, g0, ng, dt, n):
                base = arena[:, g0 * GR: (g0 + ng) * GR]
                if dt == F32:
                    self.ap = base.bitcast(F32)[:, 0:n]
                elif dt == I32:
                    self.ap = base.bitcast(I32)[:, 0:n]
                else:
                    self.ap = base[:, 0:n]
                self.keys = [("ar", g) for g in range(g0, g0 + ng)]

        def ld(dst_ap, src_ap, key, name):
            S.op("sp", lambda e: e.dma_start(out=dst_ap, in_=src_ap), writes=[key], dsem=name)

        ld(ident[:], cid_d[:, :], "ident", "c0")
        ld(identf[:], cidf_d[:, :], "identf", "c1")
        ld(onesf[:], conesf_d[:, :], "onesf", "c2")
        ld(onesS[:], cones_d[:, :], "onesS", "c3")
        ld(retmask[:].rearrange("p a b -> p (a b)"), crm_d[:, :], "retmask", "c4")
        ld(ssdm[:].rearrange("p a b -> p (a b)"), cssd_d[:, :], "ssdm", "c5")
        ld(qdec[:].rearrange("p a b -> p (a b)"), cqd_d[:, :], "qdec", "c6")
        ld(kdec[:], ckd_d[:, :], "kdec", "c7")
        ld(invf[:], cinvf_d[:, :], "invf", "c8")
        ld(flg[:], flag_d[:, :], "flg", "c17")
        ld(normf[:], normf_d.partition_broadcast(128), "normf", "c9")
        ld(n1col[:], n1col_d[:, :], "n1col", "c10")
        ld(sncol[:], sncol_d[:, :], "sncol", "c11")
        ld(convw[:].rearrange("p a b -> p (a b)"), convw_d[:, :], "convw", "c12")
        ld(convb[:], convb_d[:, :], "convb", "c13")
        ld(dch[:], dch_d[:, :], "dch", "c14")
        ld(dtb[:], dtb_d.partition_broadcast(128), "dtb", "c15")
        ld(arow[:], alog_d.partition_broadcast(128), "arow", "c16")
        S.op("act", lambda e: e.activation(out=arow[:], in_=arow[:], func=AF.Exp), reads=["arow"], writes=["arow"])
        S.op("dve", lambda e: e.tensor_scalar_mul(arow[:], arow[:], -1.0), reads=["arow"], writes=["arow"])
        S.op("pool", lambda e: e.memset(Sret[:], 0.0), writes=["Sret"])
        S.op("pool", lambda e: e.memset(Sssd[:], 0.0), writes=["Sssd"])
        S.op("pool", lambda e: e.memset(halo[:], 0.0), writes=["halo"])

        TRI, UBD, UU, ONC0, ONC1, MBD = [ssdm[:, i, :] for i in range(6)]

        fslots = [(xbuf[0][:], [("xbuf", 0)]), (xbuf[1][:], [("xbuf", 1)])]
        for i_ in range(4):
            fslots.append((ysT[:, 8 * i_:8 * i_ + 8, :].rearrange("p a b -> p (a b)").bitcast(F32), [("ysT", c_) for c_ in range(8 * i_, 8 * i_ + 8)]))
        for i_ in range(2):
            fslots.append((yR[:, 8 * i_:8 * i_ + 8, :].rearrange("p a b -> p (a b)").bitcast(F32), [("yR", c_) for c_ in range(8 * i_, 8 * i_ + 8)]))
        NFS = len(fslots)
        it = 0
        fi = 0
        for gi, (gname, src, kc0, segs, scale) in enumerate(WG):
            for kq in range(4):
                bi = it % 8
                it += 1
                k0 = kc0 + kq * 4
                off = 0
                bst = wbuf[bi // 4][:, 4 * (bi % 4):4 * (bi % 4) + 4, :]
                bkey = ("wbuf", bi // 4, bi % 4)
                for si, (c0, n) in enumerate(segs):
                    fap, fkeys = fslots[fi % NFS]
                    fsem = f"cv{fi % NFS}"
                    fi += 1
                    fst = fap[:, 0:n * 4].rearrange("p (k c) -> p k c", k=4)
                    srcap = wsrc[src][k0 * 128:(k0 + 4) * 128, c0:c0 + n].rearrange("(k p) c -> p k c", p=128)
                    S.op("sp", lambda e, fst=fst, srcap=srcap: e.dma_start(out=fst, in_=srcap),
                         writes=fkeys, dsem=fsem)
                    eng = ("dve", "dve", "pool")[fi % 3]
                    dst = bst[:, :, off:off + n]
                    if scale is None:
                        S.op(eng, lambda e, dst=dst, fst=fst: e.tensor_copy(dst, fst),
                             reads=fkeys, writes=[bkey + (si,)])
                    else:
                        col = n1col if scale == "n1" else sncol
                        cb = col[:, k0:k0 + 4].unsqueeze(2).to_broadcast([128, 4, n])
                        S.op(eng, lambda e, dst=dst, fst=fst, cb=cb: e.tensor_tensor(out=dst, in0=fst, in1=cb, op=ALU.mult),
                             reads=fkeys + ["n1col", "sncol"], writes=[bkey + (si,)])
                    off += n
                dstd = ws[gi, :, kq * 2048:(kq + 1) * 2048].rearrange("p (k c) -> p k c", k=4)[:, :, 0:off]
                srcs = bst[:, :, 0:off]
                S.op("act", lambda e, dstd=dstd, srcs=srcs: e.dma_start(out=dstd, in_=srcs),
                     reads=[bkey], writes=[("ws", gi)], dsem=f"cs{bi}")

        wq = {"n": 0, "loaded": -1}
        order = []
        for ti_ in range(NT):
            for (gname_, _, _, _, _) in WG:
                if ti_ < NPRE and not (gname_.startswith("qk") or gname_.startswith("vg") or gname_ == "dt" or gname_.startswith("x") or gname_.startswith("bc")):
                    continue
                order.append((gidx[gname_], ti_ < NPRE))
        total_loads = len(order)

        def issue_load(n):
            gi, ispre = order[n]
            slot = n % 2
            gname = WG[gi][0]
            ncol = sum(nn for (_, nn) in WG[gi][3])
            c_lo, c_hi = 0, ncol
            if ispre and gname.startswith("qk"):
                c_lo, c_hi = 256, 512
            if ispre and gname.startswith("vg"):
                c_lo, c_hi = 0, 256
            src = ws[gi, :, :].rearrange("p (k c) -> p k c", c=512)[:, :, c_lo:c_hi]
            S.op("sp", lambda e, slot=slot, src=src, c_lo=c_lo, c_hi=c_hi: e.dma_start(out=wbuf[slot][:, :, c_lo:c_hi], in_=src),
                 reads=[("ws", gi)], writes=[("wbuf", slot)], dsem=f"w{slot}")

        def next_w(expect, prefetch=True):
            n = wq["n"]
            assert WG[order[n][0]][0] == expect, (WG[order[n][0]][0], expect)
            while wq["loaded"] < min(n + (1 if prefetch else 0), total_loads - 1):
                wq["loaded"] += 1
                issue_load(wq["loaded"])
            wq["n"] += 1
            return n % 2

        def prefetch_next():
            n = wq["n"]
            while wq["loaded"] < min(n, total_loads - 1):
                wq["loaded"] += 1
                issue_load(wq["loaded"])

        def proj_ws(slot, cbase, m, rhs_fn, rkeys, bank, nk=16, first=True, last=True, kofs=0):
            for kc in range(nk):
                S.op("pe", lambda e, kc=kc, slot=slot: e.matmul(PB(bank), lhsT=wbuf[slot][:, kc, cbase + m * 128: cbase + (m + 1) * 128],
                                                      rhs=rhs_fn(kc + kofs), start=(first and kc == 0), stop=(last and kc == nk - 1)),
                     reads=[("wbuf", slot)] + rkeys, writes=[pk(bank)], inc=(last and kc == nk - 1))

        def proj_as(slot, c0, n, j, lhs_fn, lkeys, bank):
            for kc in range(16):
                S.op("pe", lambda e, kc=kc, slot=slot: e.matmul(PB(bank)[:, 0:n], lhsT=lhs_fn(kc)[:, j * 128:(j + 1) * 128],
                                                      rhs=wbuf[slot][:, kc, c0:c0 + n], start=(kc == 0), stop=(kc == 15)),
                     reads=[("wbuf", slot)] + lkeys, writes=[pk(bank)], inc=(kc == 15))

        hT_fn = lambda kc: hT[:, kc, :]
        PI = math.pi
        MAGIC = 12582912.0
        C1 = 6.28125
        C2 = 2.0 * math.pi - 6.28125
        log_gamma = [math.log1p(-(2.0 ** (-5.0 - h))) for h in range(8)]
        cdec = [math.exp(64.0 * lg) for lg in log_gamma]

        def chk(k):
            if debug == k:
                raise _Stop()

        try:
          chk(1)
          for ti in range(NT):
            tok0 = ti * T
            PRE = ti < NPRE
            if NPRE > 0 and ti == NPRE:
                S.op("dve", lambda e: e.tensor_scalar_mul(Sret[:].rearrange("p a b -> p (a b)"), Sret[:].rearrange("p a b -> p (a b)"), flg[:, 0:1]), reads=[("Sret",), "flg"], writes=[("Sret",)])
                S.op("dve", lambda e: e.tensor_scalar_mul(Sssd[:].rearrange("p a b -> p (a b)"), Sssd[:].rearrange("p a b -> p (a b)"), flg[:, 0:1]), reads=[("Sssd",), "flg"], writes=[("Sssd",)])
                S.op("dve", lambda e: e.tensor_scalar_mul(halo[:].rearrange("p a b -> p (a b)"), halo[:].rearrange("p a b -> p (a b)"), flg[:, 0:1]), reads=[("halo",), "flg"], writes=[("halo",)])
            for j in range(4):
                xb = xbuf[j % 2]
                xk = ("xbuf", j % 2)
                r0 = tok0 + j * 128
                S.op("sp", lambda e, xb=xb, r0=r0: e.dma_start(out=xb[:], in_=x[r0:r0 + 128, :]), writes=[xk], dsem=f"x{j % 2}")
                S.op("act", lambda e, xb=xb: e.activation(out=hb, in_=xb[:], func=AF.Square, accum_out=sml[:, 0:1]),
                     reads=[xk], writes=HBK + [("sml", 0)])
                S.op("dve", lambda e: e.tensor_scalar(out=sml[:, 1:2], in0=sml[:, 0:1], scalar1=1.0 / D, scalar2=EPS, op0=ALU.mult, op1=ALU.add),
                     reads=[("sml", 0)], writes=[("sml", 1)])
                S.op("act", lambda e: e.activation(out=sml[:, 2:3], in_=sml[:, 1:2], func=AF.Sqrt), reads=[("sml", 1)], writes=[("sml", 2)])
                S.op("dve", lambda e: e.reciprocal(sml[:, 3:4], sml[:, 2:3]), reads=[("sml", 2)], writes=[("sml", 3)])
                S.op("act", lambda e, xb=xb: e.activation(out=hb, in_=xb[:], func=AF.Identity, scale=sml[:, 3:4]),
                     reads=[xk, ("sml", 3)], writes=HBK)
                for half in range(2):
                    bk = bankB()
                    for q in range(8):
                        kc = half * 8 + q
                        S.op("pe", lambda e, bk=bk, q=q, kc=kc: e.transpose(PBb(bk)[:, q * 128:(q + 1) * 128], hb[:, kc * 128:(kc + 1) * 128], ident[:]),
                             reads=HBK + ["ident"], writes=[pk(bk)], inc=(q == 7))
                    dst = hT[:, half * 8:(half + 1) * 8, j * 128:(j + 1) * 128]
                    srcp = PBb(bk).rearrange("p (a b) -> p a b", b=128)
                    if half == 0:
                        S.op("dve", lambda e, dst=dst, srcp=srcp: e.tensor_copy(dst, srcp), reads=[pk(bk)], writes=[("hT", j, half)])
                    else:
                        S.op("act", lambda e, dst=dst, srcp=srcp: e.activation(out=dst, in_=srcp, func=AF.Copy), reads=[pk(bk)], writes=[("hT", j, half)])

            chk(2)
            if ti == 0:
                dump("d_hT", hT[:].rearrange("p a b -> p (a b)"), [("hT",)], [128, 16 * 512], BF16)
            cosb = AB(27, 2, F32, 512)
            sinb = AB(29, 2, F32, 512)
            tA = AB(0, 2, F32, 512)
            tB = AB(2, 2, F32, 512)
            tC = AB(4, 2, F32, 512)
            tD = AB(6, 2, F32, 512)
            posi = AB(0, 2, I32, 512)
            S.op("sp", lambda e, tok0=tok0: e.dma_start(out=posi.ap, in_=pos[0:1, tok0:tok0 + T].partition_broadcast(128)), writes=posi.keys, dsem="pos")
            S.op("dve", lambda e: e.tensor_copy(tB.ap, posi.ap), reads=posi.keys, writes=tB.keys)
            S.op("dve", lambda e: e.tensor_scalar_mul(tC.ap, tB.ap, invf[:, 0:1]), reads=tB.keys + [("invf",)], writes=tC.keys)
            S.op("dve", lambda e: e.tensor_scalar(out=tB.ap, in0=tC.ap, scalar1=1.0 / (2.0 * PI), scalar2=MAGIC, op0=ALU.mult, op1=ALU.add),
                 reads=tC.keys, writes=tB.keys)
            S.op("dve", lambda e: e.tensor_scalar_add(tB.ap, tB.ap, -MAGIC), reads=tB.keys, writes=tB.keys)
            S.op("dve", lambda e: e.scalar_tensor_tensor(out=tC.ap, in0=tB.ap, scalar=-C1, in1=tC.ap, op0=ALU.mult, op1=ALU.add),
                 reads=tB.keys + tC.keys, writes=tC.keys)
            S.op("dve", lambda e: e.scalar_tensor_tensor(out=tC.ap, in0=tB.ap, scalar=-C2, in1=tC.ap, op0=ALU.mult, op1=ALU.add),
                 reads=tB.keys + tC.keys, writes=tC.keys)
            S.op("dve", lambda e: e.tensor_scalar(out=tC.ap, in0=tC.ap, scalar1=-PI, scalar2=PI, op0=ALU.max, op1=ALU.min),
                 reads=tC.keys, writes=tC.keys)
            S.op("act", lambda e: e.activation(out=sinb.ap, in_=tC.ap, func=AF.Sin), reads=tC.keys, writes=sinb.keys)
            S.op("act", lambda e: e.activation(out=tD.ap, in_=tC.ap, func=AF.Abs), reads=tC.keys, writes=tD.keys)
            S.op("dve", lambda e: e.tensor_scalar(out=tD.ap, in0=tD.ap, scalar1=-1.0, scalar2=PI / 2, op0=ALU.mult, op1=ALU.add),
                 reads=tD.keys, writes=tD.keys)
            S.op("act", lambda e: e.activation(out=cosb.ap, in_=tD.ap, func=AF.Sin), reads=tD.keys, writes=cosb.keys)

            chk(3)
            qT = [AB(8, 1, BF16, 512), AB(9, 1, BF16, 512)]
            qd = [AB(10, 1, BF16, 512), AB(11, 1, BF16, 512)]
            kT = [AB(12, 1, BF16, 512), AB(13, 1, BF16, 512)]
            ktok = AB(14, 2, BF16, 1024)
            vtok = AB(16, 2, BF16, 1024)
            sg = [AB(18, 1, BF16, 512), AB(19, 1, BF16, 512)]
            Pb = AB(20, 1, BF16, 256)
            Sb = [AB(21, 1, BF16, 512), AB(22, 1, BF16, 512)]
            ysb = [AB(23, 1, BF16, 512), AB(24, 1, BF16, 512)]
            ysq = [AB(25, 1, BF16, 512), AB(26, 1, BF16, 512)]
            sbi = 0
            AL[0] = [0, 1, 2, 5, 6, 7]
            for h in range(8):
                slot = next_w(f"qk{h}")
                for which in ([1] if PRE else [0, 1]):
                    ba = bankA()
                    proj_ws(slot, which * 256, 0, hT_fn, [("hT",)], ba)
                    bb = bankA()
                    proj_ws(slot, which * 256, 1, hT_fn, [("hT",)], bb)
                    dstT = qT if which == 0 else kT
                    S.op("dve", lambda e, ba=ba: e.tensor_tensor(out=tA.ap, in0=PB(ba), in1=cosb.ap, op=ALU.mult), reads=[pk(ba)] + cosb.keys, writes=tA.keys)
                    S.op("dve", lambda e, bb=bb: e.tensor_tensor(out=tB.ap, in0=PB(bb), in1=sinb.ap, op=ALU.mult), reads=[pk(bb)] + sinb.keys, writes=tB.keys)
                    S.op("pool", lambda e, dstT=dstT: e.tensor_tensor(out=dstT[0].ap, in0=tA.ap, in1=tB.ap, op=ALU.subtract), reads=tA.keys + tB.keys, writes=dstT[0].keys)
                    S.op("dve", lambda e, bb=bb: e.tensor_tensor(out=tC.ap, in0=PB(bb), in1=cosb.ap, op=ALU.mult), reads=[pk(bb)] + cosb.keys, writes=tC.keys)
                    S.op("dve", lambda e, ba=ba: e.tensor_tensor(out=tD.ap, in0=PB(ba), in1=sinb.ap, op=ALU.mult), reads=[pk(ba)] + sinb.keys, writes=tD.keys)
                    S.op("pool", lambda e, dstT=dstT: e.tensor_tensor(out=dstT[1].ap, in0=tC.ap, in1=tD.ap, op=ALU.add), reads=tC.keys + tD.keys, writes=dstT[1].keys)
                    if which == 0:
                        for u in range(2):
                            S.op("pool", lambda e, u=u, h=h: e.tensor_tensor(out=qd[u].ap.rearrange("p (c l) -> p c l", l=64),
                                                                               in0=qT[u].ap.rearrange("p (c l) -> p c l", l=64),
                                                                               in1=qdec[:, h:h + 1, :].to_broadcast([128, 8, 64]), op=ALU.mult),
                                 reads=qT[u].keys + ["qdec"], writes=qd[u].keys)
                slot = next_w(f"vg{h}")
                for j in range(4):
                    bk = bankA()
                    proj_as(slot, 0, 256, j, hT_fn, [("hT",)], bk)
                    S.op("act", lambda e, bk=bk, j=j: e.activation(out=vtok.ap[:, j * 256:(j + 1) * 256], in_=PB(bk)[:, 0:256], func=AF.Copy),
                         reads=[pk(bk)], writes=vtok.keys)
                S.skip = PRE
                for u in range(2):
                    bk = bankA()
                    proj_ws(slot, 256, u, hT_fn, [("hT",)], bk)
                    S.op("act", lambda e, bk=bk, u=u: e.activation(out=sg[u].ap, in_=PB(bk), func=AF.Silu), reads=[pk(bk)], writes=sg[u].keys)
                S.skip = False
                for j in range(4):
                    bk = bankB()
                    for u in range(2):
                        S.op("pe", lambda e, bk=bk, u=u, j=j: e.transpose(PBb(bk)[:, u * 128:(u + 1) * 128], kT[u].ap[:, j * 128:(j + 1) * 128], ident[:]),
                             reads=kT[u].keys + ["ident"], writes=[pk(bk)], inc=(u == 1))
                    S.op("act", lambda e, bk=bk, j=j, h=h: e.activation(out=ktok.ap[:, j * 256:(j + 1) * 256], in_=PBb(bk)[:, 0:256], func=AF.Identity, scale=kdec[:, h:h + 1]),
                         reads=[pk(bk), "kdec"], writes=ktok.keys)
                S.skip = PRE
                BL[0] = [5, 6, 7]
                by = [3, 4]
                Sh = Sret[:, h, :]
                skey = ("Sret", h)
                cur = Sb[sbi % 2]
                sbi += 1
                S.op("act", lambda e, cur=cur, Sh=Sh: e.activation(out=cur.ap, in_=Sh, func=AF.Copy), reads=[skey], writes=cur.keys)
                for j in range(4):
                    S.skip = PRE
                    blk = slice(j * 128, (j + 1) * 128)
                    bsc = bankB()
                    for u in range(2):
                        S.op("pe", lambda e, bsc=bsc, u=u, blk=blk: e.matmul(PB(bsc)[:, 0:128], lhsT=kT[u].ap[:, blk], rhs=qT[u].ap[:, blk], start=(u == 0), stop=(u == 1)),
                             reads=kT[u].keys + qT[u].keys, writes=[pk(bsc)], inc=(u == 1))
                    pslot = Pb.ap[:, (j % 2) * 128:(j % 2 + 1) * 128]
                    S.op("dve", lambda e, bsc=bsc, pslot=pslot, h=h: e.tensor_tensor(out=pslot, in0=PB(bsc)[:, 0:128], in1=retmask[:, h, :], op=ALU.mult),
                         reads=[pk(bsc), "retmask"], writes=Pb.keys)
                    for u in range(2):
                        S.op("pe", lambda e, u=u, j=j, pslot=pslot, blk=blk, by=by: e.matmul(PB(by[u])[:, blk], lhsT=vtok.ap[:, j * 256 + u * 128: j * 256 + (u + 1) * 128], rhs=pslot,
                                                                                  start=(j == 0), stop=False),
                             reads=vtok.keys + Pb.keys, writes=[pk(by[u])], inc=False)
                    for c in range(2):
                        S.skip = PRE
                        csl = slice(j * 128 + c * 64, j * 128 + (c + 1) * 64)
                        for u in range(2):
                            for dh in range(2):
                                last = (j == 3 and c == 1 and dh == 1)
                                S.op("pe", lambda e, u=u, dh=dh, cur=cur, csl=csl, last=last, by=by: e.matmul(PB(by[u])[:, csl], lhsT=cur.ap[:, dh * 256 + u * 128: dh * 256 + (u + 1) * 128],
                                                                                                 rhs=qd[dh].ap[:, csl], start=False, stop=last),
                                     reads=cur.keys + qd[dh].keys, writes=[pk(by[u])], inc=last)
                        S.skip = False
                        bsu = bankB()
                        rows = slice(c * 64, (c + 1) * 64)
                        for dh in range(2):
                            S.op("pe", lambda e, bsu=bsu, dh=dh, rows=rows, j=j: e.matmul(PB(bsu)[:, dh * 256:(dh + 1) * 256], lhsT=ktok.ap[rows, j * 256 + dh * 128: j * 256 + (dh + 1) * 128],
                                                                                       rhs=vtok.ap[rows, j * 256:(j + 1) * 256], start=(dh == 0), stop=(dh == 1)),
                                 reads=ktok.keys + vtok.keys, writes=[pk(bsu)], inc=(dh == 1))
                        S.op("dve", lambda e, bsu=bsu, Sh=Sh, h=h: e.scalar_tensor_tensor(out=Sh, in0=Sh, scalar=cdec[h], in1=PB(bsu), op0=ALU.mult, op1=ALU.add),
                             reads=[pk(bsu), skey], writes=[skey])
                        S.skip = PRE
                        if not (j == 3 and c == 1):
                            cur = Sb[sbi % 2]
                            sbi += 1
                            S.op("act", lambda e, cur=cur, Sh=Sh: e.activation(out=cur.ap, in_=Sh, func=AF.Copy), reads=[skey], writes=cur.keys)
                for u in range(2):
                    S.op("act", lambda e, u=u, by=by: e.activation(out=ysb[u].ap, in_=PB(by[u]), func=AF.Copy), reads=[pk(by[u])], writes=ysb[u].keys)
                    S.op("act", lambda e, u=u, by=by: e.activation(out=ysq[u].ap, in_=PB(by[u]), func=AF.Square), reads=[pk(by[u])], writes=ysq[u].keys)
                bm = bankB()
                be = bankB()
                for u in range(2):
                    S.op("pe", lambda e, u=u, bm=bm: e.matmul(PB(bm), lhsT=onesS[:], rhs=ysb[u].ap, start=(u == 0), stop=(u == 1)),
                         reads=["onesS"] + ysb[u].keys, writes=[pk(bm)], inc=(u == 1))
                for u in range(2):
                    S.op("pe", lambda e, u=u, be=be: e.matmul(PB(be), lhsT=onesS[:], rhs=ysq[u].ap, start=(u == 0), stop=(u == 1)),
                         reads=["onesS"] + ysq[u].keys, writes=[pk(be)], inc=(u == 1))
                S.op("act", lambda e, bm=bm: e.activation(out=tA.ap, in_=PB(bm), func=AF.Copy), reads=[pk(bm)], writes=tA.keys)
                S.op("act", lambda e, bm=bm: e.activation(out=tB.ap, in_=PB(bm), func=AF.Square), reads=[pk(bm)], writes=tB.keys)
                S.op("dve", lambda e, be=be: e.tensor_tensor(out=tB.ap, in0=PB(be), in1=tB.ap, op=ALU.subtract), reads=[pk(be)] + tB.keys, writes=tB.keys)
                S.op("dve", lambda e: e.tensor_scalar(out=tB.ap, in0=tB.ap, scalar1=0.0, scalar2=EPS, op0=ALU.max, op1=ALU.add), reads=tB.keys, writes=tB.keys)
                S.op("act", lambda e: e.activation(out=tB.ap, in_=tB.ap, func=AF.Sqrt), reads=tB.keys, writes=tB.keys)
                S.op("dve", lambda e: e.reciprocal(tC.ap, tB.ap), reads=tB.keys, writes=tC.keys)
                for u in range(2):
                    S.op("dve", lambda e, u=u, by=by: e.tensor_tensor(out=tD.ap, in0=PB(by[u]), in1=tA.ap, op=ALU.subtract), reads=[pk(by[u])] + tA.keys, writes=tD.keys)
                    S.op("dve", lambda e: e.tensor_tensor(out=tD.ap, in0=tD.ap, in1=tC.ap, op=ALU.mult), reads=tD.keys + tC.keys, writes=tD.keys)
                    S.op("pool", lambda e, u=u, h=h: e.tensor_tensor(out=yR[:, 2 * h + u, :], in0=tD.ap, in1=sg[u].ap, op=ALU.mult),
                         reads=tD.keys + sg[u].keys, writes=[("yR", 2 * h + u)])
                S.skip = False

            BL[0] = [3, 4, 5, 6, 7]
            AL[0] = [0, 1, 2]
            chk(4)
            if ti == 0:
                dump("d_yR", yR[:].rearrange("p a b -> p (a b)"), [("yR",)], [128, 16 * 512], BF16)
                dump("d_Sret", Sret[:].rearrange("p a b -> p (a b)"), [("Sret",)], [128, 8 * 512], F32)
            if ti == 0:
                dtt = sb("dtt", [128, 3, 64], F32)
            dtxB = AB(0, 1, F32, 256)
            dtaB = AB(1, 1, F32, 256)
            dexB = AB(2, 4, F32, 1024)
            dtx = dtxB.ap.rearrange("p (j h) -> p j h", h=64)
            dta = dtaB.ap.rearrange("p (j h) -> p j h", h=64)
            dex = dexB.ap.rearrange("p (j h) -> p j h", h=256)
            slot = next_w("dt")
            for j in range(4):
                bk = bankA()
                proj_as(slot, 0, 64, j, hT_fn, [("hT",)], bk)
                S.op("dve", lambda e, bk=bk: e.tensor_tensor(out=dtt[:, 0, :], in0=PB(bk)[:, 0:64], in1=dtb[:], op=ALU.add),
                     reads=[pk(bk), "dtb"], writes=[("dtt", 0)])
                S.op("act", lambda e: e.activation(out=dtt[:, 1, :], in_=dtt[:, 0, :], func=AF.Abs),
                     reads=[("dtt", 0)], writes=[("dtt", 1)])
                S.op("act", lambda e: e.activation(out=dtt[:, 1, :], in_=dtt[:, 1, :], func=AF.Exp, scale=-1.0), reads=[("dtt", 1)], writes=[("dtt", 1)])
                S.op("act", lambda e: e.activation(out=dtt[:, 2, :], in_=dtt[:, 1, :], func=AF.Ln, bias=1.0), reads=[("dtt", 1)], writes=[("dtt", 2)])
                S.op("dve", lambda e, j=j: e.scalar_tensor_tensor(out=dtx[:, j, :], in0=dtt[:, 0, :], scalar=0.0, in1=dtt[:, 2, :], op0=ALU.max, op1=ALU.add),
                     reads=[("dtt", 0), ("dtt", 2)], writes=[dtxB.keys[0]])
                S.op("dve", lambda e, j=j: e.tensor_tensor(out=dta[:, j, :], in0=dtx[:, j, :], in1=arow[:], op=ALU.mult),
                     reads=[dtxB.keys[0], "arow"], writes=[dtaB.keys[0]])
                bk2 = bankB()
                for qi, msk in enumerate([TRI, UBD, ONC0, ONC1]):
                    S.op("pe", lambda e, bk2=bk2, qi=qi, msk=msk, j=j: e.matmul(PB(bk2)[:, qi * 64:(qi + 1) * 64], lhsT=msk, rhs=dta[:, j, :], start=(qi == 0), stop=(qi == 3)),
                         reads=["ssdm", dtaB.keys[0]], writes=[pk(bk2)], inc=(qi == 3))
                S.op("act", lambda e, bk2=bk2, j=j: e.activation(out=dex[:, j, :], in_=PB(bk2)[:, 0:256], func=AF.Exp), reads=[pk(bk2)], writes=dexB.keys)

            chk(5)
            BCpre = [AB(6, 1, BF16, 516), AB(7, 1, BF16, 516)]
            BT = AB(8, 1, BF16, 512)
            CT = AB(9, 1, BF16, 512)
            Btok = AB(10, 1, BF16, 512)
            CT0 = AB(11, 1, BF16, 512)
            CT1 = AB(12, 1, BF16, 512)
            xpre = [AB(13 + m, 1, BF16, 516) for m in range(4)]
            xsT = [AB(17 + m, 1, BF16, 512) for m in range(4)]
            zs = AB(21, 1, BF16, 512)
            diagc = [AB(22, 1, BF16, 512), AB(23, 1, BF16, 512)]
            diagD = AB(24, 1, BF16, 512)
            Rb = AB(25, 4, F32, 1024)
            dec = AB(29, 2, BF16, 1024)
            cbTm = AB(31, 1, F32, 128)
            xdt = AB(32, 1, BF16, 512)
            xdtt = AB(33, 1, BF16, 512)
            tmpf = AB(34, 2, F32, 512)
            yg = AB(36, 1, BF16, 512)
            stb = [AB(37, 1, BF16, 512), AB(38, 1, BF16, 512)]
            junk = zs
            dci = 0
            sti = 0
            xh = {"slot": None, "banks": []}
            for g in range(8):
                S.skip = False
                Sg = Sssd[:, g, :]
                sgk = ("Sssd", g)
                if xh["slot"] is None:
                    slot = next_w(f"x{g}")
                else:
                    slot = xh["slot"]
                pres = []
                for m in range(4):
                    if m < len(xh["banks"]):
                        bk = xh["banks"][m]
                    else:
                        bk = bankA()
                        proj_ws(slot, 0, m, hT_fn, [("hT",)], bk)
                    S.op("act", lambda e, bk=bk, m=m: e.activation(out=xpre[m].ap[:, 3:515], in_=PB(bk), func=AF.Copy), reads=[pk(bk)], writes=xpre[m].keys)
                    pres.append((xpre[m], g * 4 + m, xsT[m]))
                slot = next_w(f"bc{g}")
                for m in range(2):
                    bk = bankA()
                    proj_ws(slot, 0, m, hT_fn, [("hT",)], bk)
                    S.op("act", lambda e, bk=bk, m=m: e.activation(out=BCpre[m].ap[:, 3:515], in_=PB(bk), func=AF.Copy), reads=[pk(bk)], writes=BCpre[m].keys)
                    pres.append((BCpre[m], 32 + m * 8 + g, BT if m == 0 else CT))
                for (pre, c48, post) in pres:
                    S.op("dve", lambda e, pre=pre, c48=c48: e.tensor_copy(pre.ap[:, 0:3], halo[:, c48, 0:3]), reads=[("halo", c48)], writes=pre.keys)
                    S.op("pool", lambda e, pre=pre, c48=c48: e.tensor_copy(halo[:, c48, 0:3], pre.ap[:, 512:515]), reads=pre.keys, writes=[("halo", c48)])
                    S.skip = PRE and (post is CT)
                    dg = diagc[dci % 2]
                    dci += 1
                    for k in range(4):
                        S.op("pool", lambda e, dg=dg, k=k, c48=c48: e.tensor_scalar(out=dg.ap[:, k * 128:(k + 1) * 128], in0=ident[:], scalar1=convw[:, c48, k:k + 1], scalar2=1.0, op0=ALU.mult, op1=ALU.mult),
                             reads=["ident", "convw"], writes=dg.keys)
                    bk = bankB()
                    for k in range(4):
                        S.op("pe", lambda e, bk=bk, dg=dg, k=k, pre=pre: e.matmul(PB(bk), lhsT=dg.ap[:, k * 128:(k + 1) * 128], rhs=pre.ap[:, k:k + 512], start=(k == 0), stop=(k == 3)),
                             reads=dg.keys + pre.keys, writes=[pk(bk)], inc=(k == 3))
                    S.op("act", lambda e, bk=bk, post=post, c48=c48: e.activation(out=post.ap, in_=PB(bk), func=AF.Silu, bias=convb[:, c48:c48 + 1]),
                         reads=[pk(bk), "convb"], writes=post.keys)
                    S.skip = False
                S.skip = PRE
                for m in range(4):
                    S.op("pool", lambda e, m=m, g=g: e.tensor_scalar(out=diagD.ap[:, m * 128:(m + 1) * 128], in0=ident[:], scalar1=dch[:, g * 4 + m: g * 4 + m + 1], scalar2=1.0, op0=ALU.mult, op1=ALU.mult),
                         reads=["ident", "dch"], writes=diagD.keys)
                S.op("pool", lambda e: e.memset(CT0.ap, 0.0), writes=CT0.keys)
                S.op("pool", lambda e: e.memset(CT1.ap, 0.0), writes=CT1.keys)
                ct4 = CT.ap.rearrange("p (j c l) -> p j c l", c=2, l=64)
                S.op("pool", lambda e, ct4=ct4: e.tensor_copy(CT0.ap.rearrange("p (j c l) -> p j c l", c=2, l=64)[:, :, 0, :], ct4[:, :, 0, :]), reads=CT.keys, writes=CT0.keys)
                S.op("pool", lambda e, ct4=ct4: e.tensor_copy(CT1.ap.rearrange("p (j c l) -> p j c l", c=2, l=64)[:, :, 1, :], ct4[:, :, 1, :]), reads=CT.keys, writes=CT1.keys)
                S.skip = False
                slot = None if PRE else next_w(f"z{g}")
                hs = slice(g * 8, (g + 1) * 8)
                BL[0] = [6, 7]
                zs4 = [AB(13 + j, 1, BF16, 512) for j in range(4)]
                S.skip = PRE
                for j in range(4):
                    bz = bankA()
                    proj_as(slot, 0, 512, j, hT_fn, [("hT",)], bz)
                    S.op("act", lambda e, bz=bz, j=j, zs4=zs4: e.activation(out=zs4[j].ap, in_=PB(bz), func=AF.Silu), reads=[pk(bz)], writes=zs4[j].keys)
                S.skip = False

                def P1(j):
                    blk = slice(j * 128, (j + 1) * 128)
                    S.skip = PRE
                    bk = bankB()
                    S.op("pe", lambda e, bk=bk, blk=blk: e.matmul(PB(bk)[:, 0:128], lhsT=BT.ap[:, blk], rhs=CT.ap[:, blk], start=True, stop=True),
                         reads=BT.keys + CT.keys, writes=[pk(bk)])
                    S.op("dve", lambda e, bk=bk: e.tensor_tensor(out=cbTm.ap, in0=PB(bk)[:, 0:128], in1=MBD, op=ALU.mult), reads=[pk(bk), "ssdm"], writes=cbTm.keys)
                    S.skip = False
                    bk = bankB()
                    S.op("pe", lambda e, bk=bk, blk=blk: e.transpose(PBb(bk)[:, 0:128], BT.ap[:, blk], ident[:]), reads=BT.keys + ["ident"], writes=[pk(bk)])
                    S.op("act", lambda e, bk=bk, j=j: e.activation(out=Btok.ap[:, j * 128:(j + 1) * 128], in_=PBb(bk)[:, 0:128], func=AF.Copy), reads=[pk(bk)], writes=Btok.keys)
                    S.skip = PRE
                    S.op("pool", lambda e, j=j, hs=hs: e.tensor_tensor(out=Rb.ap.rearrange("p (h l) -> p h l", l=128),
                                                                in0=TRI.unsqueeze(1).to_broadcast([128, 8, 128]),
                                                                in1=dta[:, j, hs].unsqueeze(2).to_broadcast([128, 8, 128]), op=ALU.mult),
                         reads=["ssdm", dtaB.keys[0]], writes=Rb.keys)
                    bs = [bankB(), bankB()]
                    for q in range(2):
                        S.op("pe", lambda e, q=q, bs=bs: e.matmul(PB(bs[q]), lhsT=UU, rhs=Rb.ap[:, q * 512:(q + 1) * 512], start=True, stop=True),
                             reads=["ssdm"] + Rb.keys, writes=[pk(bs[q])])
                        S.op("act", lambda e, q=q, bs=bs: e.activation(out=dec.ap[:, q * 512:(q + 1) * 512], in_=PB(bs[q]), func=AF.Exp), reads=[pk(bs[q])], writes=dec.keys)
                    S.op("dve", lambda e: e.tensor_tensor(out=dec.ap.rearrange("p (h l) -> p h l", l=128), in0=dec.ap.rearrange("p (h l) -> p h l", l=128),
                                                          in1=cbTm.ap.unsqueeze(1).to_broadcast([128, 8, 128]), op=ALU.mult),
                         reads=dec.keys + cbTm.keys, writes=dec.keys)
                    S.skip = False
                    bk = bankB()
                    for m in range(4):
                        S.op("pe", lambda e, bk=bk, m=m, blk=blk: e.transpose(PBb(bk)[:, m * 128:(m + 1) * 128], xsT[m].ap[:, blk], ident[:]),
                             reads=xsT[m].keys + ["ident"], writes=[pk(bk)], inc=(m == 3))
                    S.op("dve", lambda e, bk=bk, j=j, hs=hs: e.tensor_tensor(out=xdt.ap.rearrange("p (h q) -> p h q", q=64), in0=PBb(bk)[:, 0:512].rearrange("p (h q) -> p h q", q=64),
                                                                    in1=dtx[:, j, hs].unsqueeze(2).to_broadcast([128, 8, 64]), op=ALU.mult),
                         reads=[pk(bk), dtxB.keys[0]], writes=xdt.keys)
                    S.op("pool", lambda e, j=j, g=g: e.tensor_tensor(out=xdtt.ap.rearrange("p (h q) -> p h q", q=64), in0=xdt.ap.rearrange("p (h q) -> p h q", q=64),
                                                                     in1=dex[:, j, 64 + g * 8: 64 + (g + 1) * 8].unsqueeze(2).to_broadcast([128, 8, 64]), op=ALU.mult),
                         reads=xdt.keys + dexB.keys, writes=xdtt.keys)

                def P2a(j):
                    blk = slice(j * 128, (j + 1) * 128)
                    S.skip = PRE
                    bys = 3
                    for m in range(4):
                        S.op("pe", lambda e, m=m, blk=blk: e.matmul(PB(3)[:, m * 128:(m + 1) * 128], lhsT=xsT[m].ap[:, blk], rhs=diagD.ap[:, m * 128:(m + 1) * 128],
                                                                   start=(m == 0), stop=False),
                             reads=xsT[m].keys + diagD.keys, writes=[pk(bys)], inc=False)
                    for hh in range(8):
                        S.op("pe", lambda e, hh=hh: e.matmul(PB(3)[:, hh * 64:(hh + 1) * 64], lhsT=dec.ap[:, hh * 128:(hh + 1) * 128], rhs=xdt.ap[:, hh * 64:(hh + 1) * 64],
                                                            start=False, stop=(hh == 7)),
                             reads=dec.keys + xdt.keys, writes=[pk(bys)], inc=(hh == 7))
                    S.skip = False
                    for c in range(2):
                        rows = slice(c * 64, (c + 1) * 64)
                        S.op("pe", lambda e, c=c, rows=rows, j=j: e.matmul(PB(4 + c), lhsT=Btok.ap[rows, j * 128:(j + 1) * 128], rhs=xdtt.ap[rows, :], start=True, stop=True),
                             reads=Btok.keys + xdtt.keys, writes=[pk(4 + c)])

                def P2b(j):
                    nonlocal_sti = sti_box
                    blk = slice(j * 128, (j + 1) * 128)
                    bys = 3
                    S.skip = PRE
                    byi = bankB()
                    for c in range(2):
                        S.skip = PRE
                        cur = stb[nonlocal_sti[0] % 2]
                        nonlocal_sti[0] += 1
                        S.op("act", lambda e, cur=cur, Sg=Sg: e.activation(out=cur.ap, in_=Sg, func=AF.Copy), reads=[sgk], writes=cur.keys)
                        ctp = CT0 if c == 0 else CT1
                        S.op("pe", lambda e, byi=byi, cur=cur, ctp=ctp, c=c, blk=blk: e.matmul(PB(byi), lhsT=ctp.ap[:, blk], rhs=cur.ap, start=(c == 0), stop=(c == 1)),
                             reads=ctp.keys + cur.keys, writes=[pk(byi)], inc=(c == 1))
                        S.skip = False
                        cofs = 128 + c * 64 + g * 8
                        S.op("dve", lambda e, j=j, cofs=cofs, Sg=Sg: e.tensor_tensor(out=Sg.rearrange("p (h q) -> p h q", q=64), in0=Sg.rearrange("p (h q) -> p h q", q=64),
                                                                               in1=dex[:, j, cofs:cofs + 8].unsqueeze(2).to_broadcast([128, 8, 64]), op=ALU.mult),
                             reads=[sgk] + dexB.keys, writes=[sgk])
                        S.op("dve", lambda e, c=c, Sg=Sg: e.tensor_tensor(out=Sg, in0=PB(4 + c), in1=Sg, op=ALU.add), reads=[pk(4 + c), sgk], writes=[sgk])
                    S.skip = PRE
                    S.op("dve", lambda e, byi=byi, j=j, hs=hs: e.tensor_tensor(out=tmpf.ap.rearrange("p (h q) -> p h q", q=64), in0=PB(byi).rearrange("p (h q) -> p h q", q=64),
                                                                      in1=dex[:, j, hs].unsqueeze(2).to_broadcast([128, 8, 64]), op=ALU.mult),
                         reads=[pk(byi)] + dexB.keys, writes=tmpf.keys)
                    S.op("dve", lambda e: e.tensor_tensor(out=tmpf.ap, in0=PB(3), in1=tmpf.ap, op=ALU.add), reads=[pk(bys)] + tmpf.keys, writes=tmpf.keys)
                    S.op("dve", lambda e, j=j, zs4=zs4: e.tensor_tensor(out=yg.ap, in0=tmpf.ap, in1=zs4[j].ap, op=ALU.mult), reads=tmpf.keys + zs4[j].keys, writes=yg.keys)
                    S.op("act", lambda e, j=j, g=g: e.activation(out=junk.ap, in_=yg.ap, func=AF.Square, accum_out=ssq[:, j, g:g + 1]), reads=yg.keys, writes=junk.keys + [("ssq", j, g)])
                    bk = bankB()
                    for m in range(4):
                        S.op("pe", lambda e, bk=bk, m=m: e.transpose(PBb(bk)[:, m * 128:(m + 1) * 128], yg.ap[:, m * 128:(m + 1) * 128], ident[:]),
                             reads=yg.keys + ["ident"], writes=[pk(bk)], inc=(m == 3))
                    S.op("act", lambda e, bk=bk, g=g, blk=blk: e.activation(out=ysT[:, g * 4:(g + 1) * 4, blk], in_=PBb(bk)[:, 0:512].rearrange("p (a b) -> p a b", b=128), func=AF.Copy),
                         reads=[pk(bk)], writes=[("ysT", g * 4 + m2) for m2 in range(4)])
                    S.skip = False

                sti_box = [sti]
                P1(0)
                xh = {"slot": None, "banks": []}
                for j in range(4):
                    P2a(j)
                    if j + 1 < 4:
                        P1(j + 1)
                    elif g + 1 < 8:
                        S.skip = False
                        xh["slot"] = next_w(f"x{g + 1}")
                        for m_ in range(3):
                            proj_ws(xh["slot"], 0, m_, hT_fn, [("hT",)], m_)
                            xh["banks"].append(m_)
                    P2b(j)
                sti = sti_box[0]
                BL[0] = [3, 4, 5, 6, 7]

            S.skip = False
            chk(6)
            if PRE:
                continue
            if ti == 0:
                dump("d_ysT", ysT[:].rearrange("p a b -> p (a b)"), [("ysT",)], [128, 32 * 512], BF16)
                dump("d_ssq", ssq[:].rearrange("p a b -> p (a b)"), [("ssq",)], [128, 32], F32)
                dump("d_Sssd", Sssd[:].rearrange("p a b -> p (a b)"), [("Sssd",)], [128, 8 * 512], F32)
            sgr = [AB(m, 1, BF16, 512) for m in range(16)]
            sgs = [AB(16 + m, 1, BF16, 512) for m in range(16)]
            rsrow = AB(32, 2, F32, 512)
            dgf = AB(34, 1, F32, 128)
            t1 = AB(35, 2, F32, 512)
            t2 = AB(37, 2, F32, 512)
            S.op("dve", lambda e: e.reduce_sum(sml[:, 4:8], ssq[:], axis=mybir.AxisListType.X), reads=[("ssq",)], writes=[("sml", 4)])
            S.op("dve", lambda e: e.tensor_scalar(out=sml[:, 4:8], in0=sml[:, 4:8], scalar1=1.0 / 4096, scalar2=EPS, op0=ALU.mult, op1=ALU.add), reads=[("sml", 4)], writes=[("sml", 4)])
            S.op("act", lambda e: e.activation(out=sml[:, 4:8], in_=sml[:, 4:8], func=AF.Sqrt), reads=[("sml", 4)], writes=[("sml", 4)])
            S.op("dve", lambda e: e.reciprocal(sml[:, 8:12], sml[:, 4:8]), reads=[("sml", 4)], writes=[("sml", 8)])
            brs = bankB()
            for j in range(4):
                S.op("dve", lambda e, j=j: e.tensor_scalar_mul(dgf.ap, identf[:], sml[:, 8 + j:9 + j]), reads=["identf", ("sml", 8)], writes=dgf.keys)
                S.op("pe", lambda e, j=j, brs=brs: e.matmul(PB(brs)[:, j * 128:(j + 1) * 128], lhsT=onesf[:], rhs=dgf.ap, start=(j == 0), stop=(j == 3)),
                     reads=["onesf"] + dgf.keys, writes=[pk(brs)])
            S.op("act", lambda e, brs=brs: e.activation(out=rsrow.ap, in_=PB(brs), func=AF.Copy), reads=[pk(brs)], writes=rsrow.keys)
            yR_fn = lambda kc: yR[:, kc, :]
            ys_fn = lambda kc: ysT[:, kc, :]
            for oc in range(4):
                slot = next_w(f"gr{oc}")
                for m in range(4):
                    bk = bankA()
                    proj_ws(slot, 0, m, hT_fn, [("hT",)], bk)
                    S.op("act", lambda e, bk=bk, m=m, oc=oc: e.activation(out=sgr[oc * 4 + m].ap, in_=PB(bk), func=AF.Sigmoid), reads=[pk(bk)], writes=sgr[oc * 4 + m].keys)
                slot = next_w(f"gs{oc}")
                for m in range(4):
                    bk = bankA()
                    proj_ws(slot, 0, m, hT_fn, [("hT",)], bk)
                    S.op("act", lambda e, bk=bk: e.activation(out=t1.ap, in_=PB(bk), func=AF.Sigmoid), reads=[pk(bk)], writes=t1.keys)
                    S.op("dve", lambda e, m=m, oc=oc: e.tensor_tensor(out=sgs[oc * 4 + m].ap, in0=t1.ap, in1=rsrow.ap, op=ALU.mult), reads=t1.keys + rsrow.keys, writes=sgs[oc * 4 + m].keys)
            for oc in range(4):
                slot_r = next_w(f"br{oc}")
                prb = []
                for m in range(4):
                    bk = bankB()
                    proj_ws(slot_r, 0, m, yR_fn, [("yR",)], bk)
                    prb.append(bk)
                slot0 = next_w(f"bs0{oc}")
                slot1 = next_w(f"bs1{oc}", prefetch=False)
                for m in range(4):
                    bk = bankA()
                    proj_ws(slot0, 0, m, ys_fn, [("ysT",)], bk, first=True, last=False, kofs=0)
                    proj_ws(slot1, 0, m, ys_fn, [("ysT",)], bk, first=False, last=True, kofs=16)
                    S.op("dve", lambda e, m=m, prb=prb, oc=oc: e.tensor_tensor(out=t1.ap, in0=PB(prb[m]), in1=sgr[oc * 4 + m].ap, op=ALU.mult), reads=[pk(prb[m])] + sgr[oc * 4 + m].keys, writes=t1.keys)
                    S.op("dve", lambda e, m=m, bk=bk, oc=oc: e.tensor_tensor(out=t2.ap, in0=PB(bk), in1=sgs[oc * 4 + m].ap, op=ALU.mult), reads=[pk(bk)] + sgs[oc * 4 + m].keys, writes=t2.keys)
                    S.op("pool", lambda e, m=m, oc=oc: e.tensor_tensor(out=hT[:, oc * 4 + m, :], in0=t1.ap, in1=t2.ap, op=ALU.add),
                         reads=t1.keys + t2.keys, writes=[("hT",)])
                    if m == 3:
                        prefetch_next()
            chk(7)
            if ti == 0:
                dump("d_mrg", hT[:].rearrange("p a b -> p (a b)"), [("hT",)], [128, 16 * 512], BF16)
            xr = [AB(8 * j, 8, F32, 2048) for j in range(4)]
            for j in range(4):
                r0 = tok0 + j * 128
                S.op("sp", lambda e, j=j, r0=r0: e.dma_start(out=xr[j].ap, in_=x[r0:r0 + 128, :]), writes=xr[j].keys, dsem=f"xr{j}")
            mrg_fn = hT_fn
            for oc in range(4):
                slot = next_w(f"o{oc}")
                for j in range(4):
                    bk = bankA()
                    proj_as(slot, 0, 512, j, mrg_fn, [("hT",)], bk)
                    S.op("dve", lambda e, bk=bk, j=j, oc=oc: e.tensor_tensor(out=xr[j].ap[:, oc * 512:(oc + 1) * 512], in0=PB(bk), in1=xr[j].ap[:, oc * 512:(oc + 1) * 512], op=ALU.add),
                         reads=[pk(bk)] + xr[j].keys, writes=xr[j].keys)
            for j in range(4):
                r0 = tok0 + j * 128
                S.op("act", lambda e, j=j: e.activation(out=hb, in_=xr[j].ap, func=AF.Square, accum_out=sml[:, 12:13]), reads=xr[j].keys, writes=HBK + [("sml", 12)])
                S.op("dve", lambda e: e.tensor_scalar(out=sml[:, 13:14], in0=sml[:, 12:13], scalar1=1.0 / D, scalar2=EPS, op0=ALU.mult, op1=ALU.add), reads=[("sml", 12)], writes=[("sml", 13)])
                S.op("act", lambda e: e.activation(out=sml[:, 14:15], in_=sml[:, 13:14], func=AF.Sqrt), reads=[("sml", 13)], writes=[("sml", 14)])
                S.op("dve", lambda e: e.reciprocal(sml[:, 15:16], sml[:, 14:15]), reads=[("sml", 14)], writes=[("sml", 15)])
                S.op("dve", lambda e, j=j: e.scalar_tensor_tensor(out=xr[j].ap, in0=xr[j].ap, scalar=sml[:, 15:16], in1=normf[:], op0=ALU.mult, op1=ALU.mult),
                     reads=xr[j].keys + [("sml", 15), "normf"], writes=xr[j].keys)
                ro = (ti - NPRE) * T + j * 128
                S.op("sp", lambda e, j=j, ro=ro: e.dma_start(out=out[ro:ro + 128, :], in_=xr[j].ap), reads=xr[j].keys, dsem=f"o{j}")

        except _Stop:
            pass
        S.emit(st)
    return nc


def _consts():
    c = {}
    c["c_ident"] = np.eye(128, dtype=np.float32).astype(ml_dtypes.bfloat16)
    c["c_identf"] = np.eye(128, dtype=np.float32)
    c["c_onesf"] = np.ones((128, 128), np.float32)
    c["c_ones"] = np.full((128, 128), 1.0 / 256.0, np.float32).astype(ml_dtypes.bfloat16)
    idx = np.arange(128)
    same = (idx[:, None] // 64) == (idx[None, :] // 64)
    lg = np.log1p(-(2.0 ** (-5.0 - np.arange(8, dtype=np.float64))))
    rm = np.zeros((128, 8, 128), np.float64)
    for h in range(8):
        rm[:, h, :] = np.where(same, np.exp(np.abs(idx[:, None] - idx[None, :]) * lg[h]), 0.0) * (256.0 ** -0.5)
    c["c_retmask"] = rm.reshape(128, 1024).astype(np.float32).astype(ml_dtypes.bfloat16)
    j = idx[:, None]
    l = idx[None, :]
    TRI = (same & (j <= l)).astype(np.float32)
    UBD = (same & (j > l)).astype(np.float32)
    UU = (j > l).astype(np.float32)
    ONC0 = np.repeat((idx < 64).astype(np.float32)[:, None], 128, 1)
    ONC1 = np.repeat((idx >= 64).astype(np.float32)[:, None], 128, 1)
    MBD = (same & (l >= j)).astype(np.float32)
    c["c_ssd"] = np.stack([TRI, UBD, UU, ONC0, ONC1, MBD], 1).reshape(128, 768).astype(np.float32)
    qd = np.exp((np.arange(64)[None, :] + 1.0) * lg[:, None])
    c["c_qdec"] = np.repeat(qd.reshape(1, 512), 128, 0).astype(np.float32)
    kd = np.exp((63.0 - (idx[:, None] % 64)) * lg[None, :]) * (256.0 ** -0.5)
    c["c_kdec"] = kd.astype(np.float32)
    half = 128
    c["c_invf"] = (np.float32(10000.0) ** (-np.arange(half, dtype=np.float32) / np.float32(half))).astype(np.float32).reshape(128, 1)
    return c


def _prep_inputs(inputs, NT, ncores, NPRE=0):
    x = np.asarray(inputs["x"], np.float32)
    pos = np.asarray(inputs["positions"], np.int32)
    com = {}
    com["w_in"] = np.ascontiguousarray(np.asarray(inputs["w_in"], np.float32)[0])
    com["w_brr"] = np.ascontiguousarray(np.asarray(inputs["w_br_ret"], np.float32)[0])
    com["w_brs"] = np.ascontiguousarray(np.asarray(inputs["w_br_ssd"], np.float32)[0])
    com["w_out"] = np.ascontiguousarray(np.asarray(inputs["w_out"], np.float32)[0])
    com["n1col"] = np.ascontiguousarray(np.asarray(inputs["norm1_w"], np.float32)[0].reshape(16, 128).T)
    com["sncol"] = np.ascontiguousarray(np.asarray(inputs["ssd_norm_w"], np.float32)[0].reshape(32, 128).T)
    cw = np.asarray(inputs["conv_w"], np.float32)[0]
    com["convw"] = np.ascontiguousarray(cw.reshape(4, 48, 128).transpose(2, 1, 0).reshape(128, 192))
    com["convb"] = np.ascontiguousarray(np.asarray(inputs["conv_b"], np.float32)[0].reshape(48, 128).T)
    dsk = np.repeat(np.asarray(inputs["d_skip"], np.float32)[0], 64)
    com["dch"] = np.ascontiguousarray(dsk.reshape(32, 128).T)
    com["dtb"] = np.asarray(inputs["dt_bias"], np.float32).reshape(1, 64)
    com["alog"] = np.asarray(inputs["a_log"], np.float32).reshape(1, 64)
    com["normf"] = np.asarray(inputs["norm_f_w"], np.float32).reshape(1, 2048)
    com.update(_consts())
    maps = []
    nmain = (NT - NPRE) * T
    npre = NPRE * T
    for c in range(ncores):
        m = dict(com)
        if NPRE == 0:
            b = c % x.shape[0]
            m["x"] = np.ascontiguousarray(x[b, :NT * T])
            m["pos"] = np.ascontiguousarray(pos[b, :NT * T].reshape(1, -1))
            m["flag"] = np.ones((128, 1), np.float32)
        else:
            nhalf = x.shape[1] // nmain
            b, hf = divmod(c, nhalf)
            own = x[b, hf * nmain:(hf + 1) * nmain]
            pown = pos[b, hf * nmain:(hf + 1) * nmain]
            if hf == 0:
                prev = np.zeros((npre, x.shape[2]), np.float32)
                pprev = np.zeros((npre,), np.int32)
            else:
                prev = x[b, hf * nmain - npre: hf * nmain]
                pprev = pos[b, hf * nmain - npre: hf * nmain]
            m["x"] = np.ascontiguousarray(np.concatenate([prev, own], 0))
            m["pos"] = np.ascontiguousarray(np.concatenate([pprev, pown], 0).reshape(1, -1))
            m["flag"] = np.full((128, 1), float(hf != 0), np.float32)
        maps.append(m)
    return maps


_NC_CACHE = {}


def kernel(**inputs):
    NT, NPRE = 16, 8
    key = (NT, NPRE)
    if key not in _NC_CACHE:
        _NC_CACHE[key] = build_nc(NT, NPRE=NPRE)
    nc = _NC_CACHE[key]
    maps = _prep_inputs(inputs, NT, 8, NPRE=NPRE)
    res = run_bass_kernel_spmd(nc, maps, core_ids=list(range(8)))
    x = inputs["x"]
    B, SEQ = x.shape[0], x.shape[1]
    nmain = (NT - NPRE) * T
    nhalf = SEQ // nmain
    outp = np.empty((B, SEQ, D), np.float32)
    for c in range(8):
        b, hf = divmod(c, nhalf)
        outp[b, hf * nmain:(hf + 1) * nmain] = res.results[c]["out"]
    return outp
```

```python
import math
import numpy as np
import ml_dtypes
from contextlib import ExitStack
import concourse.bass as bass
import concourse.mybir as mybir
from concourse.bass_utils import run_bass_kernel_spmd

F32 = mybir.dt.float32
BF16 = mybir.dt.bfloat16
I32 = mybir.dt.int32
AF = mybir.ActivationFunctionType
ALU = mybir.AluOpType

SAME_ENGINE_SYNC = True
RAW_ONLY_SAME_ENGINE = True
D = 2048
T = 512
EPS = 1e-6
GR = 528
NGRAN = 39


class _Stop(Exception):
    pass


class _Op:
    __slots__ = ("eng", "fn", "deps", "inc", "idx", "dsem", "dval", "raw")

    def __init__(self, eng, fn, inc, dsem):
        self.eng = eng
        self.fn = fn
        self.deps = []
        self.inc = inc
        self.idx = -1
        self.dsem = dsem
        self.dval = 0


class Sched:
    ENGS = ("pe", "act", "dve", "pool", "sp")

    def __init__(self, nc):
        self.nc = nc
        self.ops = {e: [] for e in self.ENGS}
        self.state = {}
        self.dcount = {}
        self.children = {}
        self.skip = False

    def _conf(self, key):
        fam = self.children.get(key[0], ())
        out = []
        for k in fam:
            n = min(len(k), len(key))
            if k[:n] == key[:n]:
                out.append(k)
        return out

    def _norm(self, keys):
        out = []
        for key in keys:
            if isinstance(key, list):
                out.extend(self._norm(key))
            elif isinstance(key, tuple):
                out.append(key)
            else:
                out.append((key,))
        return out

    def op(self, eng, fn, reads=(), writes=(), inc=True, dsem=None):
        if self.skip:
            return None
        reads = self._norm(reads)
        writes = self._norm(writes)
        o = _Op(eng, fn, inc, dsem)
        deps = []
        raw = set()
        for key in reads:
            for k in self._conf(key):
                w = self.state[k][0]
                if w is not None:
                    deps.append(w)
                    raw.add(id(w))
        for key in writes:
            for k in self._conf(key):
                w, rs = self.state[k]
                if w is not None:
                    deps.append(w)
                deps.extend(rs)
        for key in reads:
            if key not in self.state:
                self.state[key] = [None, []]
                self.children.setdefault(key[0], set()).add(key)
            self.state[key][1].append(o)
        for key in writes:
            if key not in self.state:
                self.state[key] = [None, []]
                self.children.setdefault(key[0], set()).add(key)
            for k in self._conf(key):
                if k != key and len(k) > len(key):
                    self.state[k] = [None, []]
            self.state[key] = [o, []]
        if dsem is not None:
            self.dcount[dsem] = self.dcount.get(dsem, 0) + 1
            o.dval = 16 * self.dcount[dsem]
            o.inc = True
        o.idx = len(self.ops[eng])
        seen = set()
        for d in deps:
            if id(d) in seen or d is o:
                continue
            seen.add(id(d))
            if d.dsem is None and not d.inc:
                lst = self.ops[d.eng]
                covered = False
                for j in range(d.idx + 1, len(lst)):
                    if lst[j].inc and lst[j].dsem is None:
                        covered = True
                        break
                if not covered:
                    d.inc = True
            o.deps.append(d)
        o.raw = raw
        self.ops[eng].append(o)
        return o

    def emit(self, stack):
        nc = self.nc
        sems = {}
        for e in self.ENGS:
            sems[e] = stack.enter_context(nc.semaphore("s_" + e))
        for name in self.dcount:
            sems["d:" + name] = stack.enter_context(nc.semaphore("d_" + name))
        for e in self.ENGS:
            for o in reversed(self.ops[e]):
                if o.dsem is None:
                    o.inc = True
                    break
        val = {}
        for e in self.ENGS:
            lst = self.ops[e]
            cnt = 0
            for o in lst:
                if o.dsem is not None:
                    val[id(o)] = ("d:" + o.dsem, o.dval)
                elif o.inc:
                    cnt += 1
                    val[id(o)] = (e, cnt)
                else:
                    val[id(o)] = (e, cnt + 1)
        block = stack.enter_context(nc.Block())

        def body_for(e):
            def body(engobj):
                waited = {}
                for o in self.ops[e]:
                    need = {}
                    for d in o.deps:
                        sname, v = val[id(d)]
                        if d.dsem is None and d.eng == e:
                            if e == "pe" or not SAME_ENGINE_SYNC:
                                continue
                            if RAW_ONLY_SAME_ENGINE and id(d) not in o.raw:
                                continue
                        if waited.get(sname, 0) >= v:
                            continue
                        if need.get(sname, 0) < v:
                            need[sname] = v
                    for sname, v in need.items():
                        engobj.wait_ge(sems[sname], v)
                        waited[sname] = v
                    ins = o.fn(engobj)
                    if o.dsem is not None:
                        ins.then_inc(sems["d:" + o.dsem], 16)
                    elif o.inc:
                        ins.then_inc(sems[e], 1)
                if e == "sp":
                    for name, c in self.dcount.items():
                        engobj.wait_ge(sems["d:" + name], 16 * c)
            return body

        block.tensor(body_for("pe"))
        block.scalar(body_for("act"))
        block.vector(body_for("dve"))
        block.gpsimd(body_for("pool"))
        block.sync(body_for("sp"))


def weight_groups():
    G = []
    for h in range(8):
        G.append((f"qk{h}", "w_in", 0, [(h * 256, 256), (2048 + h * 256, 256)], "n1"))
        G.append((f"vg{h}", "w_in", 0, [(4096 + h * 256, 256), (6144 + h * 256, 256)], "n1"))
    G.append(("dt", "w_in", 0, [(18432, 64)], "n1"))
    for g in range(8):
        G.append((f"x{g}", "w_in", 0, [(12288 + g * 512, 512)], "n1"))
        G.append((f"bc{g}", "w_in", 0, [(16384 + g * 128, 128), (17408 + g * 128, 128)], "n1"))
        G.append((f"z{g}", "w_in", 0, [(8192 + g * 512, 512)], "n1"))
    for oc in range(4):
        G.append((f"gr{oc}", "w_in", 0, [(18496 + oc * 512, 512)], "n1"))
        G.append((f"gs{oc}", "w_in", 0, [(20544 + oc * 512, 512)], "n1"))
    for oc in range(4):
        G.append((f"br{oc}", "w_brr", 0, [(oc * 512, 512)], None))
        G.append((f"bs0{oc}", "w_brs", 0, [(oc * 512, 512)], "sn"))
        G.append((f"bs1{oc}", "w_brs", 16, [(oc * 512, 512)], "sn"))
    for oc in range(4):
        G.append((f"o{oc}", "w_out", 0, [(oc * 512, 512)], None))
    return G


def build_nc(NT, debug=None, NPRE=0):
    nc = bass.Bass("TRN2", target_bir_lowering=False)
    NTOK = NT * T
    NMAIN = NT - NPRE

    def din(name, shape, dt=F32):
        return nc.dram_tensor(name, shape, dt, kind="ExternalInput").ap()

    x = din("x", [NTOK, D])
    pos = din("pos", [1, NTOK], I32)
    wsrc = {"w_in": din("w_in", [D, 22592]), "w_brr": din("w_brr", [2048, 2048]),
            "w_brs": din("w_brs", [4096, 2048]), "w_out": din("w_out", [2048, 2048])}
    n1col_d = din("n1col", [128, 16])
    sncol_d = din("sncol", [128, 32])
    convw_d = din("convw", [128, 192])
    convb_d = din("convb", [128, 48])
    dch_d = din("dch", [128, 32])
    dtb_d = din("dtb", [1, 64])
    alog_d = din("alog", [1, 64])
    normf_d = din("normf", [1, D])
    cid_d = din("c_ident", [128, 128], BF16)
    cidf_d = din("c_identf", [128, 128])
    conesf_d = din("c_onesf", [128, 128])
    cones_d = din("c_ones", [128, 128], BF16)
    crm_d = din("c_retmask", [128, 8 * 128], BF16)
    cssd_d = din("c_ssd", [128, 6 * 128])
    cqd_d = din("c_qdec", [128, 8 * 64])
    ckd_d = din("c_kdec", [128, 8])
    cinvf_d = din("c_invf", [128, 1])
    out = nc.dram_tensor("out", [NMAIN * T, D], F32, kind="ExternalOutput").ap()
    flag_d = din("flag", [128, 1])
    WG = weight_groups()
    NG = len(WG)
    ws = nc.dram_tensor("ws", [NG, 128, 16 * 512], BF16, kind="Internal").ap()
    gidx = {g[0]: i for i, g in enumerate(WG)}

    st = ExitStack()
    with st:
        S = Sched(nc)

        def sb(name, shape, dt):
            return st.enter_context(nc.sbuf_tensor("s_" + name, shape, dt))

        xbuf = [sb(f"xbuf{i}", [128, D], F32) for i in range(2)]
        hT = sb("hT", [128, 16, T], BF16)
        wbuf = [sb(f"wbuf{i}", [128, 16, 512], BF16) for i in range(2)]
        yR = sb("yR", [128, 16, T], BF16)
        ysT = sb("ysT", [128, 32, T], BF16)
        Sret = sb("Sret", [128, 8, 512], F32)
        Sssd = sb("Sssd", [128, 8, 512], F32)
        arena = sb("arena", [128, NGRAN * GR], BF16)
        ident = sb("ident", [128, 128], BF16)
        identf = sb("identf", [128, 128], F32)
        onesf = sb("onesf", [128, 128], F32)
        onesS = sb("onesS", [128, 128], BF16)
        retmask = sb("retmask", [128, 8, 128], BF16)
        ssdm = sb("ssdm", [128, 6, 128], F32)
        qdec = sb("qdec", [128, 8, 64], F32)
        kdec = sb("kdec", [128, 8], F32)
        invf = sb("invf", [128, 1], F32)
        normf = sb("normf", [128, D], F32)
        n1col = sb("n1col", [128, 16], F32)
        sncol = sb("sncol", [128, 32], F32)
        convw = sb("convw", [128, 48, 4], F32)
        convb = sb("convb", [128, 48], F32)
        dch = sb("dch", [128, 32], F32)
        dtb = sb("dtb", [128, 64], F32)
        arow = sb("arow", [128, 64], F32)
        halo = sb("halo", [128, 48, 4], BF16)
        ssq = sb("ssq", [128, 4, 8], F32)
        sml = sb("sml", [128, 16], F32)
        flg = sb("flg", [128, 1], F32)
        hb = ysT[:, 0:4, :].rearrange("p a b -> p (a b)")
        HBK = [("ysT", i) for i in range(4)]

        pbank = [st.enter_context(nc.psum_tensor(f"pb{i}", [128, 512], F32)) for i in range(8)]
        rrA = [0]
        rrB = [0]

        AL = [[0, 1, 2]]

        def bankA():
            lst = AL[0]
            i = lst[rrA[0] % len(lst)]
            rrA[0] += 1
            return i

        BL = [[3, 4, 5, 6, 7]]

        def bankB():
            lst = BL[0]
            i = lst[rrB[0] % len(lst)]
            rrB[0] += 1
            return i

        def PB(i):
            return pbank[i][:]

        def PBb(i):
            return pbank[i][:].bitcast(BF16)

        def pk(i):
            return ("pb", i)

        def dump(name, ap, keys, shape, dt):
            if debug != "dump":
                return
            d = nc.dram_tensor(name, shape, dt, kind="ExternalOutput").ap()
            S.op("sp", lambda e: e.dma_start(out=d, in_=ap), reads=keys, dsem="dbg_" + name)

        class AB:
            def __init__(self, g0, ng, dt, n):
                base = arena[:, g0 * GR: (g0 + ng) * GR]
                if dt == F32:
                    self.ap = base.bitcast(F32)[:, 0:n]
                elif dt == I32:
                    self.ap = base.bitcast(I32)[:, 0:n]
                else:
                    self.ap = base[:, 0:n]
                self.keys = [("ar", g) for g in range(g0, g0 + ng)]

        def ld(dst_ap, src_ap, key, name):
            S.op("sp", lambda e: e.dma_start(out=dst_ap, in_=src_ap), writes=[key], dsem=name)

        ld(ident[:], cid_d[:, :], "ident", "c0")
        ld(identf[:], cidf_d[:, :], "identf", "c1")
        ld(onesf[:], conesf_d[:, :], "onesf", "c2")
        ld(onesS[:], cones_d[:, :], "onesS", "c3")
        ld(retmask[:].rearrange("p a b -> p (a b)"), crm_d[:, :], "retmask", "c4")
        ld(ssdm[:].rearrange("p a b -> p (a b)"), cssd_d[:, :], "ssdm", "c5")
        ld(qdec[:].rearrange("p a b -> p (a b)"), cqd_d[:, :], "qdec", "c6")
        ld(kdec[:], ckd_d[:, :], "kdec", "c7")
        ld(invf[:], cinvf_d[:, :], "invf", "c8")
        ld(flg[:], flag_d[:, :], "flg", "c17")
        ld(normf[:], normf_d.partition_broadcast(128), "normf", "c9")
        ld(n1col[:], n1col_d[:, :], "n1col", "c10")
        ld(sncol[:], sncol_d[:, :], "sncol", "c11")
        ld(convw[:].rearrange("p a b -> p (a b)"), convw_d[:, :], "convw", "c12")
        ld(convb[:], convb_d[:, :], "convb", "c13")
        ld(dch[:], dch_d[:, :], "dch", "c14")
        ld(dtb[:], dtb_d.partition_broadcast(128), "dtb", "c15")
        ld(arow[:], alog_d.partition_broadcast(128), "arow", "c16")
        S.op("act", lambda e: e.activation(out=arow[:], in_=arow[:], func=AF.Exp), reads=["arow"], writes=["arow"])
        S.op("dve", lambda e: e.tensor_scalar_mul(arow[:], arow[:], -1.0), reads=["arow"], writes=["arow"])
        S.op("pool", lambda e: e.memset(Sret[:], 0.0), writes=["Sret"])
        S.op("pool", lambda e: e.memset(Sssd[:], 0.0), writes=["Sssd"])
        S.op("pool", lambda e: e.memset(halo[:], 0.0), writes=["halo"])

        TRI, UBD, UU, ONC0, ONC1, MBD = [ssdm[:, i, :] for i in range(6)]

        fslots = [(xbuf[0][:], [("xbuf", 0)]), (xbuf[1][:], [("xbuf", 1)])]
        for i_ in range(4):
            fslots.append((ysT[:, 8 * i_:8 * i_ + 8, :].rearrange("p a b -> p (a b)").bitcast(F32), [("ysT", c_) for c_ in range(8 * i_, 8 * i_ + 8)]))
        for i_ in range(2):
            fslots.append((yR[:, 8 * i_:8 * i_ + 8, :].rearrange("p a b -> p (a b)").bitcast(F32), [("yR", c_) for c_ in range(8 * i_, 8 * i_ + 8)]))
        NFS = len(fslots)
        it = 0
        fi = 0
        for gi, (gname, src, kc0, segs, scale) in enumerate(WG):
            for kq in range(4):
                bi = it % 8
                it += 1
                k0 = kc0 + kq * 4
                off = 0
                bst = wbuf[bi // 4][:, 4 * (bi % 4):4 * (bi % 4) + 4, :]
                bkey = ("wbuf", bi // 4, bi % 4)
                for si, (c0, n) in enumerate(segs):
                    fap, fkeys = fslots[fi % NFS]
                    fsem = f"cv{fi % NFS}"
                    fi += 1
                    fst = fap[:, 0:n * 4].rearrange("p (k c) -> p k c", k=4)
                    srcap = wsrc[src][k0 * 128:(k0 + 4) * 128, c0:c0 + n].rearrange("(k p) c -> p k c", p=128)
                    S.op("sp", lambda e, fst=fst, srcap=srcap: e.dma_start(out=fst, in_=srcap),
                         writes=fkeys, dsem=fsem)
                    eng = ("dve", "dve", "pool")[fi % 3]
                    dst = bst[:, :, off:off + n]
                    if scale is None:
                        S.op(eng, lambda e, dst=dst, fst=fst: e.tensor_copy(dst, fst),
                             reads=fkeys, writes=[bkey + (si,)])
                    else:
                        col = n1col if scale == "n1" else sncol
                        cb = col[:, k0:k0 + 4].unsqueeze(2).to_broadcast([128, 4, n])
                        S.op(eng, lambda e, dst=dst, fst=fst, cb=cb: e.tensor_tensor(out=dst, in0=fst, in1=cb, op=ALU.mult),
                             reads=fkeys + ["n1col", "sncol"], writes=[bkey + (si,)])
                    off += n
                dstd = ws[gi, :, kq * 2048:(kq + 1) * 2048].rearrange("p (k c) -> p k c", k=4)[:, :, 0:off]
                srcs = bst[:, :, 0:off]
                S.op("act", lambda e, dstd=dstd, srcs=srcs: e.dma_start(out=dstd, in_=srcs),
                     reads=[bkey], writes=[("ws", gi)], dsem=f"cs{bi}")

        wq = {"n": 0, "loaded": -1}
        order = []
        for ti_ in range(NT):
            for (gname_, _, _, _, _) in WG:
                if ti_ < NPRE and not (gname_.startswith("qk") or gname_.startswith("vg") or gname_ == "dt" or gname_.startswith("x") or gname_.startswith("bc")):
                    continue
                order.append((gidx[gname_], ti_ < NPRE))
        total_loads = len(order)

        def issue_load(n):
            gi, ispre = order[n]
            slot = n % 2
            gname = WG[gi][0]
            ncol = sum(nn for (_, nn) in WG[gi][3])
            c_lo, c_hi = 0, ncol
            if ispre and gname.startswith("qk"):
                c_lo, c_hi = 256, 512
            if ispre and gname.startswith("vg"):
                c_lo, c_hi = 0, 256
            src = ws[gi, :, :].rearrange("p (k c) -> p k c", c=512)[:, :, c_lo:c_hi]
            S.op("sp", lambda e, slot=slot, src=src, c_lo=c_lo, c_hi=c_hi: e.dma_start(out=wbuf[slot][:, :, c_lo:c_hi], in_=src),
                 reads=[("ws", gi)], writes=[("wbuf", slot)], dsem=f"w{slot}")

        def next_w(expect, prefetch=True):
            n = wq["n"]
            assert WG[order[n][0]][0] == expect, (WG[order[n][0]][0], expect)
            while wq["loaded"] < min(n + (1 if prefetch else 0), total_loads - 1):
                wq["loaded"] += 1
                issue_load(wq["loaded"])
            wq["n"] += 1
            return n % 2

        def prefetch_next():
            n = wq["n"]
            while wq["loaded"] < min(n, total_loads - 1):
                wq["loaded"] += 1
                issue_load(wq["loaded"])

        def proj_ws(slot, cbase, m, rhs_fn, rkeys, bank, nk=16, first=True, last=True, kofs=0):
            for kc in range(nk):
                S.op("pe", lambda e, kc=kc, slot=slot: e.matmul(PB(bank), lhsT=wbuf[slot][:, kc, cbase + m * 128: cbase + (m + 1) * 128],
                                                      rhs=rhs_fn(kc + kofs), start=(first and kc == 0), stop=(last and kc == nk - 1)),
                     reads=[("wbuf", slot)] + rkeys, writes=[pk(bank)], inc=(last and kc == nk - 1))

        def proj_as(slot, c0, n, j, lhs_fn, lkeys, bank):
            for kc in range(16):
                S.op("pe", lambda e, kc=kc, slot=slot: e.matmul(PB(bank)[:, 0:n], lhsT=lhs_fn(kc)[:, j * 128:(j + 1) * 128],
                                                      rhs=wbuf[slot][:, kc, c0:c0 + n], start=(kc == 0), stop=(kc == 15)),
                     reads=[("wbuf", slot)] + lkeys, writes=[pk(bank)], inc=(kc == 15))

        hT_fn = lambda kc: hT[:, kc, :]
        PI = math.pi
        MAGIC = 12582912.0
        C1 = 6.28125
        C2 = 2.0 * math.pi - 6.28125
        log_gamma = [math.log1p(-(2.0 ** (-5.0 - h))) for h in range(8)]
        cdec = [math.exp(64.0 * lg) for lg in log_gamma]

        def chk(k):
            if debug == k:
                raise _Stop()

        try:
          chk(1)
          for ti in range(NT):
            tok0 = ti * T
            PRE = ti < NPRE
            if NPRE > 0 and ti == NPRE:
                S.op("dve", lambda e: e.tensor_scalar_mul(Sret[:].rearrange("p a b -> p (a b)"), Sret[:].rearrange("p a b -> p (a b)"), flg[:, 0:1]), reads=[("Sret",), "flg"], writes=[("Sret",)])
                S.op("dve", lambda e: e.tensor_scalar_mul(Sssd[:].rearrange("p a b -> p (a b)"), Sssd[:].rearrange("p a b -> p (a b)"), flg[:, 0:1]), reads=[("Sssd",), "flg"], writes=[("Sssd",)])
                S.op("dve", lambda e: e.tensor_scalar_mul(halo[:].rearrange("p a b -> p (a b)"), halo[:].rearrange("p a b -> p (a b)"), flg[:, 0:1]), reads=[("halo",), "flg"], writes=[("halo",)])
            for j in range(4):
                xb = xbuf[j % 2]
                xk = ("xbuf", j % 2)
                r0 = tok0 + j * 128
                S.op("sp", lambda e, xb=xb, r0=r0: e.dma_start(out=xb[:], in_=x[r0:r0 + 128, :]), writes=[xk], dsem=f"x{j % 2}")
                S.op("act", lambda e, xb=xb: e.activation(out=hb, in_=xb[:], func=AF.Square, accum_out=sml[:, 0:1]),
                     reads=[xk], writes=HBK + [("sml", 0)])
                S.op("dve", lambda e: e.tensor_scalar(out=sml[:, 1:2], in0=sml[:, 0:1], scalar1=1.0 / D, scalar2=EPS, op0=ALU.mult, op1=ALU.add),
                     reads=[("sml", 0)], writes=[("sml", 1)])
                S.op("act", lambda e: e.activation(out=sml[:, 2:3], in_=sml[:, 1:2], func=AF.Sqrt), reads=[("sml", 1)], writes=[("sml", 2)])
                S.op("dve", lambda e: e.reciprocal(sml[:, 3:4], sml[:, 2:3]), reads=[("sml", 2)], writes=[("sml", 3)])
                S.op("act", lambda e, xb=xb: e.activation(out=hb, in_=xb[:], func=AF.Identity, scale=sml[:, 3:4]),
                     reads=[xk, ("sml", 3)], writes=HBK)
                for half in range(2):
                    bk = bankB()
                    for q in range(8):
                        kc = half * 8 + q
                        S.op("pe", lambda e, bk=bk, q=q, kc=kc: e.transpose(PBb(bk)[:, q * 128:(q + 1) * 128], hb[:, kc * 128:(kc + 1) * 128], ident[:]),
                             reads=HBK + ["ident"], writes=[pk(bk)], inc=(q == 7))
                    dst = hT[:, half * 8:(half + 1) * 8, j * 128:(j + 1) * 128]
                    srcp = PBb(bk).rearrange("p (a b) -> p a b", b=128)
                    if half == 0:
                        S.op("dve", lambda e, dst=dst, srcp=srcp: e.tensor_copy(dst, srcp), reads=[pk(bk)], writes=[("hT", j, half)])
                    else:
                        S.op("act", lambda e, dst=dst, srcp=srcp: e.activation(out=dst, in_=srcp, func=AF.Copy), reads=[pk(bk)], writes=[("hT", j, half)])

            chk(2)
            if ti == 0:
                dump("d_hT", hT[:].rearrange("p a b -> p (a b)"), [("hT",)], [128, 16 * 512], BF16)
            cosb = AB(27, 2, F32, 512)
            sinb = AB(29, 2, F32, 512)
            tA = AB(0, 2, F32, 512)
            tB = AB(2, 2, F32, 512)
            tC = AB(4, 2, F32, 512)
            tD = AB(6, 2, F32, 512)
            posi = AB(0, 2, I32, 512)
            S.op("sp", lambda e, tok0=tok0: e.dma_start(out=posi.ap, in_=pos[0:1, tok0:tok0 + T].partition_broadcast(128)), writes=posi.keys, dsem="pos")
            S.op("dve", lambda e: e.tensor_copy(tB.ap, posi.ap), reads=posi.keys, writes=tB.keys)
            S.op("dve", lambda e: e.tensor_scalar_mul(tC.ap, tB.ap, invf[:, 0:1]), reads=tB.keys + [("invf",)], writes=tC.keys)
            S.op("dve", lambda e: e.tensor_scalar(out=tB.ap, in0=tC.ap, scalar1=1.0 / (2.0 * PI), scalar2=MAGIC, op0=ALU.mult, op1=ALU.add),
                 reads=tC.keys, writes=tB.keys)
            S.op("dve", lambda e: e.tensor_scalar_add(tB.ap, tB.ap, -MAGIC), reads=tB.keys, writes=tB.keys)
            S.op("dve", lambda e: e.scalar_tensor_tensor(out=tC.ap, in0=tB.ap, scalar=-C1, in1=tC.ap, op0=ALU.mult, op1=ALU.add),
                 reads=tB.keys + tC.keys, writes=tC.keys)
            S.op("dve", lambda e: e.scalar_tensor_tensor(out=tC.ap, in0=tB.ap, scalar=-C2, in1=tC.ap, op0=ALU.mult, op1=ALU.add),
                 reads=tB.keys + tC.keys, writes=tC.keys)
            S.op("dve", lambda e: e.tensor_scalar(out=tC.ap, in0=tC.ap, scalar1=-PI, scalar2=PI, op0=ALU.max, op1=ALU.min),
                 reads=tC.keys, writes=tC.keys)
            S.op("act", lambda e: e.activation(out=sinb.ap, in_=tC.ap, func=AF.Sin), reads=tC.keys, writes=sinb.keys)
            S.op("act", lambda e: e.activation(out=tD.ap, in_=tC.ap, func=AF.Abs), reads=tC.keys, writes=tD.keys)
            S.op("dve", lambda e: e.tensor_scalar(out=tD.ap, in0=tD.ap, scalar1=-1.0, scalar2=PI / 2, op0=ALU.mult, op1=ALU.add),
                 reads=tD.keys, writes=tD.keys)
            S.op("act", lambda e: e.activation(out=cosb.ap, in_=tD.ap, func=AF.Sin), reads=tD.keys, writes=cosb.keys)

            chk(3)
            qT = [AB(8, 1, BF16, 512), AB(9, 1, BF16, 512)]
            qd = [AB(10, 1, BF16, 512), AB(11, 1, BF16, 512)]
            kT = [AB(12, 1, BF16, 512), AB(13, 1, BF16, 512)]
            ktok = AB(14, 2, BF16, 1024)
            vtok = AB(16, 2, BF16, 1024)
            sg = [AB(18, 1, BF16, 512), AB(19, 1, BF16, 512)]
            Pb = AB(20, 1, BF16, 256)
            Sb = [AB(21, 1, BF16, 512), AB(22, 1, BF16, 512)]
            ysb = [AB(23, 1, BF16, 512), AB(24, 1, BF16, 512)]
            ysq = [AB(25, 1, BF16, 512), AB(26, 1, BF16, 512)]
            sbi = 0
            AL[0] = [0, 1, 2, 5, 6, 7]
            for h in range(8):
                slot = next_w(f"qk{h}")
                for which in ([1] if PRE else [0, 1]):
                    ba = bankA()
                    proj_ws(slot, which * 256, 0, hT_fn, [("hT",)], ba)
                    bb = bankA()
                    proj_ws(slot, which * 256, 1, hT_fn, [("hT",)], bb)
                    dstT = qT if which == 0 else kT
                    S.op("dve", lambda e, ba=ba: e.tensor_tensor(out=tA.ap, in0=PB(ba), in1=cosb.ap, op=ALU.mult), reads=[pk(ba)] + cosb.keys, writes=tA.keys)
                    S.op("dve", lambda e, bb=bb: e.tensor_tensor(out=tB.ap, in0=PB(bb), in1=sinb.ap, op=ALU.mult), reads=[pk(bb)] + sinb.keys, writes=tB.keys)
                    S.op("pool", lambda e, dstT=dstT: e.tensor_tensor(out=dstT[0].ap, in0=tA.ap, in1=tB.ap, op=ALU.subtract), reads=tA.keys + tB.keys, writes=dstT[0].keys)
                    S.op("dve", lambda e, bb=bb: e.tensor_tensor(out=tC.ap, in0=PB(bb), in1=cosb.ap, op=ALU.mult), reads=[pk(bb)] + cosb.keys, writes=tC.keys)
                    S.op("dve", lambda e, ba=ba: e.tensor_tensor(out=tD.ap, in0=PB(ba), in1=sinb.ap, op=ALU.mult), reads=[pk(ba)] + sinb.keys, writes=tD.keys)
                    S.op("pool", lambda e, dstT=dstT: e.tensor_tensor(out=dstT[1].ap, in0=tC.ap, in1=tD.ap, op=ALU.add), reads=tC.keys + tD.keys, writes=dstT[1].keys)
                    if which == 0:
                        for u in range(2):
                            S.op("pool", lambda e, u=u, h=h: e.tensor_tensor(out=qd[u].ap.rearrange("p (c l) -> p c l", l=64),
                                                                               in0=qT[u].ap.rearrange("p (c l) -> p c l", l=64),
                                                                               in1=qdec[:, h:h + 1, :].to_broadcast([128, 8, 64]), op=ALU.mult),
                                 reads=qT[u].keys + ["qdec"], writes=qd[u].keys)
                slot = next_w(f"vg{h}")
                for j in range(4):
                    bk = bankA()
                    proj_as(slot, 0, 256, j, hT_fn, [("hT",)], bk)
                    S.op("act", lambda e, bk=bk, j=j: e.activation(out=vtok.ap[:, j * 256:(j + 1) * 256], in_=PB(bk)[:, 0:256], func=AF.Copy),
                         reads=[pk(bk)], writes=vtok.keys)
                S.skip = PRE
                for u in range(2):
                    bk = bankA()
                    proj_ws(slot, 256, u, hT_fn, [("hT",)], bk)
                    S.op("act", lambda e, bk=bk, u=u: e.activation(out=sg[u].ap, in_=PB(bk), func=AF.Silu), reads=[pk(bk)], writes=sg[u].keys)
                S.skip = False
                for j in range(4):
                    bk = bankB()
                    for u in range(2):
                        S.op("pe", lambda e, bk=bk, u=u, j=j: e.transpose(PBb(bk)[:, u * 128:(u + 1) * 128], kT[u].ap[:, j * 128:(j + 1) * 128], ident[:]),
                             reads=kT[u].keys + ["ident"], writes=[pk(bk)], inc=(u == 1))
                    S.op("act", lambda e, bk=bk, j=j, h=h: e.activation(out=ktok.ap[:, j * 256:(j + 1) * 256], in_=PBb(bk)[:, 0:256], func=AF.Identity, scale=kdec[:, h:h + 1]),
                         reads=[pk(bk), "kdec"], writes=ktok.keys)
                S.skip = PRE
                BL[0] = [5, 6, 7]
                by = [3, 4]
                Sh = Sret[:, h, :]
                skey = ("Sret", h)
                cur = Sb[sbi % 2]
                sbi += 1
                S.op("act", lambda e, cur=cur, Sh=Sh: e.activation(out=cur.ap, in_=Sh, func=AF.Copy), reads=[skey], writes=cur.keys)
                for j in range(4):
                    S.skip = PRE
                    blk = slice(j * 128, (j + 1) * 128)
                    bsc = bankB()
                    for u in range(2):
                        S.op("pe", lambda e, bsc=bsc, u=u, blk=blk: e.matmul(PB(bsc)[:, 0:128], lhsT=kT[u].ap[:, blk], rhs=qT[u].ap[:, blk], start=(u == 0), stop=(u == 1)),
                             reads=kT[u].keys + qT[u].keys, writes=[pk(bsc)], inc=(u == 1))
                    pslot = Pb.ap[:, (j % 2) * 128:(j % 2 + 1) * 128]
                    S.op("dve", lambda e, bsc=bsc, pslot=pslot, h=h: e.tensor_tensor(out=pslot, in0=PB(bsc)[:, 0:128], in1=retmask[:, h, :], op=ALU.mult),
                         reads=[pk(bsc), "retmask"], writes=Pb.keys)
                    for u in range(2):
                        S.op("pe", lambda e, u=u, j=j, pslot=pslot, blk=blk, by=by: e.matmul(PB(by[u])[:, blk], lhsT=vtok.ap[:, j * 256 + u * 128: j * 256 + (u + 1) * 128], rhs=pslot,
                                                                                  start=(j == 0), stop=False),
                             reads=vtok.keys + Pb.keys, writes=[pk(by[u])], inc=False)
                    for c in range(2):
                        S.skip = PRE
                        csl = slice(j * 128 + c * 64, j * 128 + (c + 1) * 64)
                        for u in range(2):
                            for dh in range(2):
                                last = (j == 3 and c == 1 and dh == 1)
                                S.op("pe", lambda e, u=u, dh=dh, cur=cur, csl=csl, last=last, by=by: e.matmul(PB(by[u])[:, csl], lhsT=cur.ap[:, dh * 256 + u * 128: dh * 256 + (u + 1) * 128],
                                                                                                 rhs=qd[dh].ap[:, csl], start=False, stop=last),
                                     reads=cur.keys + qd[dh].keys, writes=[pk(by[u])], inc=last)
                        S.skip = False
                        bsu = bankB()
                        rows = slice(c * 64, (c + 1) * 64)
                        for dh in range(2):
                            S.op("pe", lambda e, bsu=bsu, dh=dh, rows=rows, j=j: e.matmul(PB(bsu)[:, dh * 256:(dh + 1) * 256], lhsT=ktok.ap[rows, j * 256 + dh * 128: j * 256 + (dh + 1) * 128],
                                                                                       rhs=vtok.ap[rows, j * 256:(j + 1) * 256], start=(dh == 0), stop=(dh == 1)),
                                 reads=ktok.keys + vtok.keys, writes=[pk(bsu)], inc=(dh == 1))
                        S.op("dve", lambda e, bsu=bsu, Sh=Sh, h=h: e.scalar_tensor_tensor(out=Sh, in0=Sh, scalar=cdec[h], in1=PB(bsu), op0=ALU.mult, op1=ALU.add),
                             reads=[pk(bsu), skey], writes=[skey])
                        S.skip = PRE
                        if not (j == 3 and c == 1):
                            cur = Sb[sbi % 2]
                            sbi += 1
                            S.op("act", lambda e, cur=cur, Sh=Sh: e.activation(out=cur.ap, in_=Sh, func=AF.Copy), reads=[skey], writes=cur.keys)
                for u in range(2):
                    S.op("act", lambda e, u=u, by=by: e.activation(out=ysb[u].ap, in_=PB(by[u]), func=AF.Copy), reads=[pk(by[u])], writes=ysb[u].keys)
                    S.op("act", lambda e, u=u, by=by: e.activation(out=ysq[u].ap, in_=PB(by[u]), func=AF.Square), reads=[pk(by[u])], writes=ysq[u].keys)
                bm = bankB()
                be = bankB()
                for u in range(2):
                    S.op("pe", lambda e, u=u, bm=bm: e.matmul(PB(bm), lhsT=onesS[:], rhs=ysb[u].ap, start=(u == 0), stop=(u == 1)),
                         reads=["onesS"] + ysb[u].keys, writes=[pk(bm)], inc=(u == 1))
                for u in range(2):
                    S.op("pe", lambda e, u=u, be=be: e.matmul(PB(be), lhsT=onesS[:], rhs=ysq[u].ap, start=(u == 0), stop=(u == 1)),
                         reads=["onesS"] + ysq[u].keys, writes=[pk(be)], inc=(u == 1))
                S.op("act", lambda e, bm=bm: e.activation(out=tA.ap, in_=PB(bm), func=AF.Copy), reads=[pk(bm)], writes=tA.keys)
                S.op("act", lambda e, bm=bm: e.activation(out=tB.ap, in_=PB(bm), func=AF.Square), reads=[pk(bm)], writes=tB.keys)
                S.op("dve", lambda e, be=be: e.tensor_tensor(out=tB.ap, in0=PB(be), in1=tB.ap, op=ALU.subtract), reads=[pk(be)] + tB.keys, writes=tB.keys)
                S.op("dve", lambda e: e.tensor_scalar(out=tB.ap, in0=tB.ap, scalar1=0.0, scalar2=EPS, op0=ALU.max, op1=ALU.add), reads=tB.keys, writes=tB.keys)
                S.op("act", lambda e: e.activation(out=tB.ap, in_=tB.ap, func=AF.Sqrt), reads=tB.keys, writes=tB.keys)
                S.op("dve", lambda e: e.reciprocal(tC.ap, tB.ap), reads=tB.keys, writes=tC.keys)
                for u in range(2):
                    S.op("dve", lambda e, u=u, by=by: e.tensor_tensor(out=tD.ap, in0=PB(by[u]), in1=tA.ap, op=ALU.subtract), reads=[pk(by[u])] + tA.keys, writes=tD.keys)
                    S.op("dve", lambda e: e.tensor_tensor(out=tD.ap, in0=tD.ap, in1=tC.ap, op=ALU.mult), reads=tD.keys + tC.keys, writes=tD.keys)
                    S.op("pool", lambda e, u=u, h=h: e.tensor_tensor(out=yR[:, 2 * h + u, :], in0=tD.ap, in1=sg[u].ap, op=ALU.mult),
                         reads=tD.keys + sg[u].keys, writes=[("yR", 2 * h + u)])
                S.skip = False

            BL[0] = [3, 4, 5, 6, 7]
            AL[0] = [0, 1, 2]
            chk(4)
            if ti == 0:
                dump("d_yR", yR[:].rearrange("p a b -> p (a b)"), [("yR",)], [128, 16 * 512], BF16)
                dump("d_Sret", Sret[:].rearrange("p a b -> p (a b)"), [("Sret",)], [128, 8 * 512], F32)
            if ti == 0:
                dtt = sb("dtt", [128, 3, 64], F32)
            dtxB = AB(0, 1, F32, 256)
            dtaB = AB(1, 1, F32, 256)
            dexB = AB(2, 4, F32, 1024)
            dtx = dtxB.ap.rearrange("p (j h) -> p j h", h=64)
            dta = dtaB.ap.rearrange("p (j h) -> p j h", h=64)
            dex = dexB.ap.rearrange("p (j h) -> p j h", h=256)
            slot = next_w("dt")
            for j in range(4):
                bk = bankA()
                proj_as(slot, 0, 64, j, hT_fn, [("hT",)], bk)
                S.op("dve", lambda e, bk=bk: e.tensor_tensor(out=dtt[:, 0, :], in0=PB(bk)[:, 0:64], in1=dtb[:], op=ALU.add),
                     reads=[pk(bk), "dtb"], writes=[("dtt", 0)])
                S.op("act", lambda e: e.activation(out=dtt[:, 1, :], in_=dtt[:, 0, :], func=AF.Abs),
                     reads=[("dtt", 0)], writes=[("dtt", 1)])
                S.op("act", lambda e: e.activation(out=dtt[:, 1, :], in_=dtt[:, 1, :], func=AF.Exp, scale=-1.0), reads=[("dtt", 1)], writes=[("dtt", 1)])
                S.op("act", lambda e: e.activation(out=dtt[:, 2, :], in_=dtt[:, 1, :], func=AF.Ln, bias=1.0), reads=[("dtt", 1)], writes=[("dtt", 2)])
                S.op("dve", lambda e, j=j: e.scalar_tensor_tensor(out=dtx[:, j, :], in0=dtt[:, 0, :], scalar=0.0, in1=dtt[:, 2, :], op0=ALU.max, op1=ALU.add),
                     reads=[("dtt", 0), ("dtt", 2)], writes=[dtxB.keys[0]])
                S.op("dve", lambda e, j=j: e.tensor_tensor(out=dta[:, j, :], in0=dtx[:, j, :], in1=arow[:], op=ALU.mult),
                     reads=[dtxB.keys[0], "arow"], writes=[dtaB.keys[0]])
                bk2 = bankB()
                for qi, msk in enumerate([TRI, UBD, ONC0, ONC1]):
                    S.op("pe", lambda e, bk2=bk2, qi=qi, msk=msk, j=j: e.matmul(PB(bk2)[:, qi * 64:(qi + 1) * 64], lhsT=msk, rhs=dta[:, j, :], start=(qi == 0), stop=(qi == 3)),
                         reads=["ssdm", dtaB.keys[0]], writes=[pk(bk2)], inc=(qi == 3))
                S.op("act", lambda e, bk2=bk2, j=j: e.activation(out=dex[:, j, :], in_=PB(bk2)[:, 0:256], func=AF.Exp), reads=[pk(bk2)], writes=dexB.keys)

            chk(5)
            BCpre = [AB(6, 1, BF16, 516), AB(7, 1, BF16, 516)]
            BT = AB(8, 1, BF16, 512)
            CT = AB(9, 1, BF16, 512)
            Btok = AB(10, 1, BF16, 512)
            CT0 = AB(11, 1, BF16, 512)
            CT1 = AB(12, 1, BF16, 512)
            xpre = [AB(13 + m, 1, BF16, 516) for m in range(4)]
            xsT = [AB(17 + m, 1, BF16, 512) for m in range(4)]
            zs = AB(21, 1, BF16, 512)
            diagc = [AB(22, 1, BF16, 512), AB(23, 1, BF16, 512)]
            diagD = AB(24, 1, BF16, 512)
            Rb = AB(25, 4, F32, 1024)
            dec = AB(29, 2, BF16, 1024)
            cbTm = AB(31, 1, F32, 128)
            xdt = AB(32, 1, BF16, 512)
            xdtt = AB(33, 1, BF16, 512)
            tmpf = AB(34, 2, F32, 512)
            yg = AB(36, 1, BF16, 512)
            stb = [AB(37, 1, BF16, 512), AB(38, 1, BF16, 512)]
            junk = zs
            dci = 0
            sti = 0
            for g in range(8):
                S.skip = False
                Sg = Sssd[:, g, :]
                sgk = ("Sssd", g)
                slot = next_w(f"x{g}")
                pres = []
                for m in range(4):
                    bk = bankA()
                    proj_ws(slot, 0, m, hT_fn, [("hT",)], bk)
                    S.op("act", lambda e, bk=bk, m=m: e.activation(out=xpre[m].ap[:, 3:515], in_=PB(bk), func=AF.Copy), reads=[pk(bk)], writes=xpre[m].keys)
                    pres.append((xpre[m], g * 4 + m, xsT[m]))
                slot = next_w(f"bc{g}")
                for m in range(2):
                    bk = bankA()
                    proj_ws(slot, 0, m, hT_fn, [("hT",)], bk)
                    S.op("act", lambda e, bk=bk, m=m: e.activation(out=BCpre[m].ap[:, 3:515], in_=PB(bk), func=AF.Copy), reads=[pk(bk)], writes=BCpre[m].keys)
                    pres.append((BCpre[m], 32 + m * 8 + g, BT if m == 0 else CT))
                for (pre, c48, post) in pres:
                    S.op("dve", lambda e, pre=pre, c48=c48: e.tensor_copy(pre.ap[:, 0:3], halo[:, c48, 0:3]), reads=[("halo", c48)], writes=pre.keys)
                    S.op("pool", lambda e, pre=pre, c48=c48: e.tensor_copy(halo[:, c48, 0:3], pre.ap[:, 512:515]), reads=pre.keys, writes=[("halo", c48)])
                    S.skip = PRE and (post is CT)
                    dg = diagc[dci % 2]
                    dci += 1
                    for k in range(4):
                        S.op("pool", lambda e, dg=dg, k=k, c48=c48: e.tensor_scalar(out=dg.ap[:, k * 128:(k + 1) * 128], in0=ident[:], scalar1=convw[:, c48, k:k + 1], scalar2=1.0, op0=ALU.mult, op1=ALU.mult),
                             reads=["ident", "convw"], writes=dg.keys)
                    bk = bankB()
                    for k in range(4):
                        S.op("pe", lambda e, bk=bk, dg=dg, k=k, pre=pre: e.matmul(PB(bk), lhsT=dg.ap[:, k * 128:(k + 1) * 128], rhs=pre.ap[:, k:k + 512], start=(k == 0), stop=(k == 3)),
                             reads=dg.keys + pre.keys, writes=[pk(bk)], inc=(k == 3))
                    S.op("act", lambda e, bk=bk, post=post, c48=c48: e.activation(out=post.ap, in_=PB(bk), func=AF.Silu, bias=convb[:, c48:c48 + 1]),
                         reads=[pk(bk), "convb"], writes=post.keys)
                    S.skip = False
                S.skip = PRE
                for m in range(4):
                    S.op("pool", lambda e, m=m, g=g: e.tensor_scalar(out=diagD.ap[:, m * 128:(m + 1) * 128], in0=ident[:], scalar1=dch[:, g * 4 + m: g * 4 + m + 1], scalar2=1.0, op0=ALU.mult, op1=ALU.mult),
                         reads=["ident", "dch"], writes=diagD.keys)
                S.op("pool", lambda e: e.memset(CT0.ap, 0.0), writes=CT0.keys)
                S.op("pool", lambda e: e.memset(CT1.ap, 0.0), writes=CT1.keys)
                ct4 = CT.ap.rearrange("p (j c l) -> p j c l", c=2, l=64)
                S.op("pool", lambda e, ct4=ct4: e.tensor_copy(CT0.ap.rearrange("p (j c l) -> p j c l", c=2, l=64)[:, :, 0, :], ct4[:, :, 0, :]), reads=CT.keys, writes=CT0.keys)
                S.op("pool", lambda e, ct4=ct4: e.tensor_copy(CT1.ap.rearrange("p (j c l) -> p j c l", c=2, l=64)[:, :, 1, :], ct4[:, :, 1, :]), reads=CT.keys, writes=CT1.keys)
                S.skip = False
                slot = None if PRE else next_w(f"z{g}")
                hs = slice(g * 8, (g + 1) * 8)
                BL[0] = [6, 7]
                zs4 = [AB(13 + j, 1, BF16, 512) for j in range(4)]
                S.skip = PRE
                for j in range(4):
                    bz = bankA()
                    proj_as(slot, 0, 512, j, hT_fn, [("hT",)], bz)
                    S.op("act", lambda e, bz=bz, j=j, zs4=zs4: e.activation(out=zs4[j].ap, in_=PB(bz), func=AF.Silu), reads=[pk(bz)], writes=zs4[j].keys)
                S.skip = False

                def P1(j):
                    blk = slice(j * 128, (j + 1) * 128)
                    S.skip = PRE
                    bk = bankB()
                    S.op("pe", lambda e, bk=bk, blk=blk: e.matmul(PB(bk)[:, 0:128], lhsT=BT.ap[:, blk], rhs=CT.ap[:, blk], start=True, stop=True),
                         reads=BT.keys + CT.keys, writes=[pk(bk)])
                    S.op("dve", lambda e, bk=bk: e.tensor_tensor(out=cbTm.ap, in0=PB(bk)[:, 0:128], in1=MBD, op=ALU.mult), reads=[pk(bk), "ssdm"], writes=cbTm.keys)
                    S.skip = False
                    bk = bankB()
                    S.op("pe", lambda e, bk=bk, blk=blk: e.transpose(PBb(bk)[:, 0:128], BT.ap[:, blk], ident[:]), reads=BT.keys + ["ident"], writes=[pk(bk)])
                    S.op("act", lambda e, bk=bk, j=j: e.activation(out=Btok.ap[:, j * 128:(j + 1) * 128], in_=PBb(bk)[:, 0:128], func=AF.Copy), reads=[pk(bk)], writes=Btok.keys)
                    S.skip = PRE
                    S.op("pool", lambda e, j=j, hs=hs: e.tensor_tensor(out=Rb.ap.rearrange("p (h l) -> p h l", l=128),
                                                                in0=TRI.unsqueeze(1).to_broadcast([128, 8, 128]),
                                                                in1=dta[:, j, hs].unsqueeze(2).to_broadcast([128, 8, 128]), op=ALU.mult),
                         reads=["ssdm", dtaB.keys[0]], writes=Rb.keys)
                    bs = [bankB(), bankB()]
                    for q in range(2):
                        S.op("pe", lambda e, q=q, bs=bs: e.matmul(PB(bs[q]), lhsT=UU, rhs=Rb.ap[:, q * 512:(q + 1) * 512], start=True, stop=True),
                             reads=["ssdm"] + Rb.keys, writes=[pk(bs[q])])
                        S.op("act", lambda e, q=q, bs=bs: e.activation(out=dec.ap[:, q * 512:(q + 1) * 512], in_=PB(bs[q]), func=AF.Exp), reads=[pk(bs[q])], writes=dec.keys)
                    S.op("dve", lambda e: e.tensor_tensor(out=dec.ap.rearrange("p (h l) -> p h l", l=128), in0=dec.ap.rearrange("p (h l) -> p h l", l=128),
                                                          in1=cbTm.ap.unsqueeze(1).to_broadcast([128, 8, 128]), op=ALU.mult),
                         reads=dec.keys + cbTm.keys, writes=dec.keys)
                    S.skip = False
                    bk = bankB()
                    for m in range(4):
                        S.op("pe", lambda e, bk=bk, m=m, blk=blk: e.transpose(PBb(bk)[:, m * 128:(m + 1) * 128], xsT[m].ap[:, blk], ident[:]),
                             reads=xsT[m].keys + ["ident"], writes=[pk(bk)], inc=(m == 3))
                    S.op("dve", lambda e, bk=bk, j=j, hs=hs: e.tensor_tensor(out=xdt.ap.rearrange("p (h q) -> p h q", q=64), in0=PBb(bk)[:, 0:512].rearrange("p (h q) -> p h q", q=64),
                                                                    in1=dtx[:, j, hs].unsqueeze(2).to_broadcast([128, 8, 64]), op=ALU.mult),
                         reads=[pk(bk), dtxB.keys[0]], writes=xdt.keys)
                    S.op("pool", lambda e, j=j, g=g: e.tensor_tensor(out=xdtt.ap.rearrange("p (h q) -> p h q", q=64), in0=xdt.ap.rearrange("p (h q) -> p h q", q=64),
                                                                     in1=dex[:, j, 64 + g * 8: 64 + (g + 1) * 8].unsqueeze(2).to_broadcast([128, 8, 64]), op=ALU.mult),
                         reads=xdt.keys + dexB.keys, writes=xdtt.keys)

                def P2a(j):
                    blk = slice(j * 128, (j + 1) * 128)
                    S.skip = PRE
                    bys = 3
                    for m in range(4):
                        S.op("pe", lambda e, m=m, blk=blk: e.matmul(PB(3)[:, m * 128:(m + 1) * 128], lhsT=xsT[m].ap[:, blk], rhs=diagD.ap[:, m * 128:(m + 1) * 128],
                                                                   start=(m == 0), stop=False),
                             reads=xsT[m].keys + diagD.keys, writes=[pk(bys)], inc=False)
                    for hh in range(8):
                        S.op("pe", lambda e, hh=hh: e.matmul(PB(3)[:, hh * 64:(hh + 1) * 64], lhsT=dec.ap[:, hh * 128:(hh + 1) * 128], rhs=xdt.ap[:, hh * 64:(hh + 1) * 64],
                                                            start=False, stop=(hh == 7)),
                             reads=dec.keys + xdt.keys, writes=[pk(bys)], inc=(hh == 7))
                    S.skip = False
                    for c in range(2):
                        rows = slice(c * 64, (c + 1) * 64)
                        S.op("pe", lambda e, c=c, rows=rows, j=j: e.matmul(PB(4 + c), lhsT=Btok.ap[rows, j * 128:(j + 1) * 128], rhs=xdtt.ap[rows, :], start=True, stop=True),
                             reads=Btok.keys + xdtt.keys, writes=[pk(4 + c)])

                def P2b(j):
                    nonlocal_sti = sti_box
                    blk = slice(j * 128, (j + 1) * 128)
                    bys = 3
                    S.skip = PRE
                    byi = bankB()
                    for c in range(2):
                        S.skip = PRE
                        cur = stb[nonlocal_sti[0] % 2]
                        nonlocal_sti[0] += 1
                        S.op("act", lambda e, cur=cur, Sg=Sg: e.activation(out=cur.ap, in_=Sg, func=AF.Copy), reads=[sgk], writes=cur.keys)
                        ctp = CT0 if c == 0 else CT1
                        S.op("pe", lambda e, byi=byi, cur=cur, ctp=ctp, c=c, blk=blk: e.matmul(PB(byi), lhsT=ctp.ap[:, blk], rhs=cur.ap, start=(c == 0), stop=(c == 1)),
                             reads=ctp.keys + cur.keys, writes=[pk(byi)], inc=(c == 1))
                        S.skip = False
                        cofs = 128 + c * 64 + g * 8
                        S.op("dve", lambda e, j=j, cofs=cofs, Sg=Sg: e.tensor_tensor(out=Sg.rearrange("p (h q) -> p h q", q=64), in0=Sg.rearrange("p (h q) -> p h q", q=64),
                                                                               in1=dex[:, j, cofs:cofs + 8].unsqueeze(2).to_broadcast([128, 8, 64]), op=ALU.mult),
                             reads=[sgk] + dexB.keys, writes=[sgk])
                        S.op("dve", lambda e, c=c, Sg=Sg: e.tensor_tensor(out=Sg, in0=PB(4 + c), in1=Sg, op=ALU.add), reads=[pk(4 + c), sgk], writes=[sgk])
                    S.skip = PRE
                    S.op("dve", lambda e, byi=byi, j=j, hs=hs: e.tensor_tensor(out=tmpf.ap.rearrange("p (h q) -> p h q", q=64), in0=PB(byi).rearrange("p (h q) -> p h q", q=64),
                                                                      in1=dex[:, j, hs].unsqueeze(2).to_broadcast([128, 8, 64]), op=ALU.mult),
                         reads=[pk(byi)] + dexB.keys, writes=tmpf.keys)
                    S.op("dve", lambda e: e.tensor_tensor(out=tmpf.ap, in0=PB(3), in1=tmpf.ap, op=ALU.add), reads=[pk(bys)] + tmpf.keys, writes=tmpf.keys)
                    S.op("dve", lambda e, j=j, zs4=zs4: e.tensor_tensor(out=yg.ap, in0=tmpf.ap, in1=zs4[j].ap, op=ALU.mult), reads=tmpf.keys + zs4[j].keys, writes=yg.keys)
                    S.op("act", lambda e, j=j, g=g: e.activation(out=junk.ap, in_=yg.ap, func=AF.Square, accum_out=ssq[:, j, g:g + 1]), reads=yg.keys, writes=junk.keys + [("ssq", j, g)])
                    bk = bankB()
                    for m in range(4):
                        S.op("pe", lambda e, bk=bk, m=m: e.transpose(PBb(bk)[:, m * 128:(m + 1) * 128], yg.ap[:, m * 128:(m + 1) * 128], ident[:]),
                             reads=yg.keys + ["ident"], writes=[pk(bk)], inc=(m == 3))
                    S.op("act", lambda e, bk=bk, g=g, blk=blk: e.activation(out=ysT[:, g * 4:(g + 1) * 4, blk], in_=PBb(bk)[:, 0:512].rearrange("p (a b) -> p a b", b=128), func=AF.Copy),
                         reads=[pk(bk)], writes=[("ysT", g * 4 + m2) for m2 in range(4)])
                    S.skip = False

                sti_box = [sti]
                P1(0)
                for j in range(4):
                    P2a(j)
                    if j + 1 < 4:
                        P1(j + 1)
                    P2b(j)
                sti = sti_box[0]
                BL[0] = [3, 4, 5, 6, 7]

            S.skip = False
            chk(6)
            if PRE:
                continue
            if ti == 0:
                dump("d_ysT", ysT[:].rearrange("p a b -> p (a b)"), [("ysT",)], [128, 32 * 512], BF16)
                dump("d_ssq", ssq[:].rearrange("p a b -> p (a b)"), [("ssq",)], [128, 32], F32)
                dump("d_Sssd", Sssd[:].rearrange("p a b -> p (a b)"), [("Sssd",)], [128, 8 * 512], F32)
            sgr = [AB(m, 1, BF16, 512) for m in range(16)]
            sgs = [AB(16 + m, 1, BF16, 512) for m in range(16)]
            rsrow = AB(32, 2, F32, 512)
            dgf = AB(34, 1, F32, 128)
            t1 = AB(35, 2, F32, 512)
            t2 = AB(37, 2, F32, 512)
            S.op("dve", lambda e: e.reduce_sum(sml[:, 4:8], ssq[:], axis=mybir.AxisListType.X), reads=[("ssq",)], writes=[("sml", 4)])
            S.op("dve", lambda e: e.tensor_scalar(out=sml[:, 4:8], in0=sml[:, 4:8], scalar1=1.0 / 4096, scalar2=EPS, op0=ALU.mult, op1=ALU.add), reads=[("sml", 4)], writes=[("sml", 4)])
            S.op("act", lambda e: e.activation(out=sml[:, 4:8], in_=sml[:, 4:8], func=AF.Sqrt), reads=[("sml", 4)], writes=[("sml", 4)])
            S.op("dve", lambda e: e.reciprocal(sml[:, 8:12], sml[:, 4:8]), reads=[("sml", 4)], writes=[("sml", 8)])
            brs = bankB()
            for j in range(4):
                S.op("dve", lambda e, j=j: e.tensor_scalar_mul(dgf.ap, identf[:], sml[:, 8 + j:9 + j]), reads=["identf", ("sml", 8)], writes=dgf.keys)
                S.op("pe", lambda e, j=j, brs=brs: e.matmul(PB(brs)[:, j * 128:(j + 1) * 128], lhsT=onesf[:], rhs=dgf.ap, start=(j == 0), stop=(j == 3)),
                     reads=["onesf"] + dgf.keys, writes=[pk(brs)])
            S.op("act", lambda e, brs=brs: e.activation(out=rsrow.ap, in_=PB(brs), func=AF.Copy), reads=[pk(brs)], writes=rsrow.keys)
            yR_fn = lambda kc: yR[:, kc, :]
            ys_fn = lambda kc: ysT[:, kc, :]
            for oc in range(4):
                slot = next_w(f"gr{oc}")
                for m in range(4):
                    bk = bankA()
                    proj_ws(slot, 0, m, hT_fn, [("hT",)], bk)
                    S.op("act", lambda e, bk=bk, m=m, oc=oc: e.activation(out=sgr[oc * 4 + m].ap, in_=PB(bk), func=AF.Sigmoid), reads=[pk(bk)], writes=sgr[oc * 4 + m].keys)
                slot = next_w(f"gs{oc}")
                for m in range(4):
                    bk = bankA()
                    proj_ws(slot, 0, m, hT_fn, [("hT",)], bk)
                    S.op("act", lambda e, bk=bk: e.activation(out=t1.ap, in_=PB(bk), func=AF.Sigmoid), reads=[pk(bk)], writes=t1.keys)
                    S.op("dve", lambda e, m=m, oc=oc: e.tensor_tensor(out=sgs[oc * 4 + m].ap, in0=t1.ap, in1=rsrow.ap, op=ALU.mult), reads=t1.keys + rsrow.keys, writes=sgs[oc * 4 + m].keys)
            for oc in range(4):
                slot_r = next_w(f"br{oc}")
                prb = []
                for m in range(4):
                    bk = bankB()
                    proj_ws(slot_r, 0, m, yR_fn, [("yR",)], bk)
                    prb.append(bk)
                slot0 = next_w(f"bs0{oc}")
                slot1 = next_w(f"bs1{oc}", prefetch=False)
                for m in range(4):
                    bk = bankA()
                    proj_ws(slot0, 0, m, ys_fn, [("ysT",)], bk, first=True, last=False, kofs=0)
                    proj_ws(slot1, 0, m, ys_fn, [("ysT",)], bk, first=False, last=True, kofs=16)
                    S.op("dve", lambda e, m=m, prb=prb, oc=oc: e.tensor_tensor(out=t1.ap, in0=PB(prb[m]), in1=sgr[oc * 4 + m].ap, op=ALU.mult), reads=[pk(prb[m])] + sgr[oc * 4 + m].keys, writes=t1.keys)
                    S.op("dve", lambda e, m=m, bk=bk, oc=oc: e.tensor_tensor(out=t2.ap, in0=PB(bk), in1=sgs[oc * 4 + m].ap, op=ALU.mult), reads=[pk(bk)] + sgs[oc * 4 + m].keys, writes=t2.keys)
                    S.op("pool", lambda e, m=m, oc=oc: e.tensor_tensor(out=hT[:, oc * 4 + m, :], in0=t1.ap, in1=t2.ap, op=ALU.add),
                         reads=t1.keys + t2.keys, writes=[("hT",)])
                    if m == 3:
                        prefetch_next()
            chk(7)
            if ti == 0:
                dump("d_mrg", hT[:].rearrange("p a b -> p (a b)"), [("hT",)], [128, 16 * 512], BF16)
            xr = [AB(8 * j, 8, F32, 2048) for j in range(4)]
            for j in range(4):
                r0 = tok0 + j * 128
                S.op("sp", lambda e, j=j, r0=r0: e.dma_start(out=xr[j].ap, in_=x[r0:r0 + 128, :]), writes=xr[j].keys, dsem=f"xr{j}")
            mrg_fn = hT_fn
            for oc in range(4):
                slot = next_w(f"o{oc}")
                for j in range(4):
                    bk = bankA()
                    proj_as(slot, 0, 512, j, mrg_fn, [("hT",)], bk)
                    S.op("dve", lambda e, bk=bk, j=j, oc=oc: e.tensor_tensor(out=xr[j].ap[:, oc * 512:(oc + 1) * 512], in0=PB(bk), in1=xr[j].ap[:, oc * 512:(oc + 1) * 512], op=ALU.add),
                         reads=[pk(bk)] + xr[j].keys, writes=xr[j].keys)
            for j in range(4):
                r0 = tok0 + j * 128
                S.op("act", lambda e, j=j: e.activation(out=hb, in_=xr[j].ap, func=AF.Square, accum_out=sml[:, 12:13]), reads=xr[j].keys, writes=HBK + [("sml", 12)])
                S.op("dve", lambda e: e.tensor_scalar(out=sml[:, 13:14], in0=sml[:, 12:13], scalar1=1.0 / D, scalar2=EPS, op0=ALU.mult, op1=ALU.add), reads=[("sml", 12)], writes=[("sml", 13)])
                S.op("act", lambda e: e.activation(out=sml[:, 14:15], in_=sml[:, 13:14], func=AF.Sqrt), reads=[("sml", 13)], writes=[("sml", 14)])
                S.op("dve", lambda e: e.reciprocal(sml[:, 15:16], sml[:, 14:15]), reads=[("sml", 14)], writes=[("sml", 15)])
                S.op("dve", lambda e, j=j: e.scalar_tensor_tensor(out=xr[j].ap, in0=xr[j].ap, scalar=sml[:, 15:16], in1=normf[:], op0=ALU.mult, op1=ALU.mult),
                     reads=xr[j].keys + [("sml", 15), "normf"], writes=xr[j].keys)
                ro = (ti - NPRE) * T + j * 128
                S.op("sp", lambda e, j=j, ro=ro: e.dma_start(out=out[ro:ro + 128, :], in_=xr[j].ap), reads=xr[j].keys, dsem=f"o{j}")

        except _Stop:
            pass
        S.emit(st)
    return nc


def _consts():
    c = {}
    c["c_ident"] = np.eye(128, dtype=np.float32).astype(ml_dtypes.bfloat16)
    c["c_identf"] = np.eye(128, dtype=np.float32)
    c["c_onesf"] = np.ones((128, 128), np.float32)
    c["c_ones"] = np.full((128, 128), 1.0 / 256.0, np.float32).astype(ml_dtypes.bfloat16)
    idx = np.arange(128)
    same = (idx[:, None] // 64) == (idx[None, :] // 64)
    lg = np.log1p(-(2.0 ** (-5.0 - np.arange(8, dtype=np.float64))))
    rm = np.zeros((128, 8, 128), np.float64)
    for h in range(8):
        rm[:, h, :] = np.where(same, np.exp(np.abs(idx[:, None] - idx[None, :]) * lg[h]), 0.0) * (256.0 ** -0.5)
    c["c_retmask"] = rm.reshape(128, 1024).astype(np.float32).astype(ml_dtypes.bfloat16)
    j = idx[:, None]
    l = idx[None, :]
    TRI = (same & (j <= l)).astype(np.float32)
    UBD = (same & (j > l)).astype(np.float32)
    UU = (j > l).astype(np.float32)
    ONC0 = np.repeat((idx < 64).astype(np.float32)[:, None], 128, 1)
    ONC1 = np.repeat((idx >= 64).astype(np.float32)[:, None], 128, 1)
    MBD = (same & (l >= j)).astype(np.float32)
    c["c_ssd"] = np.stack([TRI, UBD, UU, ONC0, ONC1, MBD], 1).reshape(128, 768).astype(np.float32)
    qd = np.exp((np.arange(64)[None, :] + 1.0) * lg[:, None])
    c["c_qdec"] = np.repeat(qd.reshape(1, 512), 128, 0).astype(np.float32)
    kd = np.exp((63.0 - (idx[:, None] % 64)) * lg[None, :]) * (256.0 ** -0.5)
    c["c_kdec"] = kd.astype(np.float32)
    half = 128
    c["c_invf"] = (np.float32(10000.0) ** (-np.arange(half, dtype=np.float32) / np.float32(half))).astype(np.float32).reshape(128, 1)
    return c


def _prep_inputs(inputs, NT, ncores, NPRE=0):
    x = np.asarray(inputs["x"], np.float32)
    pos = np.asarray(inputs["positions"], np.int32)
    com = {}
    com["w_in"] = np.ascontiguousarray(np.asarray(inputs["w_in"], np.float32)[0])
    com["w_brr"] = np.ascontiguousarray(np.asarray(inputs["w_br_ret"], np.float32)[0])
    com["w_brs"] = np.ascontiguousarray(np.asarray(inputs["w_br_ssd"], np.float32)[0])
    com["w_out"] = np.ascontiguousarray(np.asarray(inputs["w_out"], np.float32)[0])
    com["n1col"] = np.ascontiguousarray(np.asarray(inputs["norm1_w"], np.float32)[0].reshape(16, 128).T)
    com["sncol"] = np.ascontiguousarray(np.asarray(inputs["ssd_norm_w"], np.float32)[0].reshape(32, 128).T)
    cw = np.asarray(inputs["conv_w"], np.float32)[0]
    com["convw"] = np.ascontiguousarray(cw.reshape(4, 48, 128).transpose(2, 1, 0).reshape(128, 192))
    com["convb"] = np.ascontiguousarray(np.asarray(inputs["conv_b"], np.float32)[0].reshape(48, 128).T)
    dsk = np.repeat(np.asarray(inputs["d_skip"], np.float32)[0], 64)
    com["dch"] = np.ascontiguousarray(dsk.reshape(32, 128).T)
    com["dtb"] = np.asarray(inputs["dt_bias"], np.float32).reshape(1, 64)
    com["alog"] = np.asarray(inputs["a_log"], np.float32).reshape(1, 64)
    com["normf"] = np.asarray(inputs["norm_f_w"], np.float32).reshape(1, 2048)
    com.update(_consts())
    maps = []
    nmain = (NT - NPRE) * T
    npre = NPRE * T
    for c in range(ncores):
        m = dict(com)
        if NPRE == 0:
            b = c % x.shape[0]
            m["x"] = np.ascontiguousarray(x[b, :NT * T])
            m["pos"] = np.ascontiguousarray(pos[b, :NT * T].reshape(1, -1))
            m["flag"] = np.ones((128, 1), np.float32)
        else:
            nhalf = x.shape[1] // nmain
            b, hf = divmod(c, nhalf)
            own = x[b, hf * nmain:(hf + 1) * nmain]
            pown = pos[b, hf * nmain:(hf + 1) * nmain]
            if hf == 0:
                prev = np.zeros((npre, x.shape[2]), np.float32)
                pprev = np.zeros((npre,), np.int32)
            else:
                prev = x[b, hf * nmain - npre: hf * nmain]
                pprev = pos[b, hf * nmain - npre: hf * nmain]
            m["x"] = np.ascontiguousarray(np.concatenate([prev, own], 0))
            m["pos"] = np.ascontiguousarray(np.concatenate([pprev, pown], 0).reshape(1, -1))
            m["flag"] = np.full((128, 1), float(hf != 0), np.float32)
        maps.append(m)
    return maps


_NC_CACHE = {}


def kernel(**inputs):
    NT, NPRE = 16, 8
    key = (NT, NPRE)
    if key not in _NC_CACHE:
        _NC_CACHE[key] = build_nc(NT, NPRE=NPRE)
    nc = _NC_CACHE[key]
    maps = _prep_inputs(inputs, NT, 8, NPRE=NPRE)
    res = run_bass_kernel_spmd(nc, maps, core_ids=list(range(8)))
    x = inputs["x"]
    B, SEQ = x.shape[0], x.shape[1]
    nmain = (NT - NPRE) * T
    nhalf = SEQ // nmain
    outp = np.empty((B, SEQ, D), np.float32)
    for c in range(8):
        b, hf = divmod(c, nhalf)
        outp[b, hf * nmain:(hf + 1) * nmain] = res.results[c]["out"]
    return outp
```

```python
import math
import numpy as np
import ml_dtypes
from contextlib import ExitStack
import concourse.bass as bass
import concourse.mybir as mybir
from concourse.bass_utils import run_bass_kernel_spmd

F32 = mybir.dt.float32
BF16 = mybir.dt.bfloat16
I32 = mybir.dt.int32
AF = mybir.ActivationFunctionType
ALU = mybir.AluOpType

SAME_ENGINE_SYNC = True
RAW_ONLY_SAME_ENGINE = True
D = 2048
T = 512
EPS = 1e-6
GR = 528
NGRAN = 39


class _Stop(Exception):
    pass


class _Op:
    __slots__ = ("eng", "fn", "deps", "inc", "idx", "dsem", "dval", "raw")

    def __init__(self, eng, fn, inc, dsem):
        self.eng = eng
        self.fn = fn
        self.deps = []
        self.inc = inc
        self.idx = -1
        self.dsem = dsem
        self.dval = 0


class Sched:
    ENGS = ("pe", "act", "dve", "pool", "sp")

    def __init__(self, nc):
        self.nc = nc
        self.ops = {e: [] for e in self.ENGS}
        self.state = {}
        self.dcount = {}
        self.children = {}
        self.skip = False

    def _conf(self, key):
        fam = self.children.get(key[0], ())
        out = []
        for k in fam:
            n = min(len(k), len(key))
            if k[:n] == key[:n]:
                out.append(k)
        return out

    def _norm(self, keys):
        out = []
        for key in keys:
            if isinstance(key, list):
                out.extend(self._norm(key))
            elif isinstance(key, tuple):
                out.append(key)
            else:
                out.append((key,))
        return out

    def op(self, eng, fn, reads=(), writes=(), inc=True, dsem=None):
        if self.skip:
            return None
        reads = self._norm(reads)
        writes = self._norm(writes)
        o = _Op(eng, fn, inc, dsem)
        deps = []
        raw = set()
        for key in reads:
            for k in self._conf(key):
                w = self.state[k][0]
                if w is not None:
                    deps.append(w)
                    raw.add(id(w))
        for key in writes:
            for k in self._conf(key):
                w, rs = self.state[k]
                if w is not None:
                    deps.append(w)
                deps.extend(rs)
        for key in reads:
            if key not in self.state:
                self.state[key] = [None, []]
                self.children.setdefault(key[0], set()).add(key)
            self.state[key][1].append(o)
        for key in writes:
            if key not in self.state:
                self.state[key] = [None, []]
                self.children.setdefault(key[0], set()).add(key)
            for k in self._conf(key):
                if k != key and len(k) > len(key):
                    self.state[k] = [None, []]
            self.state[key] = [o, []]
        if dsem is not None:
            self.dcount[dsem] = self.dcount.get(dsem, 0) + 1
            o.dval = 16 * self.dcount[dsem]
            o.inc = True
        o.idx = len(self.ops[eng])
        seen = set()
        for d in deps:
            if id(d) in seen or d is o:
                continue
            seen.add(id(d))
            if d.dsem is None and not d.inc:
                lst = self.ops[d.eng]
                covered = False
                for j in range(d.idx + 1, len(lst)):
                    if lst[j].inc and lst[j].dsem is None:
                        covered = True
                        break
                if not covered:
                    d.inc = True
            o.deps.append(d)
        o.raw = raw
        self.ops[eng].append(o)
        return o

    def emit(self, stack):
        nc = self.nc
        sems = {}
        for e in self.ENGS:
            sems[e] = stack.enter_context(nc.semaphore("s_" + e))
        for name in self.dcount:
            sems["d:" + name] = stack.enter_context(nc.semaphore("d_" + name))
        for e in self.ENGS:
            for o in reversed(self.ops[e]):
                if o.dsem is None:
                    o.inc = True
                    break
        val = {}
        for e in self.ENGS:
            lst = self.ops[e]
            cnt = 0
            for o in lst:
                if o.dsem is not None:
                    val[id(o)] = ("d:" + o.dsem, o.dval)
                elif o.inc:
                    cnt += 1
                    val[id(o)] = (e, cnt)
                else:
                    val[id(o)] = (e, cnt + 1)
        block = stack.enter_context(nc.Block())

        def body_for(e):
            def body(engobj):
                waited = {}
                for o in self.ops[e]:
                    need = {}
                    for d in o.deps:
                        sname, v = val[id(d)]
                        if d.dsem is None and d.eng == e:
                            if e == "pe" or not SAME_ENGINE_SYNC:
                                continue
                            if RAW_ONLY_SAME_ENGINE and id(d) not in o.raw:
                                continue
                        if waited.get(sname, 0) >= v:
                            continue
                        if need.get(sname, 0) < v:
                            need[sname] = v
                    for sname, v in need.items():
                        engobj.wait_ge(sems[sname], v)
                        waited[sname] = v
                    ins = o.fn(engobj)
                    if o.dsem is not None:
                        ins.then_inc(sems["d:" + o.dsem], 16)
                    elif o.inc:
                        ins.then_inc(sems[e], 1)
                if e == "sp":
                    for name, c in self.dcount.items():
                        engobj.wait_ge(sems["d:" + name], 16 * c)
            return body

        block.tensor(body_for("pe"))
        block.scalar(body_for("act"))
        block.vector(body_for("dve"))
        block.gpsimd(body_for("pool"))
        block.sync(body_for("sp"))


def weight_groups():
    G = []
    for h in range(8):
        G.append((f"qk{h}", "w_in", 0, [(h * 256, 256), (2048 + h * 256, 256)], "n1"))
        G.append((f"vg{h}", "w_in", 0, [(4096 + h * 256, 256), (6144 + h * 256, 256)], "n1"))
    G.append(("dt", "w_in", 0, [(18432, 64)], "n1"))
    for g in range(8):
        G.append((f"x{g}", "w_in", 0, [(12288 + g * 512, 512)], "n1"))
        G.append((f"bc{g}", "w_in", 0, [(16384 + g * 128, 128), (17408 + g * 128, 128)], "n1"))
        G.append((f"z{g}", "w_in", 0, [(8192 + g * 512, 512)], "n1"))
    for oc in range(4):
        G.append((f"gr{oc}", "w_in", 0, [(18496 + oc * 512, 512)], "n1"))
        G.append((f"gs{oc}", "w_in", 0, [(20544 + oc * 512, 512)], "n1"))
    for oc in range(4):
        G.append((f"br{oc}", "w_brr", 0, [(oc * 512, 512)], None))
        G.append((f"bs0{oc}", "w_brs", 0, [(oc * 512, 512)], "sn"))
        G.append((f"bs1{oc}", "w_brs", 16, [(oc * 512, 512)], "sn"))
    for oc in range(4):
        G.append((f"o{oc}", "w_out", 0, [(oc * 512, 512)], None))
    return G


def build_nc(NT, debug=None, NPRE=0):
    nc = bass.Bass("TRN2", target_bir_lowering=False)
    NTOK = NT * T
    NMAIN = NT - NPRE

    def din(name, shape, dt=F32):
        return nc.dram_tensor(name, shape, dt, kind="ExternalInput").ap()

    x = din("x", [NTOK, D])
    pos = din("pos", [1, NTOK], I32)
    wsrc = {"w_in": din("w_in", [D, 22592]), "w_brr": din("w_brr", [2048, 2048]),
            "w_brs": din("w_brs", [4096, 2048]), "w_out": din("w_out", [2048, 2048])}
    n1col_d = din("n1col", [128, 16])
    sncol_d = din("sncol", [128, 32])
    convw_d = din("convw", [128, 192])
    convb_d = din("convb", [128, 48])
    dch_d = din("dch", [128, 32])
    dtb_d = din("dtb", [1, 64])
    alog_d = din("alog", [1, 64])
    normf_d = din("normf", [1, D])
    cid_d = din("c_ident", [128, 128], BF16)
    cidf_d = din("c_identf", [128, 128])
    conesf_d = din("c_onesf", [128, 128])
    cones_d = din("c_ones", [128, 128], BF16)
    crm_d = din("c_retmask", [128, 8 * 128], BF16)
    cssd_d = din("c_ssd", [128, 6 * 128])
    cqd_d = din("c_qdec", [128, 8 * 64])
    ckd_d = din("c_kdec", [128, 8])
    cinvf_d = din("c_invf", [128, 1])
    out = nc.dram_tensor("out", [NMAIN * T, D], F32, kind="ExternalOutput").ap()
    flag_d = din("flag", [128, 1])
    WG = weight_groups()
    NG = len(WG)
    ws = nc.dram_tensor("ws", [NG, 128, 16 * 512], BF16, kind="Internal").ap()
    gidx = {g[0]: i for i, g in enumerate(WG)}

    st = ExitStack()
    with st:
        S = Sched(nc)

        def sb(name, shape, dt):
            return st.enter_context(nc.sbuf_tensor("s_" + name, shape, dt))

        xbuf = [sb(f"xbuf{i}", [128, D], F32) for i in range(2)]
        hT = sb("hT", [128, 16, T], BF16)
        wbuf = [sb(f"wbuf{i}", [128, 16, 512], BF16) for i in range(2)]
        yR = sb("yR", [128, 16, T], BF16)
        ysT = sb("ysT", [128, 32, T], BF16)
        Sret = sb("Sret", [128, 8, 512], F32)
        Sssd = sb("Sssd", [128, 8, 512], F32)
        arena = sb("arena", [128, NGRAN * GR], BF16)
        ident = sb("ident", [128, 128], BF16)
        identf = sb("identf", [128, 128], F32)
        onesf = sb("onesf", [128, 128], F32)
        onesS = sb("onesS", [128, 128], BF16)
        retmask = sb("retmask", [128, 8, 128], BF16)
        ssdm = sb("ssdm", [128, 6, 128], F32)
        qdec = sb("qdec", [128, 8, 64], F32)
        kdec = sb("kdec", [128, 8], F32)
        invf = sb("invf", [128, 1], F32)
        normf = sb("normf", [128, D], F32)
        n1col = sb("n1col", [128, 16], F32)
        sncol = sb("sncol", [128, 32], F32)
        convw = sb("convw", [128, 48, 4], F32)
        convb = sb("convb", [128, 48], F32)
        dch = sb("dch", [128, 32], F32)
        dtb = sb("dtb", [128, 64], F32)
        arow = sb("arow", [128, 64], F32)
        halo = sb("halo", [128, 48, 4], BF16)
        ssq = sb("ssq", [128, 4, 8], F32)
        sml = sb("sml", [128, 16], F32)
        flg = sb("flg", [128, 1], F32)
        hb = ysT[:, 0:4, :].rearrange("p a b -> p (a b)")
        HBK = [("ysT", i) for i in range(4)]

        pbank = [st.enter_context(nc.psum_tensor(f"pb{i}", [128, 512], F32)) for i in range(8)]
        rrA = [0]
        rrB = [0]

        AL = [[0, 1, 2]]

        def bankA():
            lst = AL[0]
            i = lst[rrA[0] % len(lst)]
            rrA[0] += 1
            return i

        BL = [[3, 4, 5, 6, 7]]

        def bankB():
            lst = BL[0]
            i = lst[rrB[0] % len(lst)]
            rrB[0] += 1
            return i

        def PB(i):
            return pbank[i][:]

        def PBb(i):
            return pbank[i][:].bitcast(BF16)

        def pk(i):
            return ("pb", i)

        def dump(name, ap, keys, shape, dt):
            if debug != "dump":
                return
            d = nc.dram_tensor(name, shape, dt, kind="ExternalOutput").ap()
            S.op("sp", lambda e: e.dma_start(out=d, in_=ap), reads=keys, dsem="dbg_" + name)

        class AB:
            def __init__(self, g0, ng, dt, n):
                base = arena[:, g0 * GR: (g0 + ng) * GR]
                if dt == F32:
                    self.ap = base.bitcast(F32)[:, 0:n]
                elif dt == I32:
                    self.ap = base.bitcast(I32)[:, 0:n]
                else:
                    self.ap = base[:, 0:n]
                self.keys = [("ar", g) for g in range(g0, g0 + ng)]

        def ld(dst_ap, src_ap, key, name):
            S.op("sp", lambda e: e.dma_start(out=dst_ap, in_=src_ap), writes=[key], dsem=name)

        ld(ident[:], cid_d[:, :], "ident", "c0")
        ld(identf[:], cidf_d[:, :], "identf", "c1")
        ld(onesf[:], conesf_d[:, :], "onesf", "c2")
        ld(onesS[:], cones_d[:, :], "onesS", "c3")
        ld(retmask[:].rearrange("p a b -> p (a b)"), crm_d[:, :], "retmask", "c4")
        ld(ssdm[:].rearrange("p a b -> p (a b)"), cssd_d[:, :], "ssdm", "c5")
        ld(qdec[:].rearrange("p a b -> p (a b)"), cqd_d[:, :], "qdec", "c6")
        ld(kdec[:], ckd_d[:, :], "kdec", "c7")
        ld(invf[:], cinvf_d[:, :], "invf", "c8")
        ld(flg[:], flag_d[:, :], "flg", "c17")
        ld(normf[:], normf_d.partition_broadcast(128), "normf", "c9")
        ld(n1col[:], n1col_d[:, :], "n1col", "c10")
        ld(sncol[:], sncol_d[:, :], "sncol", "c11")
        ld(convw[:].rearrange("p a b -> p (a b)"), convw_d[:, :], "convw", "c12")
        ld(convb[:], convb_d[:, :], "convb", "c13")
        ld(dch[:], dch_d[:, :], "dch", "c14")
        ld(dtb[:], dtb_d.partition_broadcast(128), "dtb", "c15")
        ld(arow[:], alog_d.partition_broadcast(128), "arow", "c16")
        S.op("act", lambda e: e.activation(out=arow[:], in_=arow[:], func=AF.Exp), reads=["arow"], writes=["arow"])
        S.op("dve", lambda e: e.tensor_scalar_mul(arow[:], arow[:], -1.0), reads=["arow"], writes=["arow"])
        S.op("pool", lambda e: e.memset(Sret[:], 0.0), writes=["Sret"])
        S.op("pool", lambda e: e.memset(Sssd[:], 0.0), writes=["Sssd"])
        S.op("pool", lambda e: e.memset(halo[:], 0.0), writes=["halo"])

        TRI, UBD, UU, ONC0, ONC1, MBD = [ssdm[:, i, :] for i in range(6)]

        fslots = [(xbuf[0][:], [("xbuf", 0)]), (xbuf[1][:], [("xbuf", 1)])]
        for i_ in range(4):
            fslots.append((ysT[:, 8 * i_:8 * i_ + 8, :].rearrange("p a b -> p (a b)").bitcast(F32), [("ysT", c_) for c_ in range(8 * i_, 8 * i_ + 8)]))
        for i_ in range(2):
            fslots.append((yR[:, 8 * i_:8 * i_ + 8, :].rearrange("p a b -> p (a b)").bitcast(F32), [("yR", c_) for c_ in range(8 * i_, 8 * i_ + 8)]))
        NFS = len(fslots)
        it = 0
        fi = 0
        for gi, (gname, src, kc0, segs, scale) in enumerate(WG):
            for kq in range(4):
                bi = it % 8
                it += 1
                k0 = kc0 + kq * 4
                off = 0
                bst = wbuf[bi // 4][:, 4 * (bi % 4):4 * (bi % 4) + 4, :]
                bkey = ("wbuf", bi // 4, bi % 4)
                for si, (c0, n) in enumerate(segs):
                    fap, fkeys = fslots[fi % NFS]
                    fsem = f"cv{fi % NFS}"
                    fi += 1
                    fst = fap[:, 0:n * 4].rearrange("p (k c) -> p k c", k=4)
                    srcap = wsrc[src][k0 * 128:(k0 + 4) * 128, c0:c0 + n].rearrange("(k p) c -> p k c", p=128)
                    S.op("sp", lambda e, fst=fst, srcap=srcap: e.dma_start(out=fst, in_=srcap),
                         writes=fkeys, dsem=fsem)
                    eng = ("dve", "dve", "pool")[fi % 3]
                    dst = bst[:, :, off:off + n]
                    if scale is None:
                        S.op(eng, lambda e, dst=dst, fst=fst: e.tensor_copy(dst, fst),
                             reads=fkeys, writes=[bkey + (si,)])
                    else:
                        col = n1col if scale == "n1" else sncol
                        cb = col[:, k0:k0 + 4].unsqueeze(2).to_broadcast([128, 4, n])
                        S.op(eng, lambda e, dst=dst, fst=fst, cb=cb: e.tensor_tensor(out=dst, in0=fst, in1=cb, op=ALU.mult),
                             reads=fkeys + ["n1col", "sncol"], writes=[bkey + (si,)])
                    off += n
                dstd = ws[gi, :, kq * 2048:(kq + 1) * 2048].rearrange("p (k c) -> p k c", k=4)[:, :, 0:off]
                srcs = bst[:, :, 0:off]
                S.op("act", lambda e, dstd=dstd, srcs=srcs: e.dma_start(out=dstd, in_=srcs),
                     reads=[bkey], writes=[("ws", gi)], dsem=f"cs{bi}")

        wq = {"n": 0, "loaded": -1}
        order = []
        for ti_ in range(NT):
            for (gname_, _, _, _, _) in WG:
                if ti_ < NPRE and not (gname_.startswith("qk") or gname_.startswith("vg") or gname_ == "dt" or gname_.startswith("x") or gname_.startswith("bc")):
                    continue
                order.append((gidx[gname_], ti_ < NPRE))
        total_loads = len(order)

        def issue_load(n):
            gi, ispre = order[n]
            slot = n % 2
            gname = WG[gi][0]
            ncol = sum(nn for (_, nn) in WG[gi][3])
            c_lo, c_hi = 0, ncol
            if ispre and gname.startswith("qk"):
                c_lo, c_hi = 256, 512
            if ispre and gname.startswith("vg"):
                c_lo, c_hi = 0, 256
            src = ws[gi, :, :].rearrange("p (k c) -> p k c", c=512)[:, :, c_lo:c_hi]
            S.op("sp", lambda e, slot=slot, src=src, c_lo=c_lo, c_hi=c_hi: e.dma_start(out=wbuf[slot][:, :, c_lo:c_hi], in_=src),
                 reads=[("ws", gi)], writes=[("wbuf", slot)], dsem=f"w{slot}")

        def next_w(expect, prefetch=True):
            n = wq["n"]
            assert WG[order[n][0]][0] == expect, (WG[order[n][0]][0], expect)
            while wq["loaded"] < min(n + (1 if prefetch else 0), total_loads - 1):
                wq["loaded"] += 1
                issue_load(wq["loaded"])
            wq["n"] += 1
            return n % 2

        def prefetch_next():
            n = wq["n"]
            while wq["loaded"] < min(n, total_loads - 1):
                wq["loaded"] += 1
                issue_load(wq["loaded"])

        def proj_ws(slot, cbase, m, rhs_fn, rkeys, bank, nk=16, first=True, last=True, kofs=0):
            for kc in range(nk):
                S.op("pe", lambda e, kc=kc, slot=slot: e.matmul(PB(bank), lhsT=wbuf[slot][:, kc, cbase + m * 128: cbase + (m + 1) * 128],
                                                      rhs=rhs_fn(kc + kofs), start=(first and kc == 0), stop=(last and kc == nk - 1)),
                     reads=[("wbuf", slot)] + rkeys, writes=[pk(bank)], inc=(last and kc == nk - 1))

        def proj_as(slot, c0, n, j, lhs_fn, lkeys, bank):
            for kc in range(16):
                S.op("pe", lambda e, kc=kc, slot=slot: e.matmul(PB(bank)[:, 0:n], lhsT=lhs_fn(kc)[:, j * 128:(j + 1) * 128],
                                                      rhs=wbuf[slot][:, kc, c0:c0 + n], start=(kc == 0), stop=(kc == 15)),
                     reads=[("wbuf", slot)] + lkeys, writes=[pk(bank)], inc=(kc == 15))

        hT_fn = lambda kc: hT[:, kc, :]
        PI = math.pi
        MAGIC = 12582912.0
        C1 = 6.28125
        C2 = 2.0 * math.pi - 6.28125
        log_gamma = [math.log1p(-(2.0 ** (-5.0 - h))) for h in range(8)]
        cdec = [math.exp(64.0 * lg) for lg in log_gamma]

        def chk(k):
            if debug == k:
                raise _Stop()

        try:
          chk(1)
          for ti in range(NT):
            tok0 = ti * T
            PRE = ti < NPRE
            if NPRE > 0 and ti == NPRE:
                S.op("dve", lambda e: e.tensor_scalar_mul(Sret[:].rearrange("p a b -> p (a b)"), Sret[:].rearrange("p a b -> p (a b)"), flg[:, 0:1]), reads=[("Sret",), "flg"], writes=[("Sret",)])
                S.op("dve", lambda e: e.tensor_scalar_mul(Sssd[:].rearrange("p a b -> p (a b)"), Sssd[:].rearrange("p a b -> p (a b)"), flg[:, 0:1]), reads=[("Sssd",), "flg"], writes=[("Sssd",)])
                S.op("dve", lambda e: e.tensor_scalar_mul(halo[:].rearrange("p a b -> p (a b)"), halo[:].rearrange("p a b -> p (a b)"), flg[:, 0:1]), reads=[("halo",), "flg"], writes=[("halo",)])
            for j in range(4):
                xb = xbuf[j % 2]
                xk = ("xbuf", j % 2)
                r0 = tok0 + j * 128
                S.op("sp", lambda e, xb=xb, r0=r0: e.dma_start(out=xb[:], in_=x[r0:r0 + 128, :]), writes=[xk], dsem=f"x{j % 2}")
                S.op("act", lambda e, xb=xb: e.activation(out=hb, in_=xb[:], func=AF.Square, accum_out=sml[:, 0:1]),
                     reads=[xk], writes=HBK + [("sml", 0)])
                S.op("dve", lambda e: e.tensor_scalar(out=sml[:, 1:2], in0=sml[:, 0:1], scalar1=1.0 / D, scalar2=EPS, op0=ALU.mult, op1=ALU.add),
                     reads=[("sml", 0)], writes=[("sml", 1)])
                S.op("act", lambda e: e.activation(out=sml[:, 2:3], in_=sml[:, 1:2], func=AF.Sqrt), reads=[("sml", 1)], writes=[("sml", 2)])
                S.op("dve", lambda e: e.reciprocal(sml[:, 3:4], sml[:, 2:3]), reads=[("sml", 2)], writes=[("sml", 3)])
                S.op("act", lambda e, xb=xb: e.activation(out=hb, in_=xb[:], func=AF.Identity, scale=sml[:, 3:4]),
                     reads=[xk, ("sml", 3)], writes=HBK)
                for half in range(2):
                    bk = bankB()
                    for q in range(8):
                        kc = half * 8 + q
                        S.op("pe", lambda e, bk=bk, q=q, kc=kc: e.transpose(PBb(bk)[:, q * 128:(q + 1) * 128], hb[:, kc * 128:(kc + 1) * 128], ident[:]),
                             reads=HBK + ["ident"], writes=[pk(bk)], inc=(q == 7))
                    dst = hT[:, half * 8:(half + 1) * 8, j * 128:(j + 1) * 128]
                    srcp = PBb(bk).rearrange("p (a b) -> p a b", b=128)
                    if half == 0:
                        S.op("dve", lambda e, dst=dst, srcp=srcp: e.tensor_copy(dst, srcp), reads=[pk(bk)], writes=[("hT", j, half)])
                    else:
                        S.op("act", lambda e, dst=dst, srcp=srcp: e.activation(out=dst, in_=srcp, func=AF.Copy), reads=[pk(bk)], writes=[("hT", j, half)])

            chk(2)
            if ti == 0:
                dump("d_hT", hT[:].rearrange("p a b -> p (a b)"), [("hT",)], [128, 16 * 512], BF16)
            cosb = AB(27, 2, F32, 512)
            sinb = AB(29, 2, F32, 512)
            tA = AB(0, 2, F32, 512)
            tB = AB(2, 2, F32, 512)
            tC = AB(4, 2, F32, 512)
            tD = AB(6, 2, F32, 512)
            posi = AB(0, 2, I32, 512)
            S.op("sp", lambda e, tok0=tok0: e.dma_start(out=posi.ap, in_=pos[0:1, tok0:tok0 + T].partition_broadcast(128)), writes=posi.keys, dsem="pos")
            S.op("dve", lambda e: e.tensor_copy(tB.ap, posi.ap), reads=posi.keys, writes=tB.keys)
            S.op("dve", lambda e: e.tensor_scalar_mul(tC.ap, tB.ap, invf[:, 0:1]), reads=tB.keys + [("invf",)], writes=tC.keys)
            S.op("dve", lambda e: e.tensor_scalar(out=tB.ap, in0=tC.ap, scalar1=1.0 / (2.0 * PI), scalar2=MAGIC, op0=ALU.mult, op1=ALU.add),
                 reads=tC.keys, writes=tB.keys)
            S.op("dve", lambda e: e.tensor_scalar_add(tB.ap, tB.ap, -MAGIC), reads=tB.keys, writes=tB.keys)
            S.op("dve", lambda e: e.scalar_tensor_tensor(out=tC.ap, in0=tB.ap, scalar=-C1, in1=tC.ap, op0=ALU.mult, op1=ALU.add),
                 reads=tB.keys + tC.keys, writes=tC.keys)
            S.op("dve", lambda e: e.scalar_tensor_tensor(out=tC.ap, in0=tB.ap, scalar=-C2, in1=tC.ap, op0=ALU.mult, op1=ALU.add),
                 reads=tB.keys + tC.keys, writes=tC.keys)
            S.op("dve", lambda e: e.tensor_scalar(out=tC.ap, in0=tC.ap, scalar1=-PI, scalar2=PI, op0=ALU.max, op1=ALU.min),
                 reads=tC.keys, writes=tC.keys)
            S.op("act", lambda e: e.activation(out=sinb.ap, in_=tC.ap, func=AF.Sin), reads=tC.keys, writes=sinb.keys)
            S.op("act", lambda e: e.activation(out=tD.ap, in_=tC.ap, func=AF.Abs), reads=tC.keys, writes=tD.keys)
            S.op("dve", lambda e: e.tensor_scalar(out=tD.ap, in0=tD.ap, scalar1=-1.0, scalar2=PI / 2, op0=ALU.mult, op1=ALU.add),
                 reads=tD.keys, writes=tD.keys)
            S.op("act", lambda e: e.activation(out=cosb.ap, in_=tD.ap, func=AF.Sin), reads=tD.keys, writes=cosb.keys)

            chk(3)
            qT = [AB(8, 1, BF16, 512), AB(9, 1, BF16, 512)]
            qd = [AB(10, 1, BF16, 512), AB(11, 1, BF16, 512)]
            kT = [AB(12, 1, BF16, 512), AB(13, 1, BF16, 512)]
            ktok = AB(14, 2, BF16, 1024)
            vtok = AB(16, 2, BF16, 1024)
            sg = [AB(18, 1, BF16, 512), AB(19, 1, BF16, 512)]
            Pb = AB(20, 1, BF16, 256)
            Sb = [AB(21, 1, BF16, 512), AB(22, 1, BF16, 512)]
            ysb = [AB(23, 1, BF16, 512), AB(24, 1, BF16, 512)]
            ysq = [AB(25, 1, BF16, 512), AB(26, 1, BF16, 512)]
            sbi = 0
            AL[0] = [0, 1, 2, 5, 6, 7]
            for h in range(8):
                slot = next_w(f"qk{h}")
                for which in ([1] if PRE else [0, 1]):
                    ba = bankA()
                    proj_ws(slot, which * 256, 0, hT_fn, [("hT",)], ba)
                    bb = bankA()
                    proj_ws(slot, which * 256, 1, hT_fn, [("hT",)], bb)
                    dstT = qT if which == 0 else kT
                    S.op("dve", lambda e, ba=ba: e.tensor_tensor(out=tA.ap, in0=PB(ba), in1=cosb.ap, op=ALU.mult), reads=[pk(ba)] + cosb.keys, writes=tA.keys)
                    S.op("dve", lambda e, bb=bb: e.tensor_tensor(out=tB.ap, in0=PB(bb), in1=sinb.ap, op=ALU.mult), reads=[pk(bb)] + sinb.keys, writes=tB.keys)
                    S.op("pool", lambda e, dstT=dstT: e.tensor_tensor(out=dstT[0].ap, in0=tA.ap, in1=tB.ap, op=ALU.subtract), reads=tA.keys + tB.keys, writes=dstT[0].keys)
                    S.op("dve", lambda e, bb=bb: e.tensor_tensor(out=tC.ap, in0=PB(bb), in1=cosb.ap, op=ALU.mult), reads=[pk(bb)] + cosb.keys, writes=tC.keys)
                    S.op("dve", lambda e, ba=ba: e.tensor_tensor(out=tD.ap, in0=PB(ba), in1=sinb.ap, op=ALU.mult), reads=[pk(ba)] + sinb.keys, writes=tD.keys)
                    S.op("pool", lambda e, dstT=dstT: e.tensor_tensor(out=dstT[1].ap, in0=tC.ap, in1=tD.ap, op=ALU.add), reads=tC.keys + tD.keys, writes=dstT[1].keys)
                    if which == 0:
                        for u in range(2):
                            S.op("pool", lambda e, u=u, h=h: e.tensor_tensor(out=qd[u].ap.rearrange("p (c l) -> p c l", l=64),
                                                                               in0=qT[u].ap.rearrange("p (c l) -> p c l", l=64),
                                                                               in1=qdec[:, h:h + 1, :].to_broadcast([128, 8, 64]), op=ALU.mult),
                                 reads=qT[u].keys + ["qdec"], writes=qd[u].keys)
                slot = next_w(f"vg{h}")
                for j in range(4):
                    bk = bankA()
                    proj_as(slot, 0, 256, j, hT_fn, [("hT",)], bk)
                    S.op("act", lambda e, bk=bk, j=j: e.activation(out=vtok.ap[:, j * 256:(j + 1) * 256], in_=PB(bk)[:, 0:256], func=AF.Copy),
                         reads=[pk(bk)], writes=vtok.keys)
                S.skip = PRE
                for u in range(2):
                    bk = bankA()
                    proj_ws(slot, 256, u, hT_fn, [("hT",)], bk)
                    S.op("act", lambda e, bk=bk, u=u: e.activation(out=sg[u].ap, in_=PB(bk), func=AF.Silu), reads=[pk(bk)], writes=sg[u].keys)
                S.skip = False
                for j in range(4):
                    bk = bankB()
                    for u in range(2):
                        S.op("pe", lambda e, bk=bk, u=u, j=j: e.transpose(PBb(bk)[:, u * 128:(u + 1) * 128], kT[u].ap[:, j * 128:(j + 1) * 128], ident[:]),
                             reads=kT[u].keys + ["ident"], writes=[pk(bk)], inc=(u == 1))
                    S.op("act", lambda e, bk=bk, j=j, h=h: e.activation(out=ktok.ap[:, j * 256:(j + 1) * 256], in_=PBb(bk)[:, 0:256], func=AF.Identity, scale=kdec[:, h:h + 1]),
                         reads=[pk(bk), "kdec"], writes=ktok.keys)
                S.skip = PRE
                BL[0] = [5, 6, 7]
                by = [3, 4]
                Sh = Sret[:, h, :]
                skey = ("Sret", h)
                cur = Sb[sbi % 2]
                sbi += 1
                S.op("act", lambda e, cur=cur, Sh=Sh: e.activation(out=cur.ap, in_=Sh, func=AF.Copy), reads=[skey], writes=cur.keys)
                for j in range(4):
                    S.skip = PRE
                    blk = slice(j * 128, (j + 1) * 128)
                    bsc = bankB()
                    for u in range(2):
                        S.op("pe", lambda e, bsc=bsc, u=u, blk=blk: e.matmul(PB(bsc)[:, 0:128], lhsT=kT[u].ap[:, blk], rhs=qT[u].ap[:, blk], start=(u == 0), stop=(u == 1)),
                             reads=kT[u].keys + qT[u].keys, writes=[pk(bsc)], inc=(u == 1))
                    pslot = Pb.ap[:, (j % 2) * 128:(j % 2 + 1) * 128]
                    S.op("dve", lambda e, bsc=bsc, pslot=pslot, h=h: e.tensor_tensor(out=pslot, in0=PB(bsc)[:, 0:128], in1=retmask[:, h, :], op=ALU.mult),
                         reads=[pk(bsc), "retmask"], writes=Pb.keys)
                    for u in range(2):
                        S.op("pe", lambda e, u=u, j=j, pslot=pslot, blk=blk, by=by: e.matmul(PB(by[u])[:, blk], lhsT=vtok.ap[:, j * 256 + u * 128: j * 256 + (u + 1) * 128], rhs=pslot,
                                                                                  start=(j == 0), stop=False),
                             reads=vtok.keys + Pb.keys, writes=[pk(by[u])], inc=False)
                    for c in range(2):
                        S.skip = PRE
                        csl = slice(j * 128 + c * 64, j * 128 + (c + 1) * 64)
                        for u in range(2):
                            for dh in range(2):
                                last = (j == 3 and c == 1 and dh == 1)
                                S.op("pe", lambda e, u=u, dh=dh, cur=cur, csl=csl, last=last, by=by: e.matmul(PB(by[u])[:, csl], lhsT=cur.ap[:, dh * 256 + u * 128: dh * 256 + (u + 1) * 128],
                                                                                                 rhs=qd[dh].ap[:, csl], start=False, stop=last),
                                     reads=cur.keys + qd[dh].keys, writes=[pk(by[u])], inc=last)
                        S.skip = False
                        bsu = bankB()
                        rows = slice(c * 64, (c + 1) * 64)
                        for dh in range(2):
                            S.op("pe", lambda e, bsu=bsu, dh=dh, rows=rows, j=j: e.matmul(PB(bsu)[:, dh * 256:(dh + 1) * 256], lhsT=ktok.ap[rows, j * 256 + dh * 128: j * 256 + (dh + 1) * 128],
                                                                                       rhs=vtok.ap[rows, j * 256:(j + 1) * 256], start=(dh == 0), stop=(dh == 1)),
                                 reads=ktok.keys + vtok.keys, writes=[pk(bsu)], inc=(dh == 1))
                        S.op("dve", lambda e, bsu=bsu, Sh=Sh, h=h: e.scalar_tensor_tensor(out=Sh, in0=Sh, scalar=cdec[h], in1=PB(bsu), op0=ALU.mult, op1=ALU.add),
                             reads=[pk(bsu), skey], writes=[skey])
                        S.skip = PRE
                        if not (j == 3 and c == 1):
                            cur = Sb[sbi % 2]
                            sbi += 1
                            S.op("act", lambda e, cur=cur, Sh=Sh: e.activation(out=cur.ap, in_=Sh, func=AF.Copy), reads=[skey], writes=cur.keys)
                for u in range(2):
                    S.op("act", lambda e, u=u, by=by: e.activation(out=ysb[u].ap, in_=PB(by[u]), func=AF.Copy), reads=[pk(by[u])], writes=ysb[u].keys)
                    S.op("act", lambda e, u=u, by=by: e.activation(out=ysq[u].ap, in_=PB(by[u]), func=AF.Square), reads=[pk(by[u])], writes=ysq[u].keys)
                bm = bankB()
                be = bankB()
                for u in range(2):
                    S.op("pe", lambda e, u=u, bm=bm: e.matmul(PB(bm), lhsT=onesS[:], rhs=ysb[u].ap, start=(u == 0), stop=(u == 1)),
                         reads=["onesS"] + ysb[u].keys, writes=[pk(bm)], inc=(u == 1))
                for u in range(2):
                    S.op("pe", lambda e, u=u, be=be: e.matmul(PB(be), lhsT=onesS[:], rhs=ysq[u].ap, start=(u == 0), stop=(u == 1)),
                         reads=["onesS"] + ysq[u].keys, writes=[pk(be)], inc=(u == 1))
                S.op("act", lambda e, bm=bm: e.activation(out=tA.ap, in_=PB(bm), func=AF.Copy), reads=[pk(bm)], writes=tA.keys)
                S.op("act", lambda e, bm=bm: e.activation(out=tB.ap, in_=PB(bm), func=AF.Square), reads=[pk(bm)], writes=tB.keys)
                S.op("dve", lambda e, be=be: e.tensor_tensor(out=tB.ap, in0=PB(be), in1=tB.ap, op=ALU.subtract), reads=[pk(be)] + tB.keys, writes=tB.keys)
                S.op("dve", lambda e: e.tensor_scalar(out=tB.ap, in0=tB.ap, scalar1=0.0, scalar2=EPS, op0=ALU.max, op1=ALU.add), reads=tB.keys, writes=tB.keys)
                S.op("act", lambda e: e.activation(out=tB.ap, in_=tB.ap, func=AF.Sqrt), reads=tB.keys, writes=tB.keys)
                S.op("dve", lambda e: e.reciprocal(tC.ap, tB.ap), reads=tB.keys, writes=tC.keys)
                for u in range(2):
                    S.op("dve", lambda e, u=u, by=by: e.tensor_tensor(out=tD.ap, in0=PB(by[u]), in1=tA.ap, op=ALU.subtract), reads=[pk(by[u])] + tA.keys, writes=tD.keys)
                    S.op("dve", lambda e: e.tensor_tensor(out=tD.ap, in0=tD.ap, in1=tC.ap, op=ALU.mult), reads=tD.keys + tC.keys, writes=tD.keys)
                    S.op("pool", lambda e, u=u, h=h: e.tensor_tensor(out=yR[:, 2 * h + u, :], in0=tD.ap, in1=sg[u].ap, op=ALU.mult),
                         reads=tD.keys + sg[u].keys, writes=[("yR", 2 * h + u)])
                S.skip = False

            BL[0] = [3, 4, 5, 6, 7]
            AL[0] = [0, 1, 2]
            chk(4)
            if ti == 0:
                dump("d_yR", yR[:].rearrange("p a b -> p (a b)"), [("yR",)], [128, 16 * 512], BF16)
                dump("d_Sret", Sret[:].rearrange("p a b -> p (a b)"), [("Sret",)], [128, 8 * 512], F32)
            if ti == 0:
                dtt = sb("dtt", [128, 3, 64], F32)
            dtxB = AB(0, 1, F32, 256)
            dtaB = AB(1, 1, F32, 256)
            dexB = AB(2, 4, F32, 1024)
            dtx = dtxB.ap.rearrange("p (j h) -> p j h", h=64)
            dta = dtaB.ap.rearrange("p (j h) -> p j h", h=64)
            dex = dexB.ap.rearrange("p (j h) -> p j h", h=256)
            slot = next_w("dt")
            for j in range(4):
                bk = bankA()
                proj_as(slot, 0, 64, j, hT_fn, [("hT",)], bk)
                S.op("dve", lambda e, bk=bk: e.tensor_tensor(out=dtt[:, 0, :], in0=PB(bk)[:, 0:64], in1=dtb[:], op=ALU.add),
                     reads=[pk(bk), "dtb"], writes=[("dtt", 0)])
                S.op("act", lambda e: e.activation(out=dtt[:, 1, :], in_=dtt[:, 0, :], func=AF.Abs),
                     reads=[("dtt", 0)], writes=[("dtt", 1)])
                S.op("act", lambda e: e.activation(out=dtt[:, 1, :], in_=dtt[:, 1, :], func=AF.Exp, scale=-1.0), reads=[("dtt", 1)], writes=[("dtt", 1)])
                S.op("act", lambda e: e.activation(out=dtt[:, 2, :], in_=dtt[:, 1, :], func=AF.Ln, bias=1.0), reads=[("dtt", 1)], writes=[("dtt", 2)])
                S.op("dve", lambda e, j=j: e.scalar_tensor_tensor(out=dtx[:, j, :], in0=dtt[:, 0, :], scalar=0.0, in1=dtt[:, 2, :], op0=ALU.max, op1=ALU.add),
                     reads=[("dtt", 0), ("dtt", 2)], writes=[dtxB.keys[0]])
                S.op("dve", lambda e, j=j: e.tensor_tensor(out=dta[:, j, :], in0=dtx[:, j, :], in1=arow[:], op=ALU.mult),
                     reads=[dtxB.keys[0], "arow"], writes=[dtaB.keys[0]])
                bk2 = bankB()
                for qi, msk in enumerate([TRI, UBD, ONC0, ONC1]):
                    S.op("pe", lambda e, bk2=bk2, qi=qi, msk=msk, j=j: e.matmul(PB(bk2)[:, qi * 64:(qi + 1) * 64], lhsT=msk, rhs=dta[:, j, :], start=(qi == 0), stop=(qi == 3)),
                         reads=["ssdm", dtaB.keys[0]], writes=[pk(bk2)], inc=(qi == 3))
                S.op("act", lambda e, bk2=bk2, j=j: e.activation(out=dex[:, j, :], in_=PB(bk2)[:, 0:256], func=AF.Exp), reads=[pk(bk2)], writes=dexB.keys)

            chk(5)
            BCpre = [AB(6, 1, BF16, 516), AB(7, 1, BF16, 516)]
            BT = AB(8, 1, BF16, 512)
            CT = AB(9, 1, BF16, 512)
            Btok = AB(10, 1, BF16, 512)
            CT0 = AB(11, 1, BF16, 512)
            CT1 = AB(12, 1, BF16, 512)
            xpre = [AB(13 + m, 1, BF16, 516) for m in range(4)]
            xsT = [AB(17 + m, 1, BF16, 512) for m in range(4)]
            zs = AB(21, 1, BF16, 512)
            diagc = [AB(22, 1, BF16, 512), AB(23, 1, BF16, 512)]
            diagD = AB(24, 1, BF16, 512)
            Rb = AB(25, 4, F32, 1024)
            dec = AB(29, 2, BF16, 1024)
            cbTm = AB(31, 1, F32, 128)
            xdt = AB(32, 1, BF16, 512)
            xdtt = AB(33, 1, BF16, 512)
            tmpf = AB(34, 2, F32, 512)
            yg = AB(36, 1, BF16, 512)
            stb = [AB(37, 1, BF16, 512), AB(38, 1, BF16, 512)]
            junk = zs
            dci = 0
            sti = 0
            for g in range(8):
                S.skip = False
                Sg = Sssd[:, g, :]
                sgk = ("Sssd", g)
                slot = next_w(f"x{g}")
                pres = []
                for m in range(4):
                    bk = bankA()
                    proj_ws(slot, 0, m, hT_fn, [("hT",)], bk)
                    S.op("act", lambda e, bk=bk, m=m: e.activation(out=xpre[m].ap[:, 3:515], in_=PB(bk), func=AF.Copy), reads=[pk(bk)], writes=xpre[m].keys)
                    pres.append((xpre[m], g * 4 + m, xsT[m]))
                slot = next_w(f"bc{g}")
                for m in range(2):
                    bk = bankA()
                    proj_ws(slot, 0, m, hT_fn, [("hT",)], bk)
                    S.op("act", lambda e, bk=bk, m=m: e.activation(out=BCpre[m].ap[:, 3:515], in_=PB(bk), func=AF.Copy), reads=[pk(bk)], writes=BCpre[m].keys)
                    pres.append((BCpre[m], 32 + m * 8 + g, BT if m == 0 else CT))
                for (pre, c48, post) in pres:
                    S.op("dve", lambda e, pre=pre, c48=c48: e.tensor_copy(pre.ap[:, 0:3], halo[:, c48, 0:3]), reads=[("halo", c48)], writes=pre.keys)
                    S.op("pool", lambda e, pre=pre, c48=c48: e.tensor_copy(halo[:, c48, 0:3], pre.ap[:, 512:515]), reads=pre.keys, writes=[("halo", c48)])
                    S.skip = PRE and (post is CT)
                    dg = diagc[dci % 2]
                    dci += 1
                    for k in range(4):
                        S.op("pool", lambda e, dg=dg, k=k, c48=c48: e.tensor_scalar(out=dg.ap[:, k * 128:(k + 1) * 128], in0=ident[:], scalar1=convw[:, c48, k:k + 1], scalar2=1.0, op0=ALU.mult, op1=ALU.mult),
                             reads=["ident", "convw"], writes=dg.keys)
                    bk = bankB()
                    for k in range(4):
                        S.op("pe", lambda e, bk=bk, dg=dg, k=k, pre=pre: e.matmul(PB(bk), lhsT=dg.ap[:, k * 128:(k + 1) * 128], rhs=pre.ap[:, k:k + 512], start=(k == 0), stop=(k == 3)),
                             reads=dg.keys + pre.keys, writes=[pk(bk)], inc=(k == 3))
                    S.op("act", lambda e, bk=bk, post=post, c48=c48: e.activation(out=post.ap, in_=PB(bk), func=AF.Silu, bias=convb[:, c48:c48 + 1]),
                         reads=[pk(bk), "convb"], writes=post.keys)
                    S.skip = False
                S.skip = PRE
                for m in range(4):
                    S.op("pool", lambda e, m=m, g=g: e.tensor_scalar(out=diagD.ap[:, m * 128:(m + 1) * 128], in0=ident[:], scalar1=dch[:, g * 4 + m: g * 4 + m + 1], scalar2=1.0, op0=ALU.mult, op1=ALU.mult),
                         reads=["ident", "dch"], writes=diagD.keys)
                S.op("pool", lambda e: e.memset(CT0.ap, 0.0), writes=CT0.keys)
                S.op("pool", lambda e: e.memset(CT1.ap, 0.0), writes=CT1.keys)
                ct4 = CT.ap.rearrange("p (j c l) -> p j c l", c=2, l=64)
                S.op("pool", lambda e, ct4=ct4: e.tensor_copy(CT0.ap.rearrange("p (j c l) -> p j c l", c=2, l=64)[:, :, 0, :], ct4[:, :, 0, :]), reads=CT.keys, writes=CT0.keys)
                S.op("pool", lambda e, ct4=ct4: e.tensor_copy(CT1.ap.rearrange("p (j c l) -> p j c l", c=2, l=64)[:, :, 1, :], ct4[:, :, 1, :]), reads=CT.keys, writes=CT1.keys)
                S.skip = False
                slot = None if PRE else next_w(f"z{g}")
                hs = slice(g * 8, (g + 1) * 8)
                BL[0] = [6, 7, 0, 1, 2]
                zs4 = [AB(13 + j, 1, BF16, 512) for j in range(4)]
                S.skip = PRE
                for j in range(4):
                    bz = bankA()
                    proj_as(slot, 0, 512, j, hT_fn, [("hT",)], bz)
                    S.op("act", lambda e, bz=bz, j=j, zs4=zs4: e.activation(out=zs4[j].ap, in_=PB(bz), func=AF.Silu), reads=[pk(bz)], writes=zs4[j].keys)
                S.skip = False

                def P1(j):
                    blk = slice(j * 128, (j + 1) * 128)
                    S.skip = PRE
                    bk = bankB()
                    S.op("pe", lambda e, bk=bk, blk=blk: e.matmul(PB(bk)[:, 0:128], lhsT=BT.ap[:, blk], rhs=CT.ap[:, blk], start=True, stop=True),
                         reads=BT.keys + CT.keys, writes=[pk(bk)])
                    S.op("dve", lambda e, bk=bk: e.tensor_tensor(out=cbTm.ap, in0=PB(bk)[:, 0:128], in1=MBD, op=ALU.mult), reads=[pk(bk), "ssdm"], writes=cbTm.keys)
                    S.skip = False
                    bk = bankB()
                    S.op("pe", lambda e, bk=bk, blk=blk: e.transpose(PBb(bk)[:, 0:128], BT.ap[:, blk], ident[:]), reads=BT.keys + ["ident"], writes=[pk(bk)])
                    S.op("act", lambda e, bk=bk, j=j: e.activation(out=Btok.ap[:, j * 128:(j + 1) * 128], in_=PBb(bk)[:, 0:128], func=AF.Copy), reads=[pk(bk)], writes=Btok.keys)
                    S.skip = PRE
                    S.op("pool", lambda e, j=j, hs=hs: e.tensor_tensor(out=Rb.ap.rearrange("p (h l) -> p h l", l=128),
                                                                in0=TRI.unsqueeze(1).to_broadcast([128, 8, 128]),
                                                                in1=dta[:, j, hs].unsqueeze(2).to_broadcast([128, 8, 128]), op=ALU.mult),
                         reads=["ssdm", dtaB.keys[0]], writes=Rb.keys)
                    bs = [bankB(), bankB()]
                    for q in range(2):
                        S.op("pe", lambda e, q=q, bs=bs: e.matmul(PB(bs[q]), lhsT=UU, rhs=Rb.ap[:, q * 512:(q + 1) * 512], start=True, stop=True),
                             reads=["ssdm"] + Rb.keys, writes=[pk(bs[q])])
                        S.op("act", lambda e, q=q, bs=bs: e.activation(out=dec.ap[:, q * 512:(q + 1) * 512], in_=PB(bs[q]), func=AF.Exp), reads=[pk(bs[q])], writes=dec.keys)
                    S.op("dve", lambda e: e.tensor_tensor(out=dec.ap.rearrange("p (h l) -> p h l", l=128), in0=dec.ap.rearrange("p (h l) -> p h l", l=128),
                                                          in1=cbTm.ap.unsqueeze(1).to_broadcast([128, 8, 128]), op=ALU.mult),
                         reads=dec.keys + cbTm.keys, writes=dec.keys)
                    S.skip = False
                    bk = bankB()
                    for m in range(4):
                        S.op("pe", lambda e, bk=bk, m=m, blk=blk: e.transpose(PBb(bk)[:, m * 128:(m + 1) * 128], xsT[m].ap[:, blk], ident[:]),
                             reads=xsT[m].keys + ["ident"], writes=[pk(bk)], inc=(m == 3))
                    S.op("dve", lambda e, bk=bk, j=j, hs=hs: e.tensor_tensor(out=xdt.ap.rearrange("p (h q) -> p h q", q=64), in0=PBb(bk)[:, 0:512].rearrange("p (h q) -> p h q", q=64),
                                                                    in1=dtx[:, j, hs].unsqueeze(2).to_broadcast([128, 8, 64]), op=ALU.mult),
                         reads=[pk(bk), dtxB.keys[0]], writes=xdt.keys)
                    S.op("pool", lambda e, j=j, g=g: e.tensor_tensor(out=xdtt.ap.rearrange("p (h q) -> p h q", q=64), in0=xdt.ap.rearrange("p (h q) -> p h q", q=64),
                                                                     in1=dex[:, j, 64 + g * 8: 64 + (g + 1) * 8].unsqueeze(2).to_broadcast([128, 8, 64]), op=ALU.mult),
                         reads=xdt.keys + dexB.keys, writes=xdtt.keys)

                def P2a(j):
                    blk = slice(j * 128, (j + 1) * 128)
                    S.skip = PRE
                    bys = 3
                    for m in range(4):
                        S.op("pe", lambda e, m=m, blk=blk: e.matmul(PB(3)[:, m * 128:(m + 1) * 128], lhsT=xsT[m].ap[:, blk], rhs=diagD.ap[:, m * 128:(m + 1) * 128],
                                                                   start=(m == 0), stop=False),
                             reads=xsT[m].keys + diagD.keys, writes=[pk(bys)], inc=False)
                    for hh in range(8):
                        S.op("pe", lambda e, hh=hh: e.matmul(PB(3)[:, hh * 64:(hh + 1) * 64], lhsT=dec.ap[:, hh * 128:(hh + 1) * 128], rhs=xdt.ap[:, hh * 64:(hh + 1) * 64],
                                                            start=False, stop=(hh == 7)),
                             reads=dec.keys + xdt.keys, writes=[pk(bys)], inc=(hh == 7))
                    S.skip = False
                    for c in range(2):
                        rows = slice(c * 64, (c + 1) * 64)
                        S.op("pe", lambda e, c=c, rows=rows, j=j: e.matmul(PB(4 + c), lhsT=Btok.ap[rows, j * 128:(j + 1) * 128], rhs=xdtt.ap[rows, :], start=True, stop=True),
                             reads=Btok.keys + xdtt.keys, writes=[pk(4 + c)])

                def P2b(j):
                    nonlocal_sti = sti_box
                    blk = slice(j * 128, (j + 1) * 128)
                    bys = 3
                    S.skip = PRE
                    byi = bankB()
                    for c in range(2):
                        S.skip = PRE
                        cur = stb[nonlocal_sti[0] % 2]
                        nonlocal_sti[0] += 1
                        S.op("act", lambda e, cur=cur, Sg=Sg: e.activation(out=cur.ap, in_=Sg, func=AF.Copy), reads=[sgk], writes=cur.keys)
                        ctp = CT0 if c == 0 else CT1
                        S.op("pe", lambda e, byi=byi, cur=cur, ctp=ctp, c=c, blk=blk: e.matmul(PB(byi), lhsT=ctp.ap[:, blk], rhs=cur.ap, start=(c == 0), stop=(c == 1)),
                             reads=ctp.keys + cur.keys, writes=[pk(byi)], inc=(c == 1))
                        S.skip = False
                        cofs = 128 + c * 64 + g * 8
                        S.op("dve", lambda e, j=j, cofs=cofs, Sg=Sg: e.tensor_tensor(out=Sg.rearrange("p (h q) -> p h q", q=64), in0=Sg.rearrange("p (h q) -> p h q", q=64),
                                                                               in1=dex[:, j, cofs:cofs + 8].unsqueeze(2).to_broadcast([128, 8, 64]), op=ALU.mult),
                             reads=[sgk] + dexB.keys, writes=[sgk])
                        S.op("dve", lambda e, c=c, Sg=Sg: e.tensor_tensor(out=Sg, in0=PB(4 + c), in1=Sg, op=ALU.add), reads=[pk(4 + c), sgk], writes=[sgk])
                    S.skip = PRE
                    S.op("dve", lambda e, byi=byi, j=j, hs=hs: e.tensor_tensor(out=tmpf.ap.rearrange("p (h q) -> p h q", q=64), in0=PB(byi).rearrange("p (h q) -> p h q", q=64),
                                                                      in1=dex[:, j, hs].unsqueeze(2).to_broadcast([128, 8, 64]), op=ALU.mult),
                         reads=[pk(byi)] + dexB.keys, writes=tmpf.keys)
                    S.op("dve", lambda e: e.tensor_tensor(out=tmpf.ap, in0=PB(3), in1=tmpf.ap, op=ALU.add), reads=[pk(bys)] + tmpf.keys, writes=tmpf.keys)
                    S.op("dve", lambda e, j=j, zs4=zs4: e.tensor_tensor(out=yg.ap, in0=tmpf.ap, in1=zs4[j].ap, op=ALU.mult), reads=tmpf.keys + zs4[j].keys, writes=yg.keys)
                    S.op("act", lambda e, j=j, g=g: e.activation(out=junk.ap, in_=yg.ap, func=AF.Square, accum_out=ssq[:, j, g:g + 1]), reads=yg.keys, writes=junk.keys + [("ssq", j, g)])
                    bk = bankB()
                    for m in range(4):
                        S.op("pe", lambda e, bk=bk, m=m: e.transpose(PBb(bk)[:, m * 128:(m + 1) * 128], yg.ap[:, m * 128:(m + 1) * 128], ident[:]),
                             reads=yg.keys + ["ident"], writes=[pk(bk)], inc=(m == 3))
                    S.op("act", lambda e, bk=bk, g=g, blk=blk: e.activation(out=ysT[:, g * 4:(g + 1) * 4, blk], in_=PBb(bk)[:, 0:512].rearrange("p (a b) -> p a b", b=128), func=AF.Copy),
                         reads=[pk(bk)], writes=[("ysT", g * 4 + m2) for m2 in range(4)])
                    S.skip = False

                sti_box = [sti]
                P1(0)
                for j in range(4):
                    P2a(j)
                    if j + 1 < 4:
                        P1(j + 1)
                    P2b(j)
                sti = sti_box[0]
                BL[0] = [3, 4, 5, 6, 7]

            S.skip = False
            chk(6)
            if PRE:
                continue
            if ti == 0:
                dump("d_ysT", ysT[:].rearrange("p a b -> p (a b)"), [("ysT",)], [128, 32 * 512], BF16)
                dump("d_ssq", ssq[:].rearrange("p a b -> p (a b)"), [("ssq",)], [128, 32], F32)
                dump("d_Sssd", Sssd[:].rearrange("p a b -> p (a b)"), [("Sssd",)], [128, 8 * 512], F32)
            sgr = [AB(m, 1, BF16, 512) for m in range(16)]
            sgs = [AB(16 + m, 1, BF16, 512) for m in range(16)]
            rsrow = AB(32, 2, F32, 512)
            dgf = AB(34, 1, F32, 128)
            t1 = AB(35, 2, F32, 512)
            t2 = AB(37, 2, F32, 512)
            S.op("dve", lambda e: e.reduce_sum(sml[:, 4:8], ssq[:], axis=mybir.AxisListType.X), reads=[("ssq",)], writes=[("sml", 4)])
            S.op("dve", lambda e: e.tensor_scalar(out=sml[:, 4:8], in0=sml[:, 4:8], scalar1=1.0 / 4096, scalar2=EPS, op0=ALU.mult, op1=ALU.add), reads=[("sml", 4)], writes=[("sml", 4)])
            S.op("act", lambda e: e.activation(out=sml[:, 4:8], in_=sml[:, 4:8], func=AF.Sqrt), reads=[("sml", 4)], writes=[("sml", 4)])
            S.op("dve", lambda e: e.reciprocal(sml[:, 8:12], sml[:, 4:8]), reads=[("sml", 4)], writes=[("sml", 8)])
            brs = bankB()
            for j in range(4):
                S.op("dve", lambda e, j=j: e.tensor_scalar_mul(dgf.ap, identf[:], sml[:, 8 + j:9 + j]), reads=["identf", ("sml", 8)], writes=dgf.keys)
                S.op("pe", lambda e, j=j, brs=brs: e.matmul(PB(brs)[:, j * 128:(j + 1) * 128], lhsT=onesf[:], rhs=dgf.ap, start=(j == 0), stop=(j == 3)),
                     reads=["onesf"] + dgf.keys, writes=[pk(brs)])
            S.op("act", lambda e, brs=brs: e.activation(out=rsrow.ap, in_=PB(brs), func=AF.Copy), reads=[pk(brs)], writes=rsrow.keys)
            yR_fn = lambda kc: yR[:, kc, :]
            ys_fn = lambda kc: ysT[:, kc, :]
            for oc in range(4):
                slot = next_w(f"gr{oc}")
                for m in range(4):
                    bk = bankA()
                    proj_ws(slot, 0, m, hT_fn, [("hT",)], bk)
                    S.op("act", lambda e, bk=bk, m=m, oc=oc: e.activation(out=sgr[oc * 4 + m].ap, in_=PB(bk), func=AF.Sigmoid), reads=[pk(bk)], writes=sgr[oc * 4 + m].keys)
                slot = next_w(f"gs{oc}")
                for m in range(4):
                    bk = bankA()
                    proj_ws(slot, 0, m, hT_fn, [("hT",)], bk)
                    S.op("act", lambda e, bk=bk: e.activation(out=t1.ap, in_=PB(bk), func=AF.Sigmoid), reads=[pk(bk)], writes=t1.keys)
                    S.op("dve", lambda e, m=m, oc=oc: e.tensor_tensor(out=sgs[oc * 4 + m].ap, in0=t1.ap, in1=rsrow.ap, op=ALU.mult), reads=t1.keys + rsrow.keys, writes=sgs[oc * 4 + m].keys)
            for oc in range(4):
                slot_r = next_w(f"br{oc}")
                prb = []
                for m in range(4):
                    bk = bankB()
                    proj_ws(slot_r, 0, m, yR_fn, [("yR",)], bk)
                    prb.append(bk)
                slot0 = next_w(f"bs0{oc}")
                slot1 = next_w(f"bs1{oc}", prefetch=False)
                for m in range(4):
                    bk = bankA()
                    proj_ws(slot0, 0, m, ys_fn, [("ysT",)], bk, first=True, last=False, kofs=0)
                    proj_ws(slot1, 0, m, ys_fn, [("ysT",)], bk, first=False, last=True, kofs=16)
                    S.op("dve", lambda e, m=m, prb=prb, oc=oc: e.tensor_tensor(out=t1.ap, in0=PB(prb[m]), in1=sgr[oc * 4 + m].ap, op=ALU.mult), reads=[pk(prb[m])] + sgr[oc * 4 + m].keys, writes=t1.keys)
                    S.op("dve", lambda e, m=m, bk=bk, oc=oc: e.tensor_tensor(out=t2.ap, in0=PB(bk), in1=sgs[oc * 4 + m].ap, op=ALU.mult), reads=[pk(bk)] + sgs[oc * 4 + m].keys, writes=t2.keys)
                    S.op("pool", lambda e, m=m, oc=oc: e.tensor_tensor(out=hT[:, oc * 4 + m, :], in0=t1.ap, in1=t2.ap, op=ALU.add),
                         reads=t1.keys + t2.keys, writes=[("hT",)])
                    if m == 3:
                        prefetch_next()
            chk(7)
            if ti == 0:
                dump("d_mrg", hT[:].rearrange("p a b -> p (a b)"), [("hT",)], [128, 16 * 512], BF16)
            xr = [AB(8 * j, 8, F32, 2048) for j in range(4)]
            for j in range(4):
                r0 = tok0 + j * 128
                S.op("sp", lambda e, j=j, r0=r0: e.dma_start(out=xr[j].ap, in_=x[r0:r0 + 128, :]), writes=xr[j].keys, dsem=f"xr{j}")
            mrg_fn = hT_fn
            for oc in range(4):
                slot = next_w(f"o{oc}")
                for j in range(4):
                    bk = bankA()
                    proj_as(slot, 0, 512, j, mrg_fn, [("hT",)], bk)
                    S.op("dve", lambda e, bk=bk, j=j, oc=oc: e.tensor_tensor(out=xr[j].ap[:, oc * 512:(oc + 1) * 512], in0=PB(bk), in1=xr[j].ap[:, oc * 512:(oc + 1) * 512], op=ALU.add),
                         reads=[pk(bk)] + xr[j].keys, writes=xr[j].keys)
            for j in range(4):
                r0 = tok0 + j * 128
                S.op("act", lambda e, j=j: e.activation(out=hb, in_=xr[j].ap, func=AF.Square, accum_out=sml[:, 12:13]), reads=xr[j].keys, writes=HBK + [("sml", 12)])
                S.op("dve", lambda e: e.tensor_scalar(out=sml[:, 13:14], in0=sml[:, 12:13], scalar1=1.0 / D, scalar2=EPS, op0=ALU.mult, op1=ALU.add), reads=[("sml", 12)], writes=[("sml", 13)])
                S.op("act", lambda e: e.activation(out=sml[:, 14:15], in_=sml[:, 13:14], func=AF.Sqrt), reads=[("sml", 13)], writes=[("sml", 14)])
                S.op("dve", lambda e: e.reciprocal(sml[:, 15:16], sml[:, 14:15]), reads=[("sml", 14)], writes=[("sml", 15)])
                S.op("dve", lambda e, j=j: e.scalar_tensor_tensor(out=xr[j].ap, in0=xr[j].ap, scalar=sml[:, 15:16], in1=normf[:], op0=ALU.mult, op1=ALU.mult),
                     reads=xr[j].keys + [("sml", 15), "normf"], writes=xr[j].keys)
                ro = (ti - NPRE) * T + j * 128
                S.op("sp", lambda e, j=j, ro=ro: e.dma_start(out=out[ro:ro + 128, :], in_=xr[j].ap), reads=xr[j].keys, dsem=f"o{j}")

        except _Stop:
            pass
        S.emit(st)
    return nc


def _consts():
    c = {}
    c["c_ident"] = np.eye(128, dtype=np.float32).astype(ml_dtypes.bfloat16)
    c["c_identf"] = np.eye(128, dtype=np.float32)
    c["c_onesf"] = np.ones((128, 128), np.float32)
    c["c_ones"] = np.full((128, 128), 1.0 / 256.0, np.float32).astype(ml_dtypes.bfloat16)
    idx = np.arange(128)
    same = (idx[:, None] // 64) == (idx[None, :] // 64)
    lg = np.log1p(-(2.0 ** (-5.0 - np.arange(8, dtype=np.float64))))
    rm = np.zeros((128, 8, 128), np.float64)
    for h in range(8):
        rm[:, h, :] = np.where(same, np.exp(np.abs(idx[:, None] - idx[None, :]) * lg[h]), 0.0) * (256.0 ** -0.5)
    c["c_retmask"] = rm.reshape(128, 1024).astype(np.float32).astype(ml_dtypes.bfloat16)
    j = idx[:, None]
    l = idx[None, :]
    TRI = (same & (j <= l)).astype(np.float32)
    UBD = (same & (j > l)).astype(np.float32)
    UU = (j > l).astype(np.float32)
    ONC0 = np.repeat((idx < 64).astype(np.float32)[:, None], 128, 1)
    ONC1 = np.repeat((idx >= 64).astype(np.float32)[:, None], 128, 1)
    MBD = (same & (l >= j)).astype(np.float32)
    c["c_ssd"] = np.stack([TRI, UBD, UU, ONC0, ONC1, MBD], 1).reshape(128, 768).astype(np.float32)
    qd = np.exp((np.arange(64)[None, :] + 1.0) * lg[:, None])
    c["c_qdec"] = np.repeat(qd.reshape(1, 512), 128, 0).astype(np.float32)
    kd = np.exp((63.0 - (idx[:, None] % 64)) * lg[None, :]) * (256.0 ** -0.5)
    c["c_kdec"] = kd.astype(np.float32)
    half = 128
    c["c_invf"] = (np.float32(10000.0) ** (-np.arange(half, dtype=np.float32) / np.float32(half))).astype(np.float32).reshape(128, 1)
    return c


def _prep_inputs(inputs, NT, ncores, NPRE=0):
    x = np.asarray(inputs["x"], np.float32)
    pos = np.asarray(inputs["positions"], np.int32)
    com = {}
    com["w_in"] = np.ascontiguousarray(np.asarray(inputs["w_in"], np.float32)[0])
    com["w_brr"] = np.ascontiguousarray(np.asarray(inputs["w_br_ret"], np.float32)[0])
    com["w_brs"] = np.ascontiguousarray(np.asarray(inputs["w_br_ssd"], np.float32)[0])
    com["w_out"] = np.ascontiguousarray(np.asarray(inputs["w_out"], np.float32)[0])
    com["n1col"] = np.ascontiguousarray(np.asarray(inputs["norm1_w"], np.float32)[0].reshape(16, 128).T)
    com["sncol"] = np.ascontiguousarray(np.asarray(inputs["ssd_norm_w"], np.float32)[0].reshape(32, 128).T)
    cw = np.asarray(inputs["conv_w"], np.float32)[0]
    com["convw"] = np.ascontiguousarray(cw.reshape(4, 48, 128).transpose(2, 1, 0).reshape(128, 192))
    com["convb"] = np.ascontiguousarray(np.asarray(inputs["conv_b"], np.float32)[0].reshape(48, 128).T)
    dsk = np.repeat(np.asarray(inputs["d_skip"], np.float32)[0], 64)
    com["dch"] = np.ascontiguousarray(dsk.reshape(32, 128).T)
    com["dtb"] = np.asarray(inputs["dt_bias"], np.float32).reshape(1, 64)
    com["alog"] = np.asarray(inputs["a_log"], np.float32).reshape(1, 64)
    com["normf"] = np.asarray(inputs["norm_f_w"], np.float32).reshape(1, 2048)
    com.update(_consts())
    maps = []
    nmain = (NT - NPRE) * T
    npre = NPRE * T
    for c in range(ncores):
        m = dict(com)
        if NPRE == 0:
            b = c % x.shape[0]
            m["x"] = np.ascontiguousarray(x[b, :NT * T])
            m["pos"] = np.ascontiguousarray(pos[b, :NT * T].reshape(1, -1))
            m["flag"] = np.ones((128, 1), np.float32)
        else:
            nhalf = x.shape[1] // nmain
            b, hf = divmod(c, nhalf)
            own = x[b, hf * nmain:(hf + 1) * nmain]
            pown = pos[b, hf * nmain:(hf + 1) * nmain]
            if hf == 0:
                prev = np.zeros((npre, x.shape[2]), np.float32)
                pprev = np.zeros((npre,), np.int32)
            else:
                prev = x[b, hf * nmain - npre: hf * nmain]
                pprev = pos[b, hf * nmain - npre: hf * nmain]
            m["x"] = np.ascontiguousarray(np.concatenate([prev, own], 0))
            m["pos"] = np.ascontiguousarray(np.concatenate([pprev, pown], 0).reshape(1, -1))
            m["flag"] = np.full((128, 1), float(hf != 0), np.float32)
        maps.append(m)
    return maps


_NC_CACHE = {}


def kernel(**inputs):
    NT, NPRE = 16, 8
    key = (NT, NPRE)
    if key not in _NC_CACHE:
        _NC_CACHE[key] = build_nc(NT, NPRE=NPRE)
    nc = _NC_CACHE[key]
    maps = _prep_inputs(inputs, NT, 8, NPRE=NPRE)
    res = run_bass_kernel_spmd(nc, maps, core_ids=list(range(8)))
    x = inputs["x"]
    B, SEQ = x.shape[0], x.shape[1]
    nmain = (NT - NPRE) * T
    nhalf = SEQ // nmain
    outp = np.empty((B, SEQ, D), np.float32)
    for c in range(8):
        b, hf = divmod(c, nhalf)
        outp[b, hf * nmain:(hf + 1) * nmain] = res.results[c]["out"]
    return outp
```
